# Optimizing a Trainium2 kernel written in Bass

```python
import math
import jax, jax.numpy as jnp
from jax import lax
import numpy as np

D_MODEL = 1024
BATCH = 16
SEQ = 4096
DEPTH = 2
DEC_BATCH = 32
DEC_SEQ = 16
PAST_LEN = 4096

CHUNK = 64
N_MIXERS = 2
N_CONV_LAYERS = (DEPTH + 1) // 2
N_FOX_LAYERS = DEPTH // 2
E_CONV = D_MODEL
CONV_WIDTH = 31
CONV_STATE = CONV_WIDTH - 1
N_HEADS = 16
HEAD_DIM = D_MODEL // N_HEADS
E_FOX = N_HEADS * HEAD_DIM
Q_BLOCK = 128
FORGET_BIAS_LO = 2.0
FORGET_BIAS_HI = 5.0
EPS = 1e-6

kernel_name = 'streaming_conv_fox_hybrid'


def rmsnorm(x, g):
    xf = x.astype(jnp.float32)
    y = xf * lax.rsqrt(jnp.mean(xf * xf, axis=-1, keepdims=True) + EPS)
    return y.astype(x.dtype) * g


def layernorm(x, g, b):
    xf = x.astype(jnp.float32)
    mu = jnp.mean(xf, axis=-1, keepdims=True)
    xc = xf - mu
    y = xc * lax.rsqrt(jnp.mean(xc * xc, axis=-1, keepdims=True) + EPS)
    return y.astype(x.dtype) * g + b


def conv_mixer(h, buf, w_in, w_dw, b_dw, ln_g, ln_b, w_out):
    u = jnp.einsum('btd,de->bte', h, w_in)
    a, g_glu, z = jnp.split(u, 3, axis=-1)
    v = a * jax.nn.sigmoid(g_glu)
    vp = jnp.concatenate([buf.astype(v.dtype), v], axis=1)
    y = lax.conv_general_dilated(
        vp, w_dw[:, None, :].astype(vp.dtype), window_strides=(1,), padding='VALID',
        dimension_numbers=('NWC', 'WIO', 'NWC'), feature_group_count=E_CONV) + b_dw
    y = jax.nn.silu(layernorm(y, ln_g, ln_b))
    out = jnp.einsum('bte,ed->btd', y * jax.nn.silu(z), w_out)
    return out, vp[:, -CONV_STATE:]


def fox_mixer(h, past_k, past_v, past_logf, w_in, b_f, qn_g, kn_g, w_out):
    B, T, _ = h.shape
    P = past_k.shape[1]
    u = jnp.einsum('btd,de->bte', h, w_in)
    q, k, v, z, fl = jnp.split(u, [E_FOX, 2 * E_FOX, 3 * E_FOX, 4 * E_FOX], axis=-1)
    q = rmsnorm(q.reshape(B, T, N_HEADS, HEAD_DIM), qn_g)
    k = rmsnorm(k.reshape(B, T, N_HEADS, HEAD_DIM), kn_g)
    v = v.reshape(B, T, N_HEADS, HEAD_DIM)
    logf = jax.nn.log_sigmoid((fl + b_f).astype(jnp.float32))
    k_all = jnp.concatenate([past_k.astype(k.dtype), k], axis=1)
    v_all = jnp.concatenate([past_v.astype(v.dtype), v], axis=1)
    c = jnp.cumsum(jnp.concatenate([past_logf.astype(jnp.float32), logf], axis=1), axis=1)
    scale = 1.0 / math.sqrt(HEAD_DIM)
    outs = []
    for start in range(0, T, Q_BLOCK):
        end = min(start + Q_BLOCK, T)
        nk = P + end
        s = jnp.einsum('bqhd,bkhd->bhqk', q[:, start:end], k_all[:, :nk]).astype(jnp.float32) * scale
        cq = jnp.transpose(c[:, P + start:P + end], (0, 2, 1))[:, :, :, None]
        ck = jnp.transpose(c[:, :nk], (0, 2, 1))[:, :, None, :]
        s = s + (cq - ck)
        q_pos = P + start + jnp.arange(end - start)
        k_pos = jnp.arange(nk)
        mask = k_pos[None, :] <= q_pos[:, None]
        s = jnp.where(mask[None, None], s, -jnp.inf)
        p = jax.nn.softmax(s, axis=-1).astype(v_all.dtype)
        outs.append(jnp.einsum('bhqk,bkhd->bqhd', p, v_all[:, :nk]))
    o = jnp.concatenate(outs, axis=1).reshape(B, T, E_FOX)
    out = jnp.einsum('bte,ed->btd', o * jax.nn.silu(z), w_out)
    return out, k, v, logf


def setup_inputs(seed: int = 0) -> dict:
    key = jax.random.key(seed)
    ks = jax.random.split(key, 20)
    nrm = jax.random.normal
    return {
        'x_prompt': nrm(ks[0], (BATCH, SEQ, D_MODEL), jnp.float32),
        'x_sample': nrm(ks[1], (DEC_BATCH, DEC_SEQ, D_MODEL), jnp.float32),
        'state_conv': 0.5 * nrm(ks[2], (N_CONV_LAYERS, DEC_BATCH, CONV_STATE, E_CONV), jnp.float32),
        'cache_k': nrm(ks[3], (N_FOX_LAYERS, DEC_BATCH, PAST_LEN, N_HEADS, HEAD_DIM), jnp.float32),
        'cache_v': nrm(ks[4], (N_FOX_LAYERS, DEC_BATCH, PAST_LEN, N_HEADS, HEAD_DIM), jnp.float32),
        'cache_logf': jax.nn.log_sigmoid(3.0 + nrm(ks[5], (N_FOX_LAYERS, DEC_BATCH, PAST_LEN, N_HEADS), jnp.float32)),
        'norm_g': 1.0 + 0.05 * nrm(ks[6], (DEPTH, D_MODEL), jnp.float32),
        'final_norm_g': 1.0 + 0.05 * nrm(ks[7], (D_MODEL,), jnp.float32),
        'w_conv_in': nrm(ks[8], (N_CONV_LAYERS, D_MODEL, 3 * E_CONV), jnp.float32) * D_MODEL ** -0.5,
        'w_dw': nrm(ks[9], (N_CONV_LAYERS, CONV_WIDTH, E_CONV), jnp.float32) * CONV_WIDTH ** -0.5,
        'b_dw': 0.02 * nrm(ks[10], (N_CONV_LAYERS, E_CONV), jnp.float32),
        'conv_ln_g': 1.0 + 0.05 * nrm(ks[11], (N_CONV_LAYERS, E_CONV), jnp.float32),
        'conv_ln_b': 0.02 * nrm(ks[12], (N_CONV_LAYERS, E_CONV), jnp.float32),
        'w_conv_out': nrm(ks[13], (N_CONV_LAYERS, E_CONV, D_MODEL), jnp.float32) * E_CONV ** -0.5,
        'w_fox_in': nrm(ks[14], (N_FOX_LAYERS, D_MODEL, 4 * E_FOX + N_HEADS), jnp.float32) * D_MODEL ** -0.5,
        'b_forget': jax.random.uniform(ks[15], (N_FOX_LAYERS, N_HEADS), jnp.float32, FORGET_BIAS_LO, FORGET_BIAS_HI),
        'q_norm_g': 1.0 + 0.05 * nrm(ks[16], (N_FOX_LAYERS, HEAD_DIM), jnp.float32),
        'k_norm_g': 1.0 + 0.05 * nrm(ks[17], (N_FOX_LAYERS, HEAD_DIM), jnp.float32),
        'w_fox_out': nrm(ks[18], (N_FOX_LAYERS, E_FOX, D_MODEL), jnp.float32) * E_FOX ** -0.5,
    }


def reference(x_prompt, x_sample, state_conv, cache_k, cache_v, cache_logf,
              norm_g, final_norm_g, w_conv_in, w_dw, b_dw, conv_ln_g, conv_ln_b, w_conv_out,
              w_fox_in, b_forget, q_norm_g, k_norm_g, w_fox_out):
    yp, ys = x_prompt, x_sample
    Bp = x_prompt.shape[0]
    conv_p, conv_s = [], []
    k_p, v_p, lf_p, k_s, v_s, lf_s = [], [], [], [], [], []
    for i in range(DEPTH):
        hp = rmsnorm(yp, norm_g[i])
        hs = rmsnorm(ys, norm_g[i])
        j = i // N_MIXERS
        if i % N_MIXERS == 0:
            cw = (w_conv_in[j], w_dw[j], b_dw[j], conv_ln_g[j], conv_ln_b[j], w_conv_out[j])
            op, bp = conv_mixer(hp, jnp.zeros((Bp, CONV_STATE, E_CONV), hp.dtype), *cw)
            os_, bs = conv_mixer(hs, state_conv[j], *cw)
            conv_p.append(bp)
            conv_s.append(bs)
        else:
            fw = (w_fox_in[j], b_forget[j], q_norm_g[j], k_norm_g[j], w_fox_out[j])
            op, kp_, vp_, lp_ = fox_mixer(
                hp, jnp.zeros((Bp, 0, N_HEADS, HEAD_DIM), hp.dtype), jnp.zeros((Bp, 0, N_HEADS, HEAD_DIM), hp.dtype),
                jnp.zeros((Bp, 0, N_HEADS), jnp.float32), *fw)
            os_, ks_, vs_, ls_ = fox_mixer(hs, cache_k[j], cache_v[j], cache_logf[j], *fw)
            k_p.append(kp_); v_p.append(vp_); lf_p.append(lp_)
            k_s.append(ks_); v_s.append(vs_); lf_s.append(ls_)
        yp = yp + op
        ys = ys + os_
    y_prompt = rmsnorm(yp, final_norm_g)
    y_sample = rmsnorm(ys, final_norm_g)
    return (y_prompt, y_sample, jnp.stack(conv_p), jnp.stack(conv_s),
            jnp.stack(k_p), jnp.stack(v_p), jnp.stack(lf_p),
            jnp.stack(k_s), jnp.stack(v_s), jnp.stack(lf_s))
```

```python
import numpy as np
from contextlib import ExitStack
import concourse.bass as bass
import concourse.mybir as mybir
from concourse.bass_utils import run_bass_kernel_spmd

F32 = mybir.dt.float32
BF16 = mybir.dt.bfloat16
AF = mybir.ActivationFunctionType
ALU = mybir.AluOpType
AX = mybir.AxisListType
EPS = 1e-6
D = 1024
H = 16
DH = 64
CW = 31
CS = 30
TS = 16


class Cfg:
    def __init__(self, T=4096, NSEQ=2, NSTR=4, PAST=4096):
        self.T, self.NSEQ, self.NSTR, self.PAST = T, NSEQ, NSTR, PAST
        self.NP = NSEQ * T
        self.NS = NSTR * TS
        self.LS = PAST + TS
        self.do_l1 = 2
        self.debug = False


class Ins:
    __slots__ = ("eng", "fn", "dma", "chan", "deps", "need", "val", "sem")


class Sched:
    def __init__(self, nc, es):
        self.nc = nc
        self.es = es
        self.csem = {e: es.enter_context(nc.semaphore("c_" + e)) for e in ("pe", "act", "dve", "pool")}
        self.ccnt = dict.fromkeys(self.csem, 0)
        self.dsem = {}
        self.dcnt = {}
        self.waited = {e: {} for e in ("sp", "pe", "act", "dve", "pool")}
        self.ninst = 0
        self.dhist = []
        self.maxdma = 1 << 30
        self.reset()

    def reset(self):
        self.ins = []
        self.lw = {}
        self.rd = {}

    def add(self, eng, fn, r=(), w=(), chan=None, noself=False):
        self.nadd = getattr(self, 'nadd', 0) + 1
        if getattr(self, 'limit', None) is not None and self.nadd > self.limit:
            return None
        i = Ins()
        if chan is not None:
            eng = "sp"
        i.eng, i.fn, i.dma, i.chan = eng, fn, chan is not None, chan
        i.need = i.dma
        deps = {}
        for k in r:
            d = self.lw.get(k)
            if d is not None:
                deps[id(d)] = d
        for k in w:
            d = self.lw.get(k)
            if d is not None:
                deps[id(d)] = d
            for d in self.rd.get(k, ()):
                deps[id(d)] = d
        deps.pop(id(i), None)
        i.deps = [d for d in deps.values() if not (d.eng == "pe" and eng == "pe" and not d.dma and chan is None)]
        if noself:
            i.deps = [d for d in i.deps if d.eng != eng or d.dma]
        for d in i.deps:
            d.need = True
        for k in r:
            self.rd.setdefault(k, []).append(i)
        for k in w:
            self.lw[k] = i
            self.rd[k] = []
        self.ins.append(i)
        return i

    def run_phase(self):
        nc = self.nc
        last = {}
        for i in self.ins:
            if not i.dma:
                last[i.eng] = i
        for i in last.values():
            i.need = True
        for i in self.ins:
            if i.dma:
                if i.chan not in self.dsem:
                    self.dsem[i.chan] = self.es.enter_context(nc.semaphore("d_" + i.chan))
                    self.dcnt[i.chan] = 0
                self.dcnt[i.chan] += 16
                i.sem, i.val = self.dsem[i.chan], self.dcnt[i.chan]
            elif i.need:
                self.ccnt[i.eng] += 1
                i.sem, i.val = self.csem[i.eng], self.ccnt[i.eng]
        self.ninst += len(self.ins)
        allsems = [(self.csem[e], self.ccnt[e]) for e in self.csem] + [(self.dsem[c], self.dcnt[c]) for c in self.dsem]
        ins = self.ins
        with nc.Block() as blk:
            for ename, deco in (("sp", blk.sync), ("pe", blk.tensor), ("act", blk.scalar),
                                ("dve", blk.vector), ("pool", blk.gpsimd)):
                def body(eng, ename=ename):
                    wd = self.waited[ename]
                    for i in ins:
                        if i.eng != ename:
                            continue
                        req = {}
                        for d in i.deps:
                            k = id(d.sem)
                            if k not in req or req[k][1] < d.val:
                                req[k] = (d.sem, d.val)
                        for k, (sem, val) in req.items():
                            if wd.get(k, 0) < val:
                                eng.wait_ge(sem, val)
                                wd[k] = val
                        if i.dma:
                            hist = self.dhist
                            if len(hist) >= self.maxdma:
                                p = hist[-self.maxdma]
                                k = id(p.sem)
                                if wd.get(k, 0) < p.val:
                                    eng.wait_ge(p.sem, p.val)
                                    wd[k] = p.val
                            hist.append(i)
                        r = i.fn(eng)
                        if i.need:
                            r.then_inc(i.sem, 16 if i.dma else 1)
                    for sem, val in allsems:
                        k = id(sem)
                        if val > 0 and wd.get(k, 0) < val:
                            eng.wait_ge(sem, val)
                            wd[k] = val
                deco(body)
        self.reset()


def build(cfg):
    c = cfg
    nc = bass.Bass("TRN2", target_bir_lowering=False)
    A = {}

    def din(name, shape, dt=F32):
        A[name] = nc.dram_tensor(name, list(shape), dt, kind="ExternalInput").ap()

    def dout(name, shape, dt=F32):
        A[name] = nc.dram_tensor(name, list(shape), dt, kind="ExternalOutput").ap()

    def dscr(name, shape, dt=F32):
        A[name] = nc.dram_tensor(name, list(shape), dt, kind="Internal").ap()

    T, NSEQ, NSTR, PAST, NP, NS, LS = c.T, c.NSEQ, c.NSTR, c.PAST, c.NP, c.NS, c.LS
    din("xp", [NP, D]); din("xs", [NS, D]); din("sconv", [NSTR, CS, D])
    din("ck", [NSTR, PAST, D]); din("cv", [NSTR, PAST, D]); din("clf", [NSTR, PAST, H])
    din("norm_g", [2, D]); din("fng", [D]); din("w_conv_in", [D, 3 * D]); din("w_dw", [CW, D])
    din("b_dw", [D]); din("lng", [D]); din("lnb", [D]); din("w_conv_out", [D, D])
    din("w_fox_in", [D, 4 * D + H]); din("b_f", [H]); din("qg", [DH]); din("kg", [DH]); din("w_fox_out", [D, D])
    din("ident", [128, 128]); din("tri", [128, 128])
    dout("yp", [NP, D]); dout("ys", [NS, D]); dout("convp", [NSEQ, CS, D]); dout("convs", [NSTR, CS, D])
    dout("kp", [NP, D]); dout("vp", [NP, D]); dout("lfp", [NP, H])
    dout("ks", [NS, D]); dout("vs", [NS, D]); dout("lfs", [NS, H])
    (dout if c.debug else dscr)("x1", [NP + NS, D])
    dscr("szs", [NP + NS, D], BF16)
    dscr("obuf", [NP + NS, D], BF16)
    dscr("qtp", [NSEQ, H, DH, T], BF16); dscr("ktp", [NSEQ, H, DH, T], BF16); dscr("vbp", [NSEQ, T, D], BF16)
    dscr("augp", [NSEQ, H, 3, T], BF16)
    dscr("qts", [NSTR, H, DH, TS], BF16); dscr("kts", [NSTR, H, DH, LS], BF16); dscr("vbs", [NSTR, LS, D], BF16)
    dscr("augs", [NSTR, H, 3, TS], BF16)

    with ExitStack() as ges:
        S = Sched(nc, ges)
        gsb = lambda name, shape, dt=F32: ges.enter_context(nc.sbuf_tensor(name, list(shape), dt))
        K = {}
        K["ident_f"] = gsb("ident_f", [128, 128]); K["ident_b"] = gsb("ident_b", [128, 128], BF16)
        K["tri_f"] = gsb("tri_f", [128, 128]); K["tri_b"] = gsb("tri_b", [128, 128], BF16)
        K["ones_f"] = gsb("ones_f", [128, 128]); K["ones_b"] = gsb("ones_b", [128, 128], BF16)
        S.add("sp", lambda e: e.dma_start(out=K["ident_f"][:], in_=A["ident"]), w=["ident_f"], chan="ident_f")
        S.add("sp", lambda e: e.dma_start(out=K["tri_f"][:], in_=A["tri"]), w=["tri_f"], chan="tri_f")
        S.add("dve", lambda e: e.tensor_copy(out=K["ident_b"][:], in_=K["ident_f"][:]), r=["ident_f"], w=["ident_b"])
        S.add("dve", lambda e: e.tensor_copy(out=K["tri_b"][:], in_=K["tri_f"][:]), r=["tri_f"], w=["tri_b"])
        S.add("dve", lambda e: e.memset(K["ones_f"][:], 1.0), w=["ones_f"])
        S.add("dve", lambda e: e.memset(K["ones_b"][:], 1.0), w=["ones_b"])
        phase_l0(S, nc, c, A, K)
        alloc_persist(nc, ges, c, K)
        if c.do_l1:
            phase_l1a(S, nc, c, A, K)
            if c.do_l1 > 1:
                phase_l1b(S, nc, c, A, K)
                phase_l1c(S, nc, c, A, K)
        print("bass instructions (logical):", S.ninst)
    return nc


def phase_l0(S, nc, c, A, K):
    T, NSEQ, NSTR, NP, NS = c.T, c.NSEQ, c.NSTR, c.NP, c.NS
    ident_f, ident_b, ones_b = K["ident_f"], K["ident_b"], K["ones_b"]
    with ExitStack() as es:
        sb = lambda name, shape, dt=F32: es.enter_context(nc.sbuf_tensor("l0_" + name, list(shape), dt))
        pm = lambda name, shape, dt=F32: es.enter_context(nc.psum_tensor("l0_" + name, list(shape), dt))
        W0in = sb("W0in", [128, 8, 3 * D], BF16); W0out = sb("W0out", [128, 8, D], BF16)
        vrow = sb("vrow", [24, 128]); vcol = sb("vcol", [128, 24])
        gcol = sb("gcol", [128, 8])
        wdwc = sb("wdwc", [128, 8, 32])
        dg = [sb(f"dg{i}", [128, CW, 128], BF16) for i in range(2)]
        xt = [sb(f"xt{i}", [128, 4, D]) for i in range(2)]
        stg = [xt[i][:, :, :].rearrange("p s (a b) -> p (s a) b", b=512) for i in range(2)]
        hb = sb("hb", [128, 4, D], BF16); hT = sb("hT", [128, 8, 512], BF16)
        junk = sb("junk", [128, D], BF16)
        ssq = sb("ssq", [128, 4]); rstd = sb("rstd", [128, 4])
        vext = [sb(f"vext{i}", [128, 8, 512 + CS], BF16) for i in range(2)]
        sig = [sb(f"sig{i}", [128, 512]) for i in range(2)]
        szT = sb("szT", [128, 8, 512], BF16)
        ybf = sb("ybf", [128, 8, 512], BF16); ysq = [sb(f"ysq{i}", [128, 512], BF16) for i in range(2)]
        mean = sb("mean", [128, 512]); rs = sb("rs", [128, 512]); tmp = [sb(f"tmp{i}", [128, 512]) for i in range(2)]
        mT = sb("mT", [128, 8, 512], BF16)
        xo = [sb(f"xo{i}", [128, D]) for i in range(2)]
        acc = [sb(f"acc{i}", [128, 512]) for i in range(2)]
        vt = sb("vt", [128, 8, 64]); cst = sb("cst", [64, D]); hist = cst; wdwr = cst
        psT = pm("psT", [128, 8, 128], BF16)
        psA = pm("psA", [128, 512]); psG = pm("psG", [128, 512]); psZ = pm("psZ", [128, 512])
        psY = [pm(f"psY{i}", [128, 512]) for i in range(2)]
        psS = [pm(f"psS{i}", [128, 512]) for i in range(2)]

        for i, nm in enumerate(("b_dw", "lng", "lnb")):
            S.add("sp", lambda e, i=i, nm=nm: e.dma_start(out=vrow[i * 8:(i + 1) * 8, :], in_=A[nm].rearrange("(j p) -> j p", p=128)),
                  w=["vrow"], chan="vrow")
        S.add("pe", lambda e: e.transpose(psS[0][:, 0:24], vrow[:], ident_f[0:24, 0:24]), r=["vrow", "ident_f"], w=["psS0"])
        S.add("dve", lambda e: e.tensor_copy(out=vcol[:], in_=psS[0][:, 0:24]), r=["psS0"], w=["vcol"])
        bdw, lng, lnb = vcol[:, 0:8], vcol[:, 8:16], vcol[:, 16:24]
        S.add("sp", lambda e: e.dma_start(out=vrow[0:8, :], in_=A["norm_g"][0].rearrange("(j p) -> j p", p=128)),
              r=["vrow"], w=["vrow"], chan="vrow")
        S.add("pe", lambda e: e.transpose(psS[1][:, 0:8], vrow[0:8, :], ident_f[0:8, 0:8]), r=["vrow", "ident_f"], w=["psS1"])
        S.add("dve", lambda e: e.tensor_copy(out=gcol[:], in_=psS[1][:, 0:8]), r=["psS1"], w=["gcol"])
        S.add("sp", lambda e: e.dma_start(out=wdwr[0:CW, :], in_=A["w_dw"]), w=["cst"], chan="cst")

        def trw(e):
            for j in range(8):
                r_ = e.transpose(psS[0][:, j * 32:j * 32 + CW], wdwr[0:CW, j * 128:(j + 1) * 128], ident_f[0:CW, 0:CW])
            return r_
        S.add("pe", trw, r=["cst", "ident_f", "vcol"], w=["psS0"])
        S.add("dve", lambda e: e.tensor_copy(out=wdwc[:, :, 0:CW], in_=psS[0][:, 0:256].rearrange("p (j k) -> p j k", k=32)[:, :, 0:CW]),
              r=["psS0"], w=["wdwc"])
        win_v = A["w_conv_in"].rearrange("(c p) e -> p c e", p=128)
        for pc in range(6):
            st = stg[pc % 2]
            S.add("sp", lambda e, st=st, pc=pc: e.dma_start(out=st, in_=win_v[:, :, pc * 512:(pc + 1) * 512]),
                  w=[f"xt{pc % 2}"], chan=f"xt{pc % 2}")
            S.add("dve", lambda e, st=st, pc=pc: e.tensor_tensor(out=W0in[:, :, pc * 512:(pc + 1) * 512], in0=st,
                                                               in1=gcol[:].unsqueeze(2).to_broadcast([128, 8, 512]), op=ALU.mult),
                  r=[f"xt{pc % 2}", "gcol"], w=["W0in"])
        wout_v = A["w_conv_out"].rearrange("(c p) e -> p c e", p=128)
        for pc in range(2):
            st = stg[pc % 2]
            S.add("sp", lambda e, st=st, pc=pc: e.dma_start(out=st, in_=wout_v[:, :, pc * 512:(pc + 1) * 512]),
                  w=[f"xt{pc % 2}"], chan=f"xt{pc % 2}")
            S.add("act", lambda e, st=st, pc=pc: e.activation(out=W0out[:, :, pc * 512:(pc + 1) * 512], in_=st, func=AF.Copy),
                  r=[f"xt{pc % 2}"], w=["W0out"])

        tiles = []
        for q in range(NSEQ):
            for i in range(T // 512):
                tiles.append(dict(kind="p", q=q, i=i, tok0=q * T + i * 512, PT=128, ns=4, G=1, n=512,
                                  first=(i == 0), last=(i == T // 512 - 1)))
        tiles.append(dict(kind="s", tok0=NP, PT=NS, ns=1, G=NSTR, n=TS, first=True, last=True))
        vkeys = lambda sl: [f"vext{sl}_{j}" for j in range(8)]

        def load_x(ti):
            t = tiles[ti]
            sl = ti % 2
            PT, ns = t["PT"], t["ns"]
            if t["kind"] == "p":
                src = A["xp"][t["tok0"]:t["tok0"] + 512, :].rearrange("(s p) d -> p s d", p=128)
            else:
                src = A["xs"][:, :].rearrange("(s p) d -> p s d", p=PT)
            S.add("sp", lambda e: e.dma_start(out=xt[sl][0:PT, 0:ns, :], in_=src), w=[f"xt{sl}"], chan=f"xt{sl}")

        def do_tile(ti, t):
            sl = ti % 2
            X = xt[sl]
            PT, ns, G, n = t["PT"], t["ns"], t["G"], t["n"]
            NT = PT * ns
            seg = CS + n
            if ti + 1 < len(tiles):
                load_x(ti + 1)
            for s in range(ns):
                S.add("act", lambda e, s=s: e.activation(out=junk[0:PT, :], in_=X[0:PT, s, :], func=AF.Square,
                                                         accum_out=ssq[0:PT, s:s + 1]),
                      r=[f"xt{sl}"], w=["junk", "ssq"])
            S.add("act", lambda e: e.activation(out=rstd[0:PT, 0:ns], in_=ssq[0:PT, 0:ns], func=AF.Sqrt, scale=1.0 / D, bias=EPS),
                  r=["ssq"], w=["rstd"])
            S.add("dve", lambda e: e.reciprocal(out=rstd[0:PT, 0:ns], in_=rstd[0:PT, 0:ns]), r=["rstd"], w=["rstd"])
            for s in range(ns):
                S.add("act", lambda e, s=s: e.activation(out=hb[0:PT, s, :], in_=X[0:PT, s, :], func=AF.Copy, scale=rstd[0:PT, s:s + 1]),
                      r=[f"xt{sl}", "rstd"], w=[f"hb{s}"])

                def tr(e, s=s):
                    for cc in range(8):
                        r_ = e.transpose(psT[:, cc, 0:PT], hb[0:PT, s, cc * 128:(cc + 1) * 128], ident_b[0:PT, 0:PT])
                    return r_
                S.add("pe", tr, r=[f"hb{s}", "ident_b"], w=["psT"])
                S.add("dve", lambda e, s=s: e.tensor_copy(out=hT[:, :, s * PT:(s + 1) * PT], in_=psT[:, :, 0:PT]), r=["psT"], w=["hT"])
            if t["kind"] == "p":
                if t["first"]:
                    S.add("pool", lambda e: e.memset(vext[sl][:, :, 0:CS], 0.0), w=vkeys(sl))
                else:
                    S.add("pool", lambda e: e.tensor_copy(out=vext[sl][:, :, 0:CS], in_=vext[1 - sl][:, :, 512:512 + CS]),
                          r=vkeys(1 - sl), w=vkeys(sl))
            else:
                for g in range(G):
                    S.add("sp", lambda e, g=g: e.dma_start(out=hist[0:CS, :], in_=A["sconv"][g]), w=["cst"], chan="cst")

                    def trh(e):
                        for j in range(8):
                            r_ = e.transpose(psS[0][:, j * 32:j * 32 + CS], hist[0:CS, j * 128:(j + 1) * 128], ident_f[0:CS, 0:CS])
                        return r_
                    S.add("pe", trh, r=["cst", "ident_f"], w=["psS0"])
                    S.add("dve", lambda e, g=g: e.tensor_copy(out=vext[sl][:, :, g * seg:g * seg + CS],
                                                              in_=psS[0][:, 0:256].rearrange("p (j k) -> p j k", k=32)[:, :, 0:CS]),
                          r=["psS0"], w=vkeys(sl))
            for j in range(8):
                for ps, off, key in ((psG, D, "psG"), (psA, 0, "psA"), (psZ, 2 * D, "psZ")):
                    def mm(e, ps=ps, col=off + j * 128):
                        for cc in range(8):
                            r_ = e.matmul(ps[:, 0:NT], lhsT=W0in[:, cc, col:col + 128], rhs=hT[:, cc, 0:NT], start=(cc == 0), stop=(cc == 7))
                        return r_
                    S.add("pe", mm, r=["hT", "W0in"], w=[key])
                sg = sig[j % 2]
                S.add("act", lambda e, sg=sg: e.activation(out=sg[:, 0:NT], in_=psG[:, 0:NT], func=AF.Sigmoid), r=["psG"], w=[f"sig{j % 2}"])
                vdst = vext[sl][:, j, 0:G * seg].rearrange("p (g w) -> p g w", w=seg)[:, :, CS:seg]
                S.add("dve", lambda e, sg=sg, vdst=vdst: e.tensor_tensor(out=vdst, in0=psA[:, 0:NT].rearrange("p (g w) -> p g w", w=n),
                                                                         in1=sg[:, 0:NT].rearrange("p (g w) -> p g w", w=n), op=ALU.mult),
                      r=["psA", f"sig{j % 2}"], w=[f"vext{sl}_{j}"])
                if t["last"]:
                    lo, cnt = (NT - CS, CS) if t["kind"] == "p" else (0, NT)
                    S.add("dve", lambda e, sg=sg, j=j, lo=lo, cnt=cnt: e.tensor_tensor(out=vt[:, j, 0:cnt], in0=psA[:, lo:lo + cnt],
                                                                                     in1=sg[:, lo:lo + cnt], op=ALU.mult),
                          r=["psA", f"sig{j % 2}"], w=["vt"])
                S.add("act", lambda e, j=j: e.activation(out=szT[:, j, 0:NT], in_=psZ[:, 0:NT], func=AF.Silu), r=["psZ"], w=[f"sz{j}"])
            if t["last"]:
                cnt = CS if t["kind"] == "p" else NT

                def trv(e, cnt=cnt):
                    for j in range(8):
                        r_ = e.transpose(psS[j // 4][0:cnt, (j % 4) * 128:(j % 4 + 1) * 128], vt[:, j, 0:cnt], ident_f[:])
                    return r_
                S.add("pe", trv, r=["vt", "ident_f"], w=["psS0", "psS1"])
                for hf in range(2):
                    S.add("act", lambda e, hf=hf, cnt=cnt: e.activation(out=cst[0:cnt, hf * 512:(hf + 1) * 512], in_=psS[hf][0:cnt, :], func=AF.Copy),
                          r=[f"psS{hf}"], w=["cst"])
                if t["kind"] == "p":
                    S.add("pool", lambda e, q=t["q"]: e.dma_start(out=A["convp"][q], in_=cst[0:CS, :]), r=["cst"], chan="cst")
                else:
                    for g in range(G):
                        S.add("pool", lambda e, g=g: e.dma_start(out=A["convs"][g, CS - TS:CS, :], in_=cst[g * TS:(g + 1) * TS, :]), r=["cst"], chan="cst")
                    S.add("pool", lambda e: e.dma_start(out=A["convs"][:, 0:CS - TS, :], in_=A["sconv"][:, TS:CS, :]), chan="d2d")
            def gen_diag(j):
                dj = dg[j % 2]
                S.add("dve", lambda e: e.tensor_tensor(out=dj[:, 0:25, :], in0=ident_b[:].unsqueeze(1).to_broadcast([128, 25, 128]),
                                                       in1=wdwc[:, j, 0:25].unsqueeze(2).to_broadcast([128, 25, 128]), op=ALU.mult),
                      r=["wdwc", "ident_b"], w=[f"dg{j % 2}"])
            gen_diag(0)
            for j in range(8):
                dj = dg[j % 2]

                NPE = 25

                def cv(e, dj=dj, j=j):
                    for g in range(G):
                        for k in range(NPE):
                            r_ = e.matmul(psY[j % 2][:, g * n:(g + 1) * n], lhsT=dj[:, k, :], rhs=vext[sl][:, j, g * seg + k:g * seg + k + n],
                                          start=(k == 0), stop=(k == NPE - 1))
                    return r_
                S.add("pe", cv, r=[f"dg{j % 2}", f"vext{sl}_{j}"], w=[f"psY{j % 2}"])
                if j + 1 < 8:
                    gen_diag(j + 1)
                aj = acc[j % 2]
                av = aj[:, 0:NT].rearrange("p (g w) -> p g w", w=n)
                vsh = lambda k, j=j: vext[sl][:, j, 0:G * seg].rearrange("p (g w) -> p g w", w=seg)[:, :, k:k + n]
                for k in range(NPE, CW):
                    if k == NPE:
                        S.add("dve", lambda e, k=k, j=j, av=av, vsh=vsh: e.tensor_scalar(out=av, in0=vsh(k), scalar1=wdwc[:, j, k:k + 1], scalar2=None, op0=ALU.mult),
                              r=[f"vext{sl}_{j}", "wdwc"], w=[f"acc{j % 2}"])
                    else:
                        S.add("dve", lambda e, k=k, j=j, av=av, vsh=vsh: e.scalar_tensor_tensor(out=av, in0=vsh(k), scalar=wdwc[:, j, k:k + 1], in1=av, op0=ALU.mult, op1=ALU.add),
                              r=[f"vext{sl}_{j}", "wdwc", f"acc{j % 2}"], w=[f"acc{j % 2}"])
                S.add("dve", lambda e, j=j, aj=aj: e.scalar_tensor_tensor(out=ybf[:, j, 0:NT], in0=psY[j % 2][:, 0:NT], scalar=bdw[:, j:j + 1], in1=aj[:, 0:NT],
                                                                        op0=ALU.add, op1=ALU.add),
                      r=[f"psY{j % 2}", "vcol", f"acc{j % 2}"], w=[f"ybf{j}"])
                S.add("dve", lambda e, j=j: e.tensor_tensor(out=ysq[j % 2][:, 0:NT], in0=ybf[:, j, 0:NT], in1=ybf[:, j, 0:NT], op=ALU.mult),
                      r=[f"ybf{j}"], w=[f"ysq{j % 2}"])
                S.add("pe", lambda e, j=j: e.matmul(psS[0][:, 0:NT], lhsT=ones_b[:], rhs=ybf[:, j, 0:NT], start=(j == 0), stop=(j == 7)),
                      r=[f"ybf{j}", "ones_b"], w=["psS0"])
                S.add("pe", lambda e, j=j: e.matmul(psS[1][:, 0:NT], lhsT=ones_b[:], rhs=ysq[j % 2][:, 0:NT], start=(j == 0), stop=(j == 7)),
                      r=[f"ysq{j % 2}", "ones_b"], w=["psS1"])
            S.add("act", lambda e: e.activation(out=mean[:, 0:NT], in_=psS[0][:, 0:NT], func=AF.Copy, scale=1.0 / D), r=["psS0"], w=["mean"])
            S.add("dve", lambda e: e.tensor_tensor(out=tmp[0][:, 0:NT], in0=mean[:, 0:NT], in1=mean[:, 0:NT], op=ALU.mult), r=["mean"], w=["tmp0"])
            S.add("dve", lambda e: e.scalar_tensor_tensor(out=rs[:, 0:NT], in0=psS[1][:, 0:NT], scalar=1.0 / D, in1=tmp[0][:, 0:NT],
                                                          op0=ALU.mult, op1=ALU.subtract), r=["psS1", "tmp0"], w=["rs"])
            S.add("act", lambda e: e.activation(out=rs[:, 0:NT], in_=rs[:, 0:NT], func=AF.Sqrt, bias=EPS), r=["rs"], w=["rs"])
            S.add("dve", lambda e: e.reciprocal(out=rs[:, 0:NT], in_=rs[:, 0:NT]), r=["rs"], w=["rs"])
            for j in range(8):
                tj = tmp[j % 2]
                S.add("dve", lambda e, j=j, tj=tj: e.tensor_tensor(out=tj[:, 0:NT], in0=ybf[:, j, 0:NT], in1=mean[:, 0:NT], op=ALU.subtract),
                      r=[f"ybf{j}", "mean"], w=[f"tmp{j % 2}"])
                S.add("pool", lambda e, tj=tj: e.tensor_tensor(out=tj[:, 0:NT], in0=tj[:, 0:NT], in1=rs[:, 0:NT], op=ALU.mult),
                      r=[f"tmp{j % 2}", "rs"], w=[f"tmp{j % 2}"])
                S.add("act", lambda e, j=j, tj=tj: e.activation(out=tj[:, 0:NT], in_=tj[:, 0:NT], func=AF.Silu, scale=lng[:, j:j + 1], bias=lnb[:, j:j + 1]),
                      r=[f"tmp{j % 2}", "vcol"], w=[f"tmp{j % 2}"])
                S.add("dve", lambda e, j=j, tj=tj: e.tensor_tensor(out=mT[:, j, 0:NT], in0=tj[:, 0:NT], in1=szT[:, j, 0:NT], op=ALU.mult),
                      r=[f"tmp{j % 2}", f"sz{j}"], w=["mT"])
            for s in range(ns):
                for hf in range(2):
                    def op(e, s=s, hf=hf):
                        for j in range(8):
                            r_ = e.matmul(psS[hf][0:PT, :], lhsT=mT[:, j, s * PT:(s + 1) * PT], rhs=W0out[:, j, hf * 512:(hf + 1) * 512],
                                          start=(j == 0), stop=(j == 7))
                        return r_
                    S.add("pe", op, r=["mT", "W0out"], w=[f"psS{hf}"])
                    S.add("dve", lambda e, s=s, hf=hf: e.tensor_tensor(out=xo[s % 2][0:PT, hf * 512:(hf + 1) * 512], in0=psS[hf][0:PT, :],
                                                                     in1=X[0:PT, s, hf * 512:(hf + 1) * 512], op=ALU.add),
                          r=[f"psS{hf}", f"xt{sl}"], w=[f"xo{s % 2}"])
                r0 = t["tok0"] + s * PT
                S.add("pool", lambda e, s=s, r0=r0: e.dma_start(out=A["x1"][r0:r0 + PT, :], in_=xo[s % 2][0:PT, :]), r=[f"xo{s % 2}"], chan=f"xo{s % 2}")
        load_x(0)
        for ti_, t_ in enumerate(tiles):
            do_tile(ti_, t_)
        S.run_phase()


def make_in_maps(cfg, inputs):
    c = cfg
    f = lambda a: np.ascontiguousarray(np.asarray(a, dtype=np.float32))
    ident = np.eye(128, dtype=np.float32)
    tri = np.triu(np.ones((128, 128), dtype=np.float32))
    n_cores = inputs["x_prompt"].shape[0] // c.NSEQ
    maps = []
    for i in range(n_cores):
        sq = slice(i * c.NSEQ, (i + 1) * c.NSEQ)
        st = slice(i * c.NSTR, (i + 1) * c.NSTR)
        m = {
            "xp": f(inputs["x_prompt"][sq]).reshape(c.NP, D),
            "xs": f(inputs["x_sample"][st]).reshape(c.NS, D),
            "sconv": f(inputs["state_conv"][0, st]),
            "ck": f(inputs["cache_k"][0, st]).reshape(c.NSTR, c.PAST, D),
            "cv": f(inputs["cache_v"][0, st]).reshape(c.NSTR, c.PAST, D),
            "clf": f(inputs["cache_logf"][0, st]),
            "norm_g": f(inputs["norm_g"]), "fng": f(inputs["final_norm_g"]),
            "w_conv_in": f(inputs["w_conv_in"][0]), "w_dw": f(inputs["w_dw"][0]), "b_dw": f(inputs["b_dw"][0]),
            "lng": f(inputs["conv_ln_g"][0]), "lnb": f(inputs["conv_ln_b"][0]), "w_conv_out": f(inputs["w_conv_out"][0]),
            "w_fox_in": f(inputs["w_fox_in"][0]), "b_f": f(inputs["b_forget"][0]), "qg": f(inputs["q_norm_g"][0]),
            "kg": f(inputs["k_norm_g"][0]), "w_fox_out": f(inputs["w_fox_out"][0]),
            "ident": ident, "tri": tri,
        }
        maps.append(m)
    return maps


def seq_list(c):
    L = []
    for q in range(c.NSEQ):
        L.append(dict(kind="p", idx=q, nfull=c.T // 128, part=0, nqt=c.T // 512, key=f"p{q}"))
    for g in range(c.NSTR):
        L.append(dict(kind="s", idx=g, nfull=c.PAST // 128, part=TS, nqt=1, key=f"s{g}"))
    return L


def alloc_persist(nc, ges, c, K):
    gsb = lambda name, shape, dt=F32: ges.enter_context(nc.sbuf_tensor("l0_" + name, list(shape), dt))
    for sq in seq_list(c):
        nb = sq["nfull"] + (1 if sq["part"] else 0)
        K["C_" + sq["key"]] = gsb("C_" + sq["key"], [128, nb, H])
        K["B_" + sq["key"]] = gsb("B_" + sq["key"], [128, sq["nqt"], H])


def phase_l1a(S, nc, c, A, K):
    import os
    S.nadd = 0
    S.limit = int(os.environ['L1A_LIMIT']) if 'L1A_LIMIT' in os.environ else None
    T, NSEQ, NSTR, NP, NS, PAST = c.T, c.NSEQ, c.NSTR, c.NP, c.NS, c.PAST
    ident_f, ident_b, ones_f, tri_f = K["ident_f"], K["ident_b"], K["ones_f"], K["tri_f"]
    NBMAX = max(T // 128, PAST // 128 + 1)
    with ExitStack() as es:
        sb = lambda name, shape, dt=F32: es.enter_context(nc.sbuf_tensor("l1a_" + name, list(shape), dt))
        pm = lambda name, shape, dt=F32: es.enter_context(nc.psum_tensor("l1a_" + name, list(shape), dt))
        NW = 4 * D + H
        W1in = sb("W1in", [128, 8, NW], BF16)
        xt = [sb(f"xt{i}", [128, 4, D]) for i in range(2)]
        stg = [xt[i][:, :, :].rearrange("p s (a b) -> p (s a) b", b=512) for i in range(2)]
        vrow = sb("vrow", [8, 128]); gcol = sb("gcol", [128, 8])
        bfb = sb("bfb", [128, H]); gqb = sb("gqb", [128, DH]); gkb = sb("gkb", [128, DH])
        hbs = [sb(f"hb{i}", [128, D], BF16) for i in range(2)]; hT = sb("hT", [128, 8, 512], BF16)
        ssq = sb("ssq", [128, 4]); rstd = sb("rstd", [128, 4])
        sq = sb("sq", [128, D]); ssh = sb("ssh", [128, 2 * H])
        kf = [sb(f"kf{i}", [128, D]) for i in range(2)]
        vf = [sb(f"vf{i}", [128, D]) for i in range(2)]
        qbs = [sb(f"qb{i}", [128, D], BF16) for i in range(2)]; kbs = [sb(f"kb{i}", [128, D], BF16) for i in range(2)]
        kb = kbs[0]
        vb = [sb(f"vb{i}", [128, D], BF16) for i in range(2)]
        szb = [sb(f"szb{i}", [128, D], BF16) for i in range(2)]
        qTs = sb("qTs", [64, H, 512], BF16); kTs = sb("kTs", [64, H, 512], BF16)
        lft = sb("lft", [128, 4 * H])
        lf = sb("lf", [128, NBMAX, H]); psa = sb("psa", [128, NBMAX + 1, H])
        clt = sb("clt", [128, NBMAX, H]); clT = sb("clT", [16, 512]); r1 = sq[0:16, 0:512]
        aug = sb("aug", [16, 3, 512], BF16)
        psT = pm("psT", [128, 8, 128], BF16)
        psP = [pm(f"psP{i}", [128, 512]) for i in range(4)]
        psQ = [pm(f"psQ{i}", [64, 8, 128], BF16) for i in range(2)]
        psF = pm("psF", [128, 512])

        S.add("sp", lambda e: e.dma_start(out=vrow[0:8, :], in_=A["norm_g"][1].rearrange("(j p) -> j p", p=128)), w=["vrow"], chan="vrow")
        S.add("pe", lambda e: e.transpose(psF[:, 0:8], vrow[0:8, :], ident_f[0:8, 0:8]), r=["vrow"], w=["psF"])
        S.add("dve", lambda e: e.tensor_copy(out=gcol[:], in_=psF[:, 0:8]), r=["psF"], w=["gcol"])
        S.add("sp", lambda e: e.dma_start(out=bfb[:], in_=A["b_f"].partition_broadcast(128)), w=["bfb"], chan="bfb")
        S.add("sp", lambda e: e.dma_start(out=gqb[:], in_=A["qg"].partition_broadcast(128)), w=["gqb"], chan="gqb")
        S.add("sp", lambda e: e.dma_start(out=gkb[:], in_=A["kg"].partition_broadcast(128)), w=["gkb"], chan="gkb")
        S.add("dve", lambda e: e.tensor_scalar_mul(out=gqb[:], in0=gqb[:], scalar1=0.125), r=["gqb"], w=["gqb"])
        win_v = A["w_fox_in"].rearrange("(c p) e -> p c e", p=128)
        for pc in range(9):
            st = stg[pc % 2]
            wd = 512 if pc < 8 else H
            S.add("sp", lambda e, st=st, pc=pc, wd=wd: e.dma_start(out=st[:, :, 0:wd], in_=win_v[:, :, pc * 512:pc * 512 + wd]),
                  w=[f"xt{pc % 2}"], chan=f"xt{pc % 2}")
            S.add("dve", lambda e, st=st, pc=pc, wd=wd: e.tensor_tensor(out=W1in[:, :, pc * 512:pc * 512 + wd], in0=st[:, :, 0:wd],
                                                                      in1=gcol[:].unsqueeze(2).to_broadcast([128, 8, wd]), op=ALU.mult),
                  r=[f"xt{pc % 2}", "gcol"], w=["W1in"])

        tiles = []
        for q in range(NSEQ):
            for i in range(T // 512):
                tiles.append(dict(kind="p", q=q, i=i, tok0=q * T + i * 512, PT=128, ns=4, last=(i == T // 512 - 1)))
        tiles.append(dict(kind="s", tok0=NP, PT=NS, ns=1, last=True))
        for g in range(NSTR):
            for i in range(PAST // 512):
                tiles.append(dict(kind="c", g=g, i=i, PT=128, ns=4, last=False))
        cnt = {"st": 0}

        def load_x(ti):
            t = tiles[ti]
            sl = ti % 2
            PT, ns = t["PT"], t["ns"]
            if t["kind"] == "c":
                src = A["ck"][t["g"], t["i"] * 512:(t["i"] + 1) * 512, :].rearrange("(s p) d -> p s d", p=128)
            else:
                src = A["x1"][t["tok0"]:t["tok0"] + PT * ns, :].rearrange("(s p) d -> p s d", p=PT)
            S.add("sp", lambda e: e.dma_start(out=xt[sl][0:PT, 0:ns, :], in_=src), w=[f"xt{sl}"], chan=f"xt{sl}")

        def transposes_to(src_b, PT, dstT, s, rkey, dkey):
            for half in range(2):
                def tr(e, half=half):
                    for hh in range(8):
                        h = half * 8 + hh
                        r_ = e.transpose(psQ[half][:, hh, 0:PT], src_b[0:PT, h * 64:(h + 1) * 64], ident_b[0:PT, 0:PT])
                    return r_
                S.add("pe", tr, r=[rkey, "ident_b"], w=[f"psQ{half}"])
                S.add("act", lambda e, half=half: e.activation(out=dstT[:, half * 8:(half + 1) * 8, s * PT:(s + 1) * PT],
                                                               in_=psQ[half][:, :, 0:PT], func=AF.Copy),
                      r=[f"psQ{half}"], w=[dkey])

        def cumsum_seq(sqd):
            nfull, part, key = sqd["nfull"], sqd["part"], sqd["key"]
            nb = nfull + (1 if part else 0)
            Call, Ball = K["C_" + key], K["B_" + key]
            S.add("dve", lambda e: e.memset(psa[:, 0, :], 0.0), w=["psa"])
            for b in range(nfull):
                S.add("dve", lambda e, b=b: e.tensor_tensor(out=psa[:, b + 1, :], in0=psa[:, b, :], in1=lf[:, b, :], op=ALU.add),
                      r=["lf", "psa"], w=["psa"])

            for c0 in range(0, nb, 32):
                c1 = min(nb, c0 + 32)

                def cs(e, c0=c0, c1=c1):
                    for b in range(c0, c1):
                        rows = 128 if b < nfull else part
                        o = psF[0:rows, (b - c0) * H:(b - c0 + 1) * H]
                        e.matmul(o, lhsT=tri_f[0:rows, 0:rows], rhs=lf[0:rows, b, :], start=True, stop=False)
                        r_ = e.matmul(o, lhsT=ones_f[:, 0:rows], rhs=psa[:, b, :], start=False, stop=True)
                    return r_
                S.add("pe", cs, r=["lf", "psa", "tri_f", "ones_f"], w=["psF"])
                f1 = min(c1, nfull)
                if f1 > c0:
                    S.add("dve", lambda e, c0=c0, f1=f1: e.tensor_copy(out=Call[:, c0:f1, :], in_=psF[:, 0:(f1 - c0) * H].rearrange("p (b h) -> p b h", h=H)),
                          r=["psF"], w=["C_" + key])
                if c1 > nfull:
                    S.add("dve", lambda e, c0=c0: e.tensor_copy(out=Call[0:part, nfull, :], in_=psF[0:part, (nfull - c0) * H:(nfull - c0 + 1) * H]),
                          r=["psF"], w=["C_" + key])
            nqt = sqd["nqt"]
            bidx = [4 * i for i in range(nqt)] if sqd["kind"] == "p" else [nfull]

            def bs(e):
                for i, bi in enumerate(bidx):
                    r_ = e.matmul(psF[:, i * H:(i + 1) * H], lhsT=ones_f[:], rhs=psa[:, bi, :], start=True, stop=True)
                return r_
            S.add("pe", bs, r=["psa", "ones_f", "C_" + key], w=["psF"])
            S.add("dve", lambda e: e.tensor_copy(out=Ball[:, :, :], in_=psF[:, 0:nqt * H].rearrange("p (b h) -> p b h", h=H)),
                  r=["psF"], w=["B_" + key])
            if sqd["kind"] == "p":
                S.add("dve", lambda e: e.tensor_tensor(out=clt[:, 0:nb, :].rearrange("p (q s) h -> p q s h", s=4),
                                                       in0=Call[:, 0:nb, :].rearrange("p (q s) h -> p q s h", s=4),
                                                       in1=Ball[:, :, :].unsqueeze(2).to_broadcast([128, nqt, 4, H]), op=ALU.subtract),
                      r=["C_" + key, "B_" + key], w=["clt"])
                qts = [(4 * i, 4, 128) for i in range(nqt)]
            else:
                S.add("dve", lambda e: e.tensor_tensor(out=clt[0:part, nfull, :], in0=Call[0:part, nfull, :], in1=Ball[0:part, 0, :], op=ALU.subtract),
                      r=["C_" + key, "B_" + key], w=["clt"])
                qts = [(nfull, 1, part)]
            for qi, (b0, nbq, rows) in enumerate(qts):
                wq = nbq * rows

                def trc(e, b0=b0, nbq=nbq, rows=rows):
                    for bb in range(nbq):
                        r_ = e.transpose(psF[0:16, bb * rows:(bb + 1) * rows], clt[0:rows, b0 + bb, :], ident_f[0:rows, 0:rows])
                    return r_
                S.add("pe", trc, r=["clt", "ident_f", "B_" + key], w=["psF"])
                S.add("act", lambda e, wq=wq: e.activation(out=clT[:, 0:wq], in_=psF[0:16, 0:wq], func=AF.Copy), r=["psF"], w=["clT"])
                S.add("dve", lambda e, wq=wq: e.tensor_copy(out=aug[:, 0, 0:wq], in_=clT[:, 0:wq]), r=["clT"], w=["aug"])
                S.add("dve", lambda e, wq=wq: e.tensor_tensor(out=r1[:, 0:wq], in0=clT[:, 0:wq], in1=aug[:, 0, 0:wq], op=ALU.subtract), r=["clT", "aug"], w=["sq"])
                S.add("dve", lambda e, wq=wq: e.tensor_copy(out=aug[:, 1, 0:wq], in_=r1[:, 0:wq]), r=["sq"], w=["aug"])
                S.add("dve", lambda e, wq=wq: e.tensor_tensor(out=r1[:, 0:wq], in0=r1[:, 0:wq], in1=aug[:, 1, 0:wq], op=ALU.subtract), r=["sq", "aug"], w=["sq"])
                S.add("dve", lambda e, wq=wq: e.tensor_copy(out=aug[:, 2, 0:wq], in_=r1[:, 0:wq]), r=["sq"], w=["aug"])
                if sqd["kind"] == "p":
                    dst = A["augp"][sqd["idx"]][:, :, qi * 512:(qi + 1) * 512]
                else:
                    dst = A["augs"][sqd["idx"]][:, :, :]
                S.add("pool", lambda e, dst=dst, wq=wq: e.dma_start(out=dst, in_=aug[:, :, 0:wq]), r=["aug"], chan="aug")

        def do_tile(ti, t):
            sl = ti % 2
            X = xt[sl]
            PT, ns = t["PT"], t["ns"]
            NT = PT * ns
            kind = t["kind"]
            if ti + 1 < len(tiles):
                load_x(ti + 1)
            if kind == "c":
                g, i = t["g"], t["i"]
                for s in range(ns):
                    S.add("act", lambda e, s=s: e.activation(out=kb[:, :], in_=X[:, s, :], func=AF.Copy), r=[f"xt{sl}"], w=["kb0"])
                    transposes_to(kb, 128, kTs, s, "kb0", "kTs")
                S.add("pool", lambda e: e.dma_start(out=A["kts"][g][:, :, i * 512:(i + 1) * 512].rearrange("h d t -> d h t"), in_=kTs[:, :, :]),
                      r=["kTs"], chan="kTs")
                for s in range(ns):
                    k2 = cnt["st"] % 2
                    cnt["st"] += 1
                    r0 = i * 512 + s * 128
                    S.add("sp", lambda e, k2=k2, r0=r0: e.dma_start(out=vf[k2][:, :], in_=A["cv"][g, r0:r0 + 128, :]), w=[f"vf{k2}"], chan=f"vf{k2}")
                    S.add("dve", lambda e, k2=k2: e.tensor_copy(out=vb[k2][:, :], in_=vf[k2][:, :]), r=[f"vf{k2}"], w=[f"vb{k2}"])
                    S.add("pool", lambda e, k2=k2, r0=r0: e.dma_start(out=A["vbs"][g][r0:r0 + 128, :], in_=vb[k2][:, :]),
                          r=[f"vb{k2}"], chan=f"vb{k2}")
                return
            for s in range(ns):
                S.add("act", lambda e, s=s: e.activation(out=sq[0:PT, :], in_=X[0:PT, s, :], func=AF.Square, accum_out=ssq[0:PT, s:s + 1]),
                      r=[f"xt{sl}"], w=["sq", "ssq"])
            S.add("act", lambda e: e.activation(out=rstd[0:PT, 0:ns], in_=ssq[0:PT, 0:ns], func=AF.Sqrt, scale=1.0 / D, bias=EPS), r=["ssq"], w=["rstd"])
            S.add("dve", lambda e: e.reciprocal(out=rstd[0:PT, 0:ns], in_=rstd[0:PT, 0:ns]), r=["rstd"], w=["rstd"])
            for s in range(ns):
                S.add("act", lambda e, s=s: e.activation(out=hbs[s % 2][0:PT, :], in_=X[0:PT, s, :], func=AF.Copy, scale=rstd[0:PT, s:s + 1]),
                      r=[f"xt{sl}", "rstd"], w=[f"hb{s % 2}"])

                def tr(e, s=s):
                    for cc in range(8):
                        r_ = e.transpose(psT[:, cc, 0:PT], hbs[s % 2][0:PT, cc * 128:(cc + 1) * 128], ident_b[0:PT, 0:PT])
                    return r_
                S.add("pe", tr, r=[f"hb{s % 2}", "ident_b"], w=["psT"])
                S.add("dve", lambda e, s=s: e.tensor_copy(out=hT[:, :, s * PT:(s + 1) * PT], in_=psT[:, :, 0:PT]), r=["psT"], w=["hT"])
            def make_sub(s):
                k2 = cnt["st"] % 2
                cnt["st"] += 1
                r0 = t["tok0"] + s * PT

                def proj(ps, col, wd, s=s):
                    def mm(e):
                        for cc in range(8):
                            r_ = e.matmul(ps[0:PT, 0:wd], lhsT=hT[:, cc, s * PT:(s + 1) * PT], rhs=W1in[:, cc, col:col + wd], start=(cc == 0), stop=(cc == 7))
                        return r_
                    return mm
                qb = qbs[s % 2]
                kb = kbs[s % 2]
                qbk, kbk = f"qb{s % 2}", f"kb{s % 2}"
                def qk_pe(which):
                    for hf in range(2):
                        S.add("pe", proj(psP[hf], which * D + hf * 512, 512), r=["hT", "W1in"], w=[f"psP{hf}"])

                def qk_sq(which):
                    for hf in range(2):
                        S.add("act", lambda e, hf=hf: e.activation(out=sq[0:PT, hf * 512:(hf + 1) * 512], in_=psP[hf][0:PT, :], func=AF.Square),
                              r=[f"psP{hf}"], w=["sq"])
                    S.add("dve", lambda e: e.tensor_reduce(out=ssh[0:PT, 0:H], in_=sq[0:PT, :].rearrange("p (h d) -> p h d", d=DH), axis=AX.X, op=ALU.add),
                          r=["sq"], w=["ssh"])

                def qk_sqrt(which):
                    S.add("act", lambda e: e.activation(out=ssh[0:PT, H:2 * H], in_=ssh[0:PT, 0:H], func=AF.Sqrt, scale=1.0 / DH, bias=EPS), r=["ssh"], w=["ssh"])
                    S.add("dve", lambda e: e.reciprocal(out=ssh[0:PT, H:2 * H], in_=ssh[0:PT, H:2 * H]), r=["ssh"], w=["ssh"])

                def qk_mul(which):
                    for hf in range(2):
                        S.add("dve", lambda e, hf=hf: e.tensor_tensor(out=sq[0:PT, hf * 512:(hf + 1) * 512].rearrange("p (h d) -> p h d", d=DH),
                                                                      in0=psP[hf][0:PT, :].rearrange("p (h d) -> p h d", d=DH),
                                                                      in1=ssh[0:PT, H + hf * 8:H + hf * 8 + 8].unsqueeze(2).to_broadcast([PT, 8, DH]), op=ALU.mult),
                              r=[f"psP{hf}", "ssh"], w=["sq"])
                    if which == 0:
                        S.add("dve", lambda e: e.tensor_tensor(out=qb[0:PT, :].rearrange("p (h d) -> p h d", d=DH), in0=sq[0:PT, :].rearrange("p (h d) -> p h d", d=DH),
                                                               in1=gqb[0:PT, :].unsqueeze(1).to_broadcast([PT, H, DH]), op=ALU.mult),
                              r=["sq", "gqb"], w=[qbk])
                    else:
                        S.add("dve", lambda e: e.tensor_tensor(out=kf[k2][0:PT, :].rearrange("p (h d) -> p h d", d=DH), in0=sq[0:PT, :].rearrange("p (h d) -> p h d", d=DH),
                                                               in1=gkb[0:PT, :].unsqueeze(1).to_broadcast([PT, H, DH]), op=ALU.mult),
                              r=["sq", "gkb"], w=[f"kf{k2}"])
                        kdst = A["kp"][r0:r0 + PT, :] if kind == "p" else A["ks"][0:PT, :]
                        S.add("sp", lambda e: e.dma_start(out=kdst, in_=kf[k2][0:PT, :]), r=[f"kf{k2}"], chan=f"kf{k2}")
                        S.add("act", lambda e: e.activation(out=kb[0:PT, :], in_=kf[k2][0:PT, :], func=AF.Copy), r=[f"kf{k2}"], w=[kbk])

                def stA():
                    qk_pe(0)
                    for hf in range(2):
                        S.add("pe", proj(psP[2 + hf], 2 * D + hf * 512, 512), r=["hT", "W1in"], w=[f"psP{2 + hf}"])
                    qk_sq(0)
                    for hf in range(2):
                        S.add("act", lambda e, hf=hf: e.activation(out=vf[k2][0:PT, hf * 512:(hf + 1) * 512], in_=psP[2 + hf][0:PT, :], func=AF.Copy),
                              r=[f"psP{2 + hf}"], w=[f"vf{k2}"])
                    qk_sqrt(0)
                    for hf in range(2):
                        S.add("dve", lambda e, hf=hf: e.tensor_copy(out=vb[k2][0:PT, hf * 512:(hf + 1) * 512], in_=vf[k2][0:PT, hf * 512:(hf + 1) * 512]),
                              r=[f"vf{k2}"], w=[f"vb{k2}"])
                    qk_mul(0)
                    vdst = A["vp"][r0:r0 + PT, :] if kind == "p" else A["vs"][0:PT, :]
                    S.add("sp", lambda e: e.dma_start(out=vdst, in_=vf[k2][0:PT, :]), r=[f"vf{k2}"], chan=f"vf{k2}")
                    if kind == "p":
                        tq = t["i"] * 512 + s * 128
                        S.add("sp", lambda e: e.dma_start(out=A["vbp"][t["q"]][tq:tq + 128, :], in_=vb[k2][:, :]), r=[f"vb{k2}"], chan=f"vb{k2}")
                    else:
                        for g in range(NSTR):
                            S.add("sp", lambda e, g=g: e.dma_start(out=A["vbs"][g][PAST:PAST + TS, :], in_=vb[k2][g * TS:(g + 1) * TS, :]),
                                  r=[f"vb{k2}"], chan=f"vb{k2}")

                def stB():
                    qk_pe(1)
                    for hf in range(2):
                        S.add("pe", proj(psP[2 + hf], 3 * D + hf * 512, 512), r=["hT", "W1in"], w=[f"psP{2 + hf}"])
                    S.add("pe", proj(psF, 4 * D, H), r=["hT", "W1in"], w=["psF"])
                    qk_sq(1)
                    S.add("dve", lambda e: e.tensor_tensor(out=lft[0:PT, 0:H], in0=psF[0:PT, 0:H], in1=bfb[0:PT, :], op=ALU.add), r=["psF", "bfb"], w=["lft"])
                    S.add("dve", lambda e: e.tensor_scalar_min(out=lft[0:PT, 2 * H:3 * H], in0=lft[0:PT, 0:H], scalar1=0.0), r=["lft"], w=["lft2"])
                    qk_sqrt(1)
                    S.add("act", lambda e: e.activation(out=lft[0:PT, H:2 * H], in_=lft[0:PT, 0:H], func=AF.Abs), r=["lft"], w=["lft1"])
                    qk_mul(1)
                    for hf in range(2):
                        S.add("act", lambda e, hf=hf: e.activation(out=szb[k2][0:PT, hf * 512:(hf + 1) * 512], in_=psP[2 + hf][0:PT, :], func=AF.Silu),
                              r=[f"psP{2 + hf}"], w=[f"szb{k2}"])
                    S.add("sp", lambda e: e.dma_start(out=A["szs"][r0:r0 + PT, :], in_=szb[k2][0:PT, :]), r=[f"szb{k2}"], chan=f"szb{k2}")
                    S.add("act", lambda e: e.activation(out=lft[0:PT, H:2 * H], in_=lft[0:PT, H:2 * H], func=AF.Exp, scale=-1.0), r=["lft1"], w=["lft1"])
                    S.add("act", lambda e: e.activation(out=lft[0:PT, H:2 * H], in_=lft[0:PT, H:2 * H], func=AF.Ln, bias=1.0), r=["lft1"], w=["lft1"])
                    if kind == "p":
                        blk = t["i"] * 4 + s
                        S.add("dve", lambda e: e.tensor_tensor(out=lf[:, blk, :], in0=lft[:, 2 * H:3 * H], in1=lft[:, H:2 * H], op=ALU.subtract),
                              r=["lft1", "lft2"], w=["lf"])
                        S.add("sp", lambda e: e.dma_start(out=A["lfp"][r0:r0 + 128, :], in_=lf[:, blk, :]), r=["lf"], chan="lf")
                    else:
                        S.add("dve", lambda e: e.tensor_tensor(out=lft[0:PT, 3 * H:4 * H], in0=lft[0:PT, 2 * H:3 * H], in1=lft[0:PT, H:2 * H], op=ALU.subtract),
                              r=["lft1", "lft2"], w=["lft3"])
                        S.add("sp", lambda e: e.dma_start(out=A["lfs"][0:PT, :], in_=lft[0:PT, 3 * H:4 * H]), r=["lft3"], w=["lfs_dram"], chan="lft")

                def stC():
                    transposes_to(qb, PT, qTs, s, qbk, "qTs")
                    transposes_to(kb, PT, kTs, s, kbk, "kTs")
                return stA, stB, stC

            stages = []
            for s in range(ns):
                stages.append(make_sub(s))
            stages[0][0]()
            stages[0][1]()
            for s in range(1, ns):
                stages[s][0]()
                stages[s - 1][2]()
                stages[s][1]()
            stages[ns - 1][2]()
            if kind == "p":
                q, i = t["q"], t["i"]
                S.add("pool", lambda e: e.dma_start(out=A["qtp"][q][:, :, i * 512:(i + 1) * 512].rearrange("h d t -> d h t"), in_=qTs[:, :, :]), r=["qTs"], chan="qTs")
                S.add("pool", lambda e: e.dma_start(out=A["ktp"][q][:, :, i * 512:(i + 1) * 512].rearrange("h d t -> d h t"), in_=kTs[:, :, :]), r=["kTs"], chan="kTs")
                if t["last"]:
                    cumsum_seq([x for x in seq_list(c) if x["kind"] == "p" and x["idx"] == q][0])
            else:
                for g in range(NSTR):
                    S.add("pool", lambda e, g=g: e.dma_start(out=A["qts"][g][:, :, :].rearrange("h d t -> d h t"), in_=qTs[:, :, g * TS:(g + 1) * TS]), r=["qTs"], chan="qTs")
                    S.add("pool", lambda e, g=g: e.dma_start(out=A["kts"][g][:, :, PAST:PAST + TS].rearrange("h d t -> d h t"), in_=kTs[:, :, g * TS:(g + 1) * TS]), r=["kTs"], chan="kTs")
                for g in range(NSTR):
                    sqd = [x for x in seq_list(c) if x["kind"] == "s" and x["idx"] == g][0]
                    nfull = sqd["nfull"]
                    S.add("sp", lambda e, g=g, nfull=nfull: e.dma_start(out=lf[:, 0:nfull, :], in_=A["clf"][g].rearrange("(b p) h -> p b h", p=128)),
                          w=["lf"], chan="lfin")
                    S.add("sp", lambda e, g=g, nfull=nfull: e.dma_start(out=lf[0:TS, nfull, :], in_=A["lfs"][g * TS:(g + 1) * TS, :]),
                          r=["lfs_dram"], w=["lf"], chan="lfin")
                    cumsum_seq(sqd)

        load_x(0)
        for ti_, t_ in enumerate(tiles):
            do_tile(ti_, t_)
        print('l1a adds', S.nadd)
        S.limit = None
        S.run_phase()


def phase_l1b(S, nc, c, A, K):
    T, NSEQ, NSTR, NP, NS, PAST, LS = c.T, c.NSEQ, c.NSTR, c.NP, c.NS, c.PAST, c.LS
    tri_b = K["tri_b"]
    LMAX = max(T, LS)
    NBMAX = max(T // 128, PAST // 128 + 1)
    NQTMAX = max(T // 512, 1)
    with ExitStack() as es:
        sb = lambda name, shape, dt=F32: es.enter_context(nc.sbuf_tensor("l1b_" + name, list(shape), dt))
        pm = lambda name, shape, dt=F32: es.enter_context(nc.psum_tensor("l1b_" + name, list(shape), dt))
        KTa = [sb(f"KTa{i}", [67, LMAX], BF16) for i in range(2)]
        QTa = [sb(f"QTa{i}", [67, T], BF16) for i in range(2)]
        Va = [sb(f"Va{i}", [128, NBMAX, 65], BF16) for i in range(2)]
        bias = [sb(f"bias{i}", [128, NBMAX, NQTMAX]) for i in range(2)]
        Pt = [sb(f"Pt{i}", [128, 512], BF16) for i in range(6)]
        ost = [sb(f"ost{i}", [128, 4, DH], BF16) for i in range(2)]
        rden = sb("rden", [128, 4])
        psS = [pm(f"psS{i}", [128, 512]) for i in range(4)]
        psO = [pm(f"psO{i}", [128, 128]) for i in range(4)]
        for i in range(2):
            S.add("pool", lambda e, i=i: e.memset(KTa[i][64:67, :], 1.0), w=[f"KTa{i}"])
            S.add("pool", lambda e, i=i: e.memset(Va[i][:, :, 64:65], 1.0), w=[f"Va{i}"])
        jobs = [(sq, h) for sq in seq_list(c) for h in range(H)]
        cnt = {"p": 0, "o": 0}

        def load(ji):
            sq, h = jobs[ji]
            sl = ji % 2
            nfull, part, nqt, key = sq["nfull"], sq["part"], sq["nqt"], sq["key"]
            L = nfull * 128 + part
            if sq["kind"] == "p":
                kt, qt, vbh, au, NQ = A["ktp"][sq["idx"]][h], A["qtp"][sq["idx"]][h], A["vbp"][sq["idx"]][:, h * DH:(h + 1) * DH], A["augp"][sq["idx"]][h], T
            else:
                kt, qt, vbh, au, NQ = A["kts"][sq["idx"]][h], A["qts"][sq["idx"]][h], A["vbs"][sq["idx"]][:, h * DH:(h + 1) * DH], A["augs"][sq["idx"]][h], TS
            S.add("sp", lambda e: e.dma_start(out=KTa[sl][0:64, 0:L], in_=kt), w=[f"KTa{sl}"], chan=f"KTa{sl}")
            S.add("sp", lambda e: e.dma_start(out=QTa[sl][0:64, 0:NQ], in_=qt), w=[f"QTa{sl}"], chan=f"QTa{sl}")
            S.add("sp", lambda e: e.dma_start(out=QTa[sl][64:67, 0:NQ], in_=au), w=[f"QTa{sl}x"], chan=f"QTa{sl}x")
            for b0 in range(0, nfull, 4):
                b1 = min(nfull, b0 + 4)
                S.add("sp", lambda e, b0=b0, b1=b1: e.dma_start(out=Va[sl][:, b0:b1, 0:DH], in_=vbh[b0 * 128:b1 * 128, :].rearrange("(b p) d -> p b d", p=128)),
                      w=[f"Va{sl}_{b0 // 4}"], chan=f"Va{sl}_{b0 // 4}")
            if part:
                S.add("sp", lambda e: e.dma_start(out=Va[sl][0:part, nfull, 0:DH], in_=vbh[nfull * 128:nfull * 128 + part, :]),
                      w=[f"Va{sl}_{nfull // 4}"], chan=f"Va{sl}_p")
            nb = nfull + (1 if part else 0)
            Call, Ball = K["C_" + key], K["B_" + key]
            S.add("dve", lambda e: e.tensor_tensor(out=bias[sl][:, 0:nfull, 0:nqt], in0=Ball[:, :, h].unsqueeze(1).to_broadcast([128, nfull, nqt]),
                                                   in1=Call[:, 0:nfull, h].unsqueeze(2).to_broadcast([128, nfull, nqt]), op=ALU.subtract),
                  w=[f"bias{sl}"])
            if part:
                S.add("dve", lambda e: e.tensor_tensor(out=bias[sl][0:part, nfull, 0:nqt], in0=Ball[0:part, :, h], in1=Call[0:part, nfull:nfull + 1, h], op=ALU.subtract),
                      w=[f"bias{sl}"])

        def run(ji):
            sq, h = jobs[ji]
            sl = ji % 2
            nfull, part, nqt = sq["nfull"], sq["part"], sq["nqt"]
            isp = sq["kind"] == "p"
            def qtile(qi):
                if isp:
                    q0, NQt, nsub, wsub = qi * 512, 512, 4, 128
                    blocks = list(range(4 * qi + 4))
                else:
                    q0, NQt, nsub, wsub = 0, TS, 1, TS
                    blocks = list(range(nfull + 1))
                oslot = cnt["o"] % 2
                cnt["o"] += 1
                LA = 3
                descs = []

                def front(b):
                    KB = 128 if b < nfull else part
                    if isp:
                        j = b - 4 * qi
                        diag = j >= 0
                        c0 = 128 * j if diag else 0
                    else:
                        diag = b == nfull
                        c0 = 0
                    c1 = NQt
                    pb = cnt["p"] % 6
                    sb_ = cnt["p"] % 4
                    cnt["p"] += 1
                    S.add("pe", lambda e: e.matmul(psS[sb_][0:KB, c0:c1], lhsT=KTa[sl][0:67, b * 128:b * 128 + KB],
                                                   rhs=QTa[sl][0:67, q0 + c0:q0 + c1], start=True, stop=True),
                          r=[f"KTa{sl}", f"QTa{sl}", f"QTa{sl}x"], w=[f"psS{sb_}"])
                    S.add("act", lambda e: e.activation(out=Pt[pb][0:KB, c0:c1], in_=psS[sb_][0:KB, c0:c1], func=AF.Exp,
                                                        bias=bias[sl][0:KB, b, qi:qi + 1]),
                          r=[f"psS{sb_}", f"bias{sl}"], w=[f"Pt{pb}"], noself=True)
                    if diag:
                        S.add("dve", lambda e: e.tensor_tensor(out=Pt[pb][0:KB, c0:c0 + KB], in0=Pt[pb][0:KB, c0:c0 + KB],
                                                                in1=tri_b[0:KB, 0:KB], op=ALU.mult),
                              r=[f"Pt{pb}", "tri_b"], w=[f"Pt{pb}"])
                    descs.append((b, KB, c0, pb))

                def back(b, KB, c0, pb):
                    def pv(e):
                        r_ = None
                        for s_ in range(nsub):
                            if s_ * wsub < c0:
                                continue
                            lastb = (4 * qi + s_) if isp else nfull
                            r_ = e.matmul(psO[s_][0:wsub, 0:65], lhsT=Pt[pb][0:KB, s_ * wsub:(s_ + 1) * wsub], rhs=Va[sl][0:KB, b, 0:65],
                                          start=(b == 0), stop=(b == lastb))
                        return r_
                    S.add("pe", pv, r=[f"Pt{pb}", f"Va{sl}_{b // 4}", f"Va{sl}"], w=["psO"])

                for idx in range(len(blocks) + LA):
                    if idx < len(blocks):
                        front(blocks[idx])
                    if idx - LA >= 0:
                        back(*descs[idx - LA])
                for s_ in range(nsub):
                    S.add("dve", lambda e, s_=s_: e.reciprocal(out=rden[0:wsub, s_:s_ + 1], in_=psO[s_][0:wsub, 64:65]), r=["psO"], w=["rden"])
                    S.add("dve", lambda e, s_=s_: e.tensor_scalar(out=ost[oslot][0:wsub, s_, :], in0=psO[s_][0:wsub, 0:DH], scalar1=rden[0:wsub, s_:s_ + 1],
                                                                 scalar2=None, op0=ALU.mult),
                          r=["psO", "rden"], w=[f"ost{oslot}"])
                if isp:
                    r0 = sq["idx"] * T + q0
                    S.add("pool", lambda e, r0=r0: e.dma_start(out=A["obuf"][r0:r0 + 512, h * DH:(h + 1) * DH].rearrange("(s p) d -> p s d", p=128),
                                                              in_=ost[oslot][:, :, :]), r=[f"ost{oslot}"], chan=f"ost{oslot}")
                else:
                    r0 = NP + sq["idx"] * TS
                    S.add("pool", lambda e, r0=r0: e.dma_start(out=A["obuf"][r0:r0 + TS, h * DH:(h + 1) * DH], in_=ost[oslot][0:TS, 0, :]),
                          r=[f"ost{oslot}"], chan=f"ost{oslot}")

            for qi_ in range(nqt):
                qtile(qi_)

        load(0)
        for ji in range(len(jobs)):
            if ji + 1 < len(jobs):
                load(ji + 1)
            run(ji)
        S.run_phase()


def phase_l1c(S, nc, c, A, K):
    T, NSEQ, NSTR, NP, NS = c.T, c.NSEQ, c.NSTR, c.NP, c.NS
    ident_b = K["ident_b"]
    with ExitStack() as es:
        sb = lambda name, shape, dt=F32: es.enter_context(nc.sbuf_tensor("l1c_" + name, list(shape), dt))
        pm = lambda name, shape, dt=F32: es.enter_context(nc.psum_tensor("l1c_" + name, list(shape), dt))
        W1out = sb("W1out", [128, 8, D], BF16)
        xt = [sb(f"xt{i}", [128, 4, D]) for i in range(2)]
        stg = [xt[i][:, :, :].rearrange("p s (a b) -> p (s a) b", b=512) for i in range(2)]
        ot = [sb(f"ot{i}", [128, 4, D], BF16) for i in range(2)]
        zt = [sb(f"zt{i}", [128, 4, D], BF16) for i in range(2)]
        fng = sb("fng", [128, D])
        mb = [sb(f"mb{i}", [128, D], BF16) for i in range(2)]
        mT = [sb(f"mT{i}", [128, 8, 128], BF16) for i in range(2)]
        x2 = [sb(f"x2{i}", [128, D]) for i in range(2)]
        junk = sb("junk", [128, D], BF16)
        ssq = [sb(f"ssq{i}", [128, 2]) for i in range(2)]
        yo = [sb(f"yo{i}", [128, D]) for i in range(2)]
        psT = [pm(f"psT{i}", [128, 8, 128], BF16) for i in range(2)]
        psP = [pm(f"psP{i}", [128, 512]) for i in range(4)]
        wout_v = A["w_fox_out"].rearrange("(c p) e -> p c e", p=128)
        for pc in range(2):
            st = stg[pc % 2]
            S.add("sp", lambda e, st=st, pc=pc: e.dma_start(out=st, in_=wout_v[:, :, pc * 512:(pc + 1) * 512]), w=[f"xt{pc % 2}"], chan=f"xt{pc % 2}")
            S.add("act", lambda e, st=st, pc=pc: e.activation(out=W1out[:, :, pc * 512:(pc + 1) * 512], in_=st, func=AF.Copy), r=[f"xt{pc % 2}"], w=["W1out"])
        S.add("sp", lambda e: e.dma_start(out=fng[:], in_=A["fng"].partition_broadcast(128)), w=["fng"], chan="fng")
        tiles = []
        for i in range(NP // 512):
            tiles.append(dict(kind="p", tok0=i * 512, PT=128, ns=4))
        tiles.append(dict(kind="s", tok0=NP, PT=NS, ns=1))

        def load(ti):
            t = tiles[ti]
            sl = ti % 2
            PT, ns = t["PT"], t["ns"]
            rows = slice(t["tok0"], t["tok0"] + PT * ns)
            S.add("sp", lambda e: e.dma_start(out=xt[sl][0:PT, 0:ns, :], in_=A["x1"][rows, :].rearrange("(s p) d -> p s d", p=PT)), w=[f"xt{sl}"], chan=f"xt{sl}")
            S.add("sp", lambda e: e.dma_start(out=ot[sl][0:PT, 0:ns, :], in_=A["obuf"][rows, :].rearrange("(s p) d -> p s d", p=PT)), w=[f"ot{sl}"], chan=f"ot{sl}")
            S.add("sp", lambda e: e.dma_start(out=zt[sl][0:PT, 0:ns, :], in_=A["szs"][rows, :].rearrange("(s p) d -> p s d", p=PT)), w=[f"zt{sl}"], chan=f"zt{sl}")

        subs = [(ti, s) for ti, t in enumerate(tiles) for s in range(t["ns"])]

        def front(k):
            ti, s = subs[k]
            t = tiles[ti]
            sl, PT, b2 = ti % 2, t["PT"], k % 2
            S.add("dve", lambda e: e.tensor_tensor(out=mb[b2][0:PT, :], in0=ot[sl][0:PT, s, :], in1=zt[sl][0:PT, s, :], op=ALU.mult),
                  r=[f"ot{sl}", f"zt{sl}"], w=[f"mb{b2}"])

            def tr(e):
                for cc in range(8):
                    r_ = e.transpose(psT[b2][:, cc, 0:PT], mb[b2][0:PT, cc * 128:(cc + 1) * 128], ident_b[0:PT, 0:PT])
                return r_
            S.add("pe", tr, r=[f"mb{b2}", "ident_b"], w=[f"psT{b2}"])
            S.add("act", lambda e: e.activation(out=mT[b2][:, :, 0:PT], in_=psT[b2][:, :, 0:PT], func=AF.Copy), r=[f"psT{b2}"], w=[f"mT{b2}"])

        def back(k):
            ti, s = subs[k]
            t = tiles[ti]
            sl, PT, b2 = ti % 2, t["PT"], k % 2
            for hf in range(2):
                pp = psP[b2 * 2 + hf]
                pk = f"psP{b2 * 2 + hf}"

                def op(e, pp=pp, hf=hf):
                    for j in range(8):
                        r_ = e.matmul(pp[0:PT, :], lhsT=mT[b2][:, j, 0:PT], rhs=W1out[:, j, hf * 512:(hf + 1) * 512], start=(j == 0), stop=(j == 7))
                    return r_
                S.add("pe", op, r=[f"mT{b2}", "W1out"], w=[pk])
                S.add("dve", lambda e, pp=pp, hf=hf: e.tensor_tensor(out=x2[b2][0:PT, hf * 512:(hf + 1) * 512], in0=pp[0:PT, :],
                                                                   in1=xt[sl][0:PT, s, hf * 512:(hf + 1) * 512], op=ALU.add),
                      r=[pk, f"xt{sl}"], w=[f"x2{b2}"])
            S.add("act", lambda e: e.activation(out=junk[0:PT, :], in_=x2[b2][0:PT, :], func=AF.Square, accum_out=ssq[b2][0:PT, 0:1]),
                  r=[f"x2{b2}"], w=["junk", f"ssq{b2}"])
            S.add("act", lambda e: e.activation(out=ssq[b2][0:PT, 1:2], in_=ssq[b2][0:PT, 0:1], func=AF.Sqrt, scale=1.0 / D, bias=EPS),
                  r=[f"ssq{b2}"], w=[f"ssq{b2}"])
            S.add("dve", lambda e: e.reciprocal(out=ssq[b2][0:PT, 1:2], in_=ssq[b2][0:PT, 1:2]), r=[f"ssq{b2}"], w=[f"ssq{b2}"])
            S.add("dve", lambda e: e.scalar_tensor_tensor(out=yo[b2][0:PT, :], in0=x2[b2][0:PT, :], scalar=ssq[b2][0:PT, 1:2], in1=fng[0:PT, :],
                                                          op0=ALU.mult, op1=ALU.mult), r=[f"x2{b2}", f"ssq{b2}", "fng"], w=[f"yo{b2}"])
            r0 = t["tok0"] + s * PT
            dst = A["yp"][r0:r0 + PT, :] if t["kind"] == "p" else A["ys"][0:PT, :]
            S.add("sp", lambda e: e.dma_start(out=dst, in_=yo[b2][0:PT, :]), r=[f"yo{b2}"], chan=f"yo{b2}")
            if s == t["ns"] - 1 and ti + 2 < len(tiles):
                load(ti + 2)

        load(0)
        if len(tiles) > 1:
            load(1)
        front(0)
        for k in range(len(subs)):
            if k + 1 < len(subs):
                front(k + 1)
            back(k)
        S.run_phase()


_NC_CACHE = {}


def kernel(**inputs):
    cfg = Cfg()
    n_cores = 8
    if "nc" not in _NC_CACHE:
        _NC_CACHE["nc"] = build(cfg)
    nc = _NC_CACHE["nc"]
    maps = make_in_maps(cfg, inputs)
    res = run_bass_kernel_spmd(nc, maps, core_ids=list(range(n_cores)))
    R = res.results
    B, Tn = 16, cfg.T
    cat = lambda k: np.concatenate([np.asarray(r[k]) for r in R], axis=0)
    y_prompt = cat("yp").reshape(B, Tn, D)
    y_sample = cat("ys").reshape(32, TS, D)
    conv_p = cat("convp").reshape(1, B, CS, D)
    conv_s = cat("convs").reshape(1, 32, CS, D)
    k_p = cat("kp").reshape(1, B, Tn, H, DH)
    v_p = cat("vp").reshape(1, B, Tn, H, DH)
    lf_p = cat("lfp").reshape(1, B, Tn, H)
    k_s = cat("ks").reshape(1, 32, TS, H, DH)
    v_s = cat("vs").reshape(1, 32, TS, H, DH)
    lf_s = cat("lfs").reshape(1, 32, TS, H)
    return (y_prompt, y_sample, conv_p, conv_s, k_p, v_p, lf_p, k_s, v_s, lf_s)
```

```python
import numpy as np
from contextlib import ExitStack
import concourse.bass as bass
import concourse.mybir as mybir
from concourse.bass_utils import run_bass_kernel_spmd

F32 = mybir.dt.float32
BF16 = mybir.dt.bfloat16
AF = mybir.ActivationFunctionType
ALU = mybir.AluOpType
AX = mybir.AxisListType
EPS = 1e-6
D = 1024
H = 16
DH = 64
CW = 31
CS = 30
TS = 16


class Cfg:
    def __init__(self, T=4096, NSEQ=2, NSTR=4, PAST=4096):
        self.T, self.NSEQ, self.NSTR, self.PAST = T, NSEQ, NSTR, PAST
        self.NP = NSEQ * T
        self.NS = NSTR * TS
        self.LS = PAST + TS
        self.do_l1 = 2
        self.debug = False


class Ins:
    __slots__ = ("eng", "fn", "dma", "chan", "deps", "need", "val", "sem")


class Sched:
    def __init__(self, nc, es):
        self.nc = nc
        self.es = es
        self.csem = {e: es.enter_context(nc.semaphore("c_" + e)) for e in ("pe", "act", "dve", "pool")}
        self.ccnt = dict.fromkeys(self.csem, 0)
        self.dsem = {}
        self.dcnt = {}
        self.waited = {e: {} for e in ("sp", "pe", "act", "dve", "pool")}
        self.ninst = 0
        self.dhist = []
        self.maxdma = 1 << 30
        self.reset()

    def reset(self):
        self.ins = []
        self.lw = {}
        self.rd = {}

    def add(self, eng, fn, r=(), w=(), chan=None, noself=False):
        self.nadd = getattr(self, 'nadd', 0) + 1
        if getattr(self, 'limit', None) is not None and self.nadd > self.limit:
            return None
        i = Ins()
        if chan is not None:
            eng = "sp"
        i.eng, i.fn, i.dma, i.chan = eng, fn, chan is not None, chan
        i.need = i.dma
        deps = {}
        for k in r:
            d = self.lw.get(k)
            if d is not None:
                deps[id(d)] = d
        for k in w:
            d = self.lw.get(k)
            if d is not None:
                deps[id(d)] = d
            for d in self.rd.get(k, ()):
                deps[id(d)] = d
        deps.pop(id(i), None)
        i.deps = [d for d in deps.values() if not (d.eng == "pe" and eng == "pe" and not d.dma and chan is None)]
        if noself:
            i.deps = [d for d in i.deps if d.eng != eng or d.dma]
        for d in i.deps:
            d.need = True
        for k in r:
            self.rd.setdefault(k, []).append(i)
        for k in w:
            self.lw[k] = i
            self.rd[k] = []
        self.ins.append(i)
        return i

    def run_phase(self):
        nc = self.nc
        last = {}
        for i in self.ins:
            if not i.dma:
                last[i.eng] = i
        for i in last.values():
            i.need = True
        for i in self.ins:
            if i.dma:
                if i.chan not in self.dsem:
                    self.dsem[i.chan] = self.es.enter_context(nc.semaphore("d_" + i.chan))
                    self.dcnt[i.chan] = 0
                self.dcnt[i.chan] += 16
                i.sem, i.val = self.dsem[i.chan], self.dcnt[i.chan]
            elif i.need:
                self.ccnt[i.eng] += 1
                i.sem, i.val = self.csem[i.eng], self.ccnt[i.eng]
        self.ninst += len(self.ins)
        allsems = [(self.csem[e], self.ccnt[e]) for e in self.csem] + [(self.dsem[c], self.dcnt[c]) for c in self.dsem]
        ins = self.ins
        with nc.Block() as blk:
            for ename, deco in (("sp", blk.sync), ("pe", blk.tensor), ("act", blk.scalar),
                                ("dve", blk.vector), ("pool", blk.gpsimd)):
                def body(eng, ename=ename):
                    wd = self.waited[ename]
                    for i in ins:
                        if i.eng != ename:
                            continue
                        req = {}
                        for d in i.deps:
                            k = id(d.sem)
                            if k not in req or req[k][1] < d.val:
                                req[k] = (d.sem, d.val)
                        for k, (sem, val) in req.items():
                            if wd.get(k, 0) < val:
                                eng.wait_ge(sem, val)
                                wd[k] = val
                        if i.dma:
                            hist = self.dhist
                            if len(hist) >= self.maxdma:
                                p = hist[-self.maxdma]
                                k = id(p.sem)
                                if wd.get(k, 0) < p.val:
                                    eng.wait_ge(p.sem, p.val)
                                    wd[k] = p.val
                            hist.append(i)
                        r = i.fn(eng)
                        if i.need:
                            r.then_inc(i.sem, 16 if i.dma else 1)
                    for sem, val in allsems:
                        k = id(sem)
                        if val > 0 and wd.get(k, 0) < val:
                            eng.wait_ge(sem, val)
                            wd[k] = val
                deco(body)
        self.reset()


def build(cfg):
    c = cfg
    nc = bass.Bass("TRN2", target_bir_lowering=False)
    A = {}

    def din(name, shape, dt=F32):
        A[name] = nc.dram_tensor(name, list(shape), dt, kind="ExternalInput").ap()

    def dout(name, shape, dt=F32):
        A[name] = nc.dram_tensor(name, list(shape), dt, kind="ExternalOutput").ap()

    def dscr(name, shape, dt=F32):
        A[name] = nc.dram_tensor(name, list(shape), dt, kind="Internal").ap()

    T, NSEQ, NSTR, PAST, NP, NS, LS = c.T, c.NSEQ, c.NSTR, c.PAST, c.NP, c.NS, c.LS
    din("xp", [NP, D]); din("xs", [NS, D]); din("sconv", [NSTR, CS, D])
    din("ck", [NSTR, PAST, D]); din("cv", [NSTR, PAST, D]); din("clf", [NSTR, PAST, H])
    din("norm_g", [2, D]); din("fng", [D]); din("w_conv_in", [D, 3 * D]); din("w_dw", [CW, D])
    din("b_dw", [D]); din("lng", [D]); din("lnb", [D]); din("w_conv_out", [D, D])
    din("w_fox_in", [D, 4 * D + H]); din("b_f", [H]); din("qg", [DH]); din("kg", [DH]); din("w_fox_out", [D, D])
    din("ident", [128, 128]); din("tri", [128, 128])
    dout("yp", [NP, D]); dout("ys", [NS, D]); dout("convp", [NSEQ, CS, D]); dout("convs", [NSTR, CS, D])
    dout("kp", [NP, D]); dout("vp", [NP, D]); dout("lfp", [NP, H])
    dout("ks", [NS, D]); dout("vs", [NS, D]); dout("lfs", [NS, H])
    (dout if c.debug else dscr)("x1", [NP + NS, D])
    dscr("szs", [NP + NS, D], BF16)
    dscr("obuf", [NP + NS, D], BF16)
    dscr("qtp", [NSEQ, H, DH, T], BF16); dscr("ktp", [NSEQ, H, DH, T], BF16); dscr("vbp", [NSEQ, T, D], BF16)
    dscr("augp", [NSEQ, H, 3, T], BF16)
    dscr("qts", [NSTR, H, DH, TS], BF16); dscr("kts", [NSTR, H, DH, LS], BF16); dscr("vbs", [NSTR, LS, D], BF16)
    dscr("augs", [NSTR, H, 3, TS], BF16)

    with ExitStack() as ges:
        S = Sched(nc, ges)
        gsb = lambda name, shape, dt=F32: ges.enter_context(nc.sbuf_tensor(name, list(shape), dt))
        K = {}
        K["ident_f"] = gsb("ident_f", [128, 128]); K["ident_b"] = gsb("ident_b", [128, 128], BF16)
        K["tri_f"] = gsb("tri_f", [128, 128]); K["tri_b"] = gsb("tri_b", [128, 128], BF16)
        K["ones_f"] = gsb("ones_f", [128, 128]); K["ones_b"] = gsb("ones_b", [128, 128], BF16)
        S.add("sp", lambda e: e.dma_start(out=K["ident_f"][:], in_=A["ident"]), w=["ident_f"], chan="ident_f")
        S.add("sp", lambda e: e.dma_start(out=K["tri_f"][:], in_=A["tri"]), w=["tri_f"], chan="tri_f")
        S.add("dve", lambda e: e.tensor_copy(out=K["ident_b"][:], in_=K["ident_f"][:]), r=["ident_f"], w=["ident_b"])
        S.add("dve", lambda e: e.tensor_copy(out=K["tri_b"][:], in_=K["tri_f"][:]), r=["tri_f"], w=["tri_b"])
        S.add("dve", lambda e: e.memset(K["ones_f"][:], 1.0), w=["ones_f"])
        S.add("dve", lambda e: e.memset(K["ones_b"][:], 1.0), w=["ones_b"])
        phase_l0(S, nc, c, A, K)
        alloc_persist(nc, ges, c, K)
        if c.do_l1:
            phase_l1a(S, nc, c, A, K)
            if c.do_l1 > 1:
                phase_l1b(S, nc, c, A, K)
                phase_l1c(S, nc, c, A, K)
        print("bass instructions (logical):", S.ninst)
    return nc


def phase_l0(S, nc, c, A, K):
    T, NSEQ, NSTR, NP, NS = c.T, c.NSEQ, c.NSTR, c.NP, c.NS
    ident_f, ident_b, ones_b = K["ident_f"], K["ident_b"], K["ones_b"]
    with ExitStack() as es:
        sb = lambda name, shape, dt=F32: es.enter_context(nc.sbuf_tensor("l0_" + name, list(shape), dt))
        pm = lambda name, shape, dt=F32: es.enter_context(nc.psum_tensor("l0_" + name, list(shape), dt))
        W0in = sb("W0in", [128, 8, 3 * D], BF16); W0out = sb("W0out", [128, 8, D], BF16)
        vrow = sb("vrow", [24, 128]); vcol = sb("vcol", [128, 24])
        gcol = sb("gcol", [128, 8])
        wdwc = sb("wdwc", [128, 8, 32])
        dg = [sb(f"dg{i}", [128, CW, 128], BF16) for i in range(2)]
        xt = [sb(f"xt{i}", [128, 4, D]) for i in range(2)]
        stg = [xt[i][:, :, :].rearrange("p s (a b) -> p (s a) b", b=512) for i in range(2)]
        hb = sb("hb", [128, 4, D], BF16); hT = sb("hT", [128, 8, 512], BF16)
        junk = sb("junk", [128, D], BF16)
        ssq = sb("ssq", [128, 4]); rstd = sb("rstd", [128, 4])
        vext = [sb(f"vext{i}", [128, 8, 512 + CS], BF16) for i in range(2)]
        sig = [sb(f"sig{i}", [128, 512]) for i in range(2)]
        szT = sb("szT", [128, 8, 512], BF16)
        ybf = sb("ybf", [128, 8, 512], BF16); ysq = [sb(f"ysq{i}", [128, 512], BF16) for i in range(2)]
        mean = sb("mean", [128, 512]); rs = sb("rs", [128, 512]); tmp = [sb(f"tmp{i}", [128, 512]) for i in range(2)]
        mT = sb("mT", [128, 8, 512], BF16)
        xo = [sb(f"xo{i}", [128, D]) for i in range(2)]
        vt = sb("vt", [128, 8, 64]); cst = sb("cst", [64, D]); hist = cst; wdwr = cst
        psT = pm("psT", [128, 8, 128], BF16)
        psA = pm("psA", [128, 512]); psG = pm("psG", [128, 512]); psZ = pm("psZ", [128, 512])
        psY = [pm(f"psY{i}", [128, 512]) for i in range(2)]
        psS = [pm(f"psS{i}", [128, 512]) for i in range(2)]

        for i, nm in enumerate(("b_dw", "lng", "lnb")):
            S.add("sp", lambda e, i=i, nm=nm: e.dma_start(out=vrow[i * 8:(i + 1) * 8, :], in_=A[nm].rearrange("(j p) -> j p", p=128)),
                  w=["vrow"], chan="vrow")
        S.add("pe", lambda e: e.transpose(psS[0][:, 0:24], vrow[:], ident_f[0:24, 0:24]), r=["vrow", "ident_f"], w=["psS0"])
        S.add("dve", lambda e: e.tensor_copy(out=vcol[:], in_=psS[0][:, 0:24]), r=["psS0"], w=["vcol"])
        bdw, lng, lnb = vcol[:, 0:8], vcol[:, 8:16], vcol[:, 16:24]
        S.add("sp", lambda e: e.dma_start(out=vrow[0:8, :], in_=A["norm_g"][0].rearrange("(j p) -> j p", p=128)),
              r=["vrow"], w=["vrow"], chan="vrow")
        S.add("pe", lambda e: e.transpose(psS[1][:, 0:8], vrow[0:8, :], ident_f[0:8, 0:8]), r=["vrow", "ident_f"], w=["psS1"])
        S.add("dve", lambda e: e.tensor_copy(out=gcol[:], in_=psS[1][:, 0:8]), r=["psS1"], w=["gcol"])
        S.add("sp", lambda e: e.dma_start(out=wdwr[0:CW, :], in_=A["w_dw"]), w=["cst"], chan="cst")

        def trw(e):
            for j in range(8):
                r_ = e.transpose(psS[0][:, j * 32:j * 32 + CW], wdwr[0:CW, j * 128:(j + 1) * 128], ident_f[0:CW, 0:CW])
            return r_
        S.add("pe", trw, r=["cst", "ident_f", "vcol"], w=["psS0"])
        S.add("dve", lambda e: e.tensor_copy(out=wdwc[:, :, 0:CW], in_=psS[0][:, 0:256].rearrange("p (j k) -> p j k", k=32)[:, :, 0:CW]),
              r=["psS0"], w=["wdwc"])
        win_v = A["w_conv_in"].rearrange("(c p) e -> p c e", p=128)
        for pc in range(6):
            st = stg[pc % 2]
            S.add("sp", lambda e, st=st, pc=pc: e.dma_start(out=st, in_=win_v[:, :, pc * 512:(pc + 1) * 512]),
                  w=[f"xt{pc % 2}"], chan=f"xt{pc % 2}")
            S.add("dve", lambda e, st=st, pc=pc: e.tensor_tensor(out=W0in[:, :, pc * 512:(pc + 1) * 512], in0=st,
                                                               in1=gcol[:].unsqueeze(2).to_broadcast([128, 8, 512]), op=ALU.mult),
                  r=[f"xt{pc % 2}", "gcol"], w=["W0in"])
        wout_v = A["w_conv_out"].rearrange("(c p) e -> p c e", p=128)
        for pc in range(2):
            st = stg[pc % 2]
            S.add("sp", lambda e, st=st, pc=pc: e.dma_start(out=st, in_=wout_v[:, :, pc * 512:(pc + 1) * 512]),
                  w=[f"xt{pc % 2}"], chan=f"xt{pc % 2}")
            S.add("act", lambda e, st=st, pc=pc: e.activation(out=W0out[:, :, pc * 512:(pc + 1) * 512], in_=st, func=AF.Copy),
                  r=[f"xt{pc % 2}"], w=["W0out"])

        tiles = []
        for q in range(NSEQ):
            for i in range(T // 512):
                tiles.append(dict(kind="p", q=q, i=i, tok0=q * T + i * 512, PT=128, ns=4, G=1, n=512,
                                  first=(i == 0), last=(i == T // 512 - 1)))
        tiles.append(dict(kind="s", tok0=NP, PT=NS, ns=1, G=NSTR, n=TS, first=True, last=True))
        vkeys = lambda sl: [f"vext{sl}_{j}" for j in range(8)]

        def load_x(ti):
            t = tiles[ti]
            sl = ti % 2
            PT, ns = t["PT"], t["ns"]
            if t["kind"] == "p":
                src = A["xp"][t["tok0"]:t["tok0"] + 512, :].rearrange("(s p) d -> p s d", p=128)
            else:
                src = A["xs"][:, :].rearrange("(s p) d -> p s d", p=PT)
            S.add("sp", lambda e: e.dma_start(out=xt[sl][0:PT, 0:ns, :], in_=src), w=[f"xt{sl}"], chan=f"xt{sl}")

        def do_tile(ti, t):
            sl = ti % 2
            X = xt[sl]
            PT, ns, G, n = t["PT"], t["ns"], t["G"], t["n"]
            NT = PT * ns
            seg = CS + n
            if ti + 1 < len(tiles):
                load_x(ti + 1)
            for s in range(ns):
                S.add("act", lambda e, s=s: e.activation(out=junk[0:PT, :], in_=X[0:PT, s, :], func=AF.Square,
                                                         accum_out=ssq[0:PT, s:s + 1]),
                      r=[f"xt{sl}"], w=["junk", "ssq"])
            S.add("act", lambda e: e.activation(out=rstd[0:PT, 0:ns], in_=ssq[0:PT, 0:ns], func=AF.Sqrt, scale=1.0 / D, bias=EPS),
                  r=["ssq"], w=["rstd"])
            S.add("dve", lambda e: e.reciprocal(out=rstd[0:PT, 0:ns], in_=rstd[0:PT, 0:ns]), r=["rstd"], w=["rstd"])
            for s in range(ns):
                S.add("act", lambda e, s=s: e.activation(out=hb[0:PT, s, :], in_=X[0:PT, s, :], func=AF.Copy, scale=rstd[0:PT, s:s + 1]),
                      r=[f"xt{sl}", "rstd"], w=[f"hb{s}"])

                def tr(e, s=s):
                    for cc in range(8):
                        r_ = e.transpose(psT[:, cc, 0:PT], hb[0:PT, s, cc * 128:(cc + 1) * 128], ident_b[0:PT, 0:PT])
                    return r_
                S.add("pe", tr, r=[f"hb{s}", "ident_b"], w=["psT"])
                S.add("dve", lambda e, s=s: e.tensor_copy(out=hT[:, :, s * PT:(s + 1) * PT], in_=psT[:, :, 0:PT]), r=["psT"], w=["hT"])
            if t["kind"] == "p":
                if t["first"]:
                    S.add("pool", lambda e: e.memset(vext[sl][:, :, 0:CS], 0.0), w=vkeys(sl))
                else:
                    S.add("pool", lambda e: e.tensor_copy(out=vext[sl][:, :, 0:CS], in_=vext[1 - sl][:, :, 512:512 + CS]),
                          r=vkeys(1 - sl), w=vkeys(sl))
            else:
                for g in range(G):
                    S.add("sp", lambda e, g=g: e.dma_start(out=hist[0:CS, :], in_=A["sconv"][g]), w=["cst"], chan="cst")

                    def trh(e):
                        for j in range(8):
                            r_ = e.transpose(psS[0][:, j * 32:j * 32 + CS], hist[0:CS, j * 128:(j + 1) * 128], ident_f[0:CS, 0:CS])
                        return r_
                    S.add("pe", trh, r=["cst", "ident_f"], w=["psS0"])
                    S.add("dve", lambda e, g=g: e.tensor_copy(out=vext[sl][:, :, g * seg:g * seg + CS],
                                                              in_=psS[0][:, 0:256].rearrange("p (j k) -> p j k", k=32)[:, :, 0:CS]),
                          r=["psS0"], w=vkeys(sl))
            for j in range(8):
                for ps, off, key in ((psG, D, "psG"), (psA, 0, "psA"), (psZ, 2 * D, "psZ")):
                    def mm(e, ps=ps, col=off + j * 128):
                        for cc in range(8):
                            r_ = e.matmul(ps[:, 0:NT], lhsT=W0in[:, cc, col:col + 128], rhs=hT[:, cc, 0:NT], start=(cc == 0), stop=(cc == 7))
                        return r_
                    S.add("pe", mm, r=["hT", "W0in"], w=[key])
                sg = sig[j % 2]
                S.add("act", lambda e, sg=sg: e.activation(out=sg[:, 0:NT], in_=psG[:, 0:NT], func=AF.Sigmoid), r=["psG"], w=[f"sig{j % 2}"])
                vdst = vext[sl][:, j, 0:G * seg].rearrange("p (g w) -> p g w", w=seg)[:, :, CS:seg]
                S.add("dve", lambda e, sg=sg, vdst=vdst: e.tensor_tensor(out=vdst, in0=psA[:, 0:NT].rearrange("p (g w) -> p g w", w=n),
                                                                         in1=sg[:, 0:NT].rearrange("p (g w) -> p g w", w=n), op=ALU.mult),
                      r=["psA", f"sig{j % 2}"], w=[f"vext{sl}_{j}"])
                if t["last"]:
                    lo, cnt = (NT - CS, CS) if t["kind"] == "p" else (0, NT)
                    S.add("dve", lambda e, sg=sg, j=j, lo=lo, cnt=cnt: e.tensor_tensor(out=vt[:, j, 0:cnt], in0=psA[:, lo:lo + cnt],
                                                                                     in1=sg[:, lo:lo + cnt], op=ALU.mult),
                          r=["psA", f"sig{j % 2}"], w=["vt"])
                S.add("act", lambda e, j=j: e.activation(out=szT[:, j, 0:NT], in_=psZ[:, 0:NT], func=AF.Silu), r=["psZ"], w=[f"sz{j}"])
            if t["last"]:
                cnt = CS if t["kind"] == "p" else NT

                def trv(e, cnt=cnt):
                    for j in range(8):
                        r_ = e.transpose(psS[j // 4][0:cnt, (j % 4) * 128:(j % 4 + 1) * 128], vt[:, j, 0:cnt], ident_f[:])
                    return r_
                S.add("pe", trv, r=["vt", "ident_f"], w=["psS0", "psS1"])
                for hf in range(2):
                    S.add("act", lambda e, hf=hf, cnt=cnt: e.activation(out=cst[0:cnt, hf * 512:(hf + 1) * 512], in_=psS[hf][0:cnt, :], func=AF.Copy),
                          r=[f"psS{hf}"], w=["cst"])
                if t["kind"] == "p":
                    S.add("pool", lambda e, q=t["q"]: e.dma_start(out=A["convp"][q], in_=cst[0:CS, :]), r=["cst"], chan="cst")
                else:
                    for g in range(G):
                        S.add("pool", lambda e, g=g: e.dma_start(out=A["convs"][g, CS - TS:CS, :], in_=cst[g * TS:(g + 1) * TS, :]), r=["cst"], chan="cst")
                    S.add("pool", lambda e: e.dma_start(out=A["convs"][:, 0:CS - TS, :], in_=A["sconv"][:, TS:CS, :]), chan="d2d")
            def gen_diag(j):
                dj = dg[j % 2]
                S.add("dve", lambda e: e.tensor_tensor(out=dj[:], in0=ident_b[:].unsqueeze(1).to_broadcast([128, CW, 128]),
                                                       in1=wdwc[:, j, 0:CW].unsqueeze(2).to_broadcast([128, CW, 128]), op=ALU.mult),
                      r=["wdwc", "ident_b"], w=[f"dg{j % 2}"])
            gen_diag(0)
            for j in range(8):
                dj = dg[j % 2]

                def cv(e, dj=dj, j=j):
                    for g in range(G):
                        for k in range(CW):
                            r_ = e.matmul(psY[j % 2][:, g * n:(g + 1) * n], lhsT=dj[:, k, :], rhs=vext[sl][:, j, g * seg + k:g * seg + k + n],
                                          start=(k == 0), stop=(k == CW - 1))
                    return r_
                S.add("pe", cv, r=[f"dg{j % 2}", f"vext{sl}_{j}"], w=[f"psY{j % 2}"])
                if j + 1 < 8:
                    gen_diag(j + 1)
                S.add("act", lambda e, j=j: e.activation(out=ybf[:, j, 0:NT], in_=psY[j % 2][:, 0:NT], func=AF.Identity, bias=bdw[:, j:j + 1]),
                      r=[f"psY{j % 2}", "vcol"], w=[f"ybf{j}"])
                S.add("dve", lambda e, j=j: e.tensor_tensor(out=ysq[j % 2][:, 0:NT], in0=ybf[:, j, 0:NT], in1=ybf[:, j, 0:NT], op=ALU.mult),
                      r=[f"ybf{j}"], w=[f"ysq{j % 2}"])
                S.add("pe", lambda e, j=j: e.matmul(psS[0][:, 0:NT], lhsT=ones_b[:], rhs=ybf[:, j, 0:NT], start=(j == 0), stop=(j == 7)),
                      r=[f"ybf{j}", "ones_b"], w=["psS0"])
                S.add("pe", lambda e, j=j: e.matmul(psS[1][:, 0:NT], lhsT=ones_b[:], rhs=ysq[j % 2][:, 0:NT], start=(j == 0), stop=(j == 7)),
                      r=[f"ysq{j % 2}", "ones_b"], w=["psS1"])
            S.add("act", lambda e: e.activation(out=mean[:, 0:NT], in_=psS[0][:, 0:NT], func=AF.Copy, scale=1.0 / D), r=["psS0"], w=["mean"])
            S.add("dve", lambda e: e.tensor_tensor(out=tmp[0][:, 0:NT], in0=mean[:, 0:NT], in1=mean[:, 0:NT], op=ALU.mult), r=["mean"], w=["tmp0"])
            S.add("dve", lambda e: e.scalar_tensor_tensor(out=rs[:, 0:NT], in0=psS[1][:, 0:NT], scalar=1.0 / D, in1=tmp[0][:, 0:NT],
                                                          op0=ALU.mult, op1=ALU.subtract), r=["psS1", "tmp0"], w=["rs"])
            S.add("act", lambda e: e.activation(out=rs[:, 0:NT], in_=rs[:, 0:NT], func=AF.Sqrt, bias=EPS), r=["rs"], w=["rs"])
            S.add("dve", lambda e: e.reciprocal(out=rs[:, 0:NT], in_=rs[:, 0:NT]), r=["rs"], w=["rs"])
            for j in range(8):
                tj = tmp[j % 2]
                S.add("dve", lambda e, j=j, tj=tj: e.tensor_tensor(out=tj[:, 0:NT], in0=ybf[:, j, 0:NT], in1=mean[:, 0:NT], op=ALU.subtract),
                      r=[f"ybf{j}", "mean"], w=[f"tmp{j % 2}"])
                S.add("pool", lambda e, tj=tj: e.tensor_tensor(out=tj[:, 0:NT], in0=tj[:, 0:NT], in1=rs[:, 0:NT], op=ALU.mult),
                      r=[f"tmp{j % 2}", "rs"], w=[f"tmp{j % 2}"])
                S.add("act", lambda e, j=j, tj=tj: e.activation(out=tj[:, 0:NT], in_=tj[:, 0:NT], func=AF.Silu, scale=lng[:, j:j + 1], bias=lnb[:, j:j + 1]),
                      r=[f"tmp{j % 2}", "vcol"], w=[f"tmp{j % 2}"])
                S.add("dve", lambda e, j=j, tj=tj: e.tensor_tensor(out=mT[:, j, 0:NT], in0=tj[:, 0:NT], in1=szT[:, j, 0:NT], op=ALU.mult),
                      r=[f"tmp{j % 2}", f"sz{j}"], w=["mT"])
            for s in range(ns):
                for hf in range(2):
                    def op(e, s=s, hf=hf):
                        for j in range(8):
                            r_ = e.matmul(psS[hf][0:PT, :], lhsT=mT[:, j, s * PT:(s + 1) * PT], rhs=W0out[:, j, hf * 512:(hf + 1) * 512],
                                          start=(j == 0), stop=(j == 7))
                        return r_
                    S.add("pe", op, r=["mT", "W0out"], w=[f"psS{hf}"])
                    S.add("dve", lambda e, s=s, hf=hf: e.tensor_tensor(out=xo[s % 2][0:PT, hf * 512:(hf + 1) * 512], in0=psS[hf][0:PT, :],
                                                                     in1=X[0:PT, s, hf * 512:(hf + 1) * 512], op=ALU.add),
                          r=[f"psS{hf}", f"xt{sl}"], w=[f"xo{s % 2}"])
                r0 = t["tok0"] + s * PT
                S.add("pool", lambda e, s=s, r0=r0: e.dma_start(out=A["x1"][r0:r0 + PT, :], in_=xo[s % 2][0:PT, :]), r=[f"xo{s % 2}"], chan=f"xo{s % 2}")
        load_x(0)
        for ti_, t_ in enumerate(tiles):
            do_tile(ti_, t_)
        S.run_phase()


def make_in_maps(cfg, inputs):
    c = cfg
    f = lambda a: np.ascontiguousarray(np.asarray(a, dtype=np.float32))
    ident = np.eye(128, dtype=np.float32)
    tri = np.triu(np.ones((128, 128), dtype=np.float32))
    n_cores = inputs["x_prompt"].shape[0] // c.NSEQ
    maps = []
    for i in range(n_cores):
        sq = slice(i * c.NSEQ, (i + 1) * c.NSEQ)
        st = slice(i * c.NSTR, (i + 1) * c.NSTR)
        m = {
            "xp": f(inputs["x_prompt"][sq]).reshape(c.NP, D),
            "xs": f(inputs["x_sample"][st]).reshape(c.NS, D),
            "sconv": f(inputs["state_conv"][0, st]),
            "ck": f(inputs["cache_k"][0, st]).reshape(c.NSTR, c.PAST, D),
            "cv": f(inputs["cache_v"][0, st]).reshape(c.NSTR, c.PAST, D),
            "clf": f(inputs["cache_logf"][0, st]),
            "norm_g": f(inputs["norm_g"]), "fng": f(inputs["final_norm_g"]),
            "w_conv_in": f(inputs["w_conv_in"][0]), "w_dw": f(inputs["w_dw"][0]), "b_dw": f(inputs["b_dw"][0]),
            "lng": f(inputs["conv_ln_g"][0]), "lnb": f(inputs["conv_ln_b"][0]), "w_conv_out": f(inputs["w_conv_out"][0]),
            "w_fox_in": f(inputs["w_fox_in"][0]), "b_f": f(inputs["b_forget"][0]), "qg": f(inputs["q_norm_g"][0]),
            "kg": f(inputs["k_norm_g"][0]), "w_fox_out": f(inputs["w_fox_out"][0]),
            "ident": ident, "tri": tri,
        }
        maps.append(m)
    return maps


def seq_list(c):
    L = []
    for q in range(c.NSEQ):
        L.append(dict(kind="p", idx=q, nfull=c.T // 128, part=0, nqt=c.T // 512, key=f"p{q}"))
    for g in range(c.NSTR):
        L.append(dict(kind="s", idx=g, nfull=c.PAST // 128, part=TS, nqt=1, key=f"s{g}"))
    return L


def alloc_persist(nc, ges, c, K):
    gsb = lambda name, shape, dt=F32: ges.enter_context(nc.sbuf_tensor("l0_" + name, list(shape), dt))
    for sq in seq_list(c):
        nb = sq["nfull"] + (1 if sq["part"] else 0)
        K["C_" + sq["key"]] = gsb("C_" + sq["key"], [128, nb, H])
        K["B_" + sq["key"]] = gsb("B_" + sq["key"], [128, sq["nqt"], H])


def phase_l1a(S, nc, c, A, K):
    import os
    S.nadd = 0
    S.limit = int(os.environ['L1A_LIMIT']) if 'L1A_LIMIT' in os.environ else None
    T, NSEQ, NSTR, NP, NS, PAST = c.T, c.NSEQ, c.NSTR, c.NP, c.NS, c.PAST
    ident_f, ident_b, ones_f, tri_f = K["ident_f"], K["ident_b"], K["ones_f"], K["tri_f"]
    NBMAX = max(T // 128, PAST // 128 + 1)
    with ExitStack() as es:
        sb = lambda name, shape, dt=F32: es.enter_context(nc.sbuf_tensor("l1a_" + name, list(shape), dt))
        pm = lambda name, shape, dt=F32: es.enter_context(nc.psum_tensor("l1a_" + name, list(shape), dt))
        NW = 4 * D + H
        W1in = sb("W1in", [128, 8, NW], BF16)
        xt = [sb(f"xt{i}", [128, 4, D]) for i in range(2)]
        stg = [xt[i][:, :, :].rearrange("p s (a b) -> p (s a) b", b=512) for i in range(2)]
        vrow = sb("vrow", [8, 128]); gcol = sb("gcol", [128, 8])
        bfb = sb("bfb", [128, H]); gqb = sb("gqb", [128, DH]); gkb = sb("gkb", [128, DH])
        hbs = [sb(f"hb{i}", [128, D], BF16) for i in range(2)]; hT = sb("hT", [128, 8, 512], BF16)
        ssq = sb("ssq", [128, 4]); rstd = sb("rstd", [128, 4])
        sq = sb("sq", [128, D]); ssh = sb("ssh", [128, 2 * H])
        kf = [sb(f"kf{i}", [128, D]) for i in range(2)]
        vf = [sb(f"vf{i}", [128, D]) for i in range(2)]
        qbs = [sb(f"qb{i}", [128, D], BF16) for i in range(2)]; kbs = [sb(f"kb{i}", [128, D], BF16) for i in range(2)]
        kb = kbs[0]
        vb = [sb(f"vb{i}", [128, D], BF16) for i in range(2)]
        szb = [sb(f"szb{i}", [128, D], BF16) for i in range(2)]
        qTs = sb("qTs", [64, H, 512], BF16); kTs = sb("kTs", [64, H, 512], BF16)
        lft = sb("lft", [128, 4 * H])
        lf = sb("lf", [128, NBMAX, H]); psa = sb("psa", [128, NBMAX + 1, H])
        clt = sb("clt", [128, NBMAX, H]); clT = sb("clT", [16, 512]); r1 = sq[0:16, 0:512]
        aug = sb("aug", [16, 3, 512], BF16)
        psT = pm("psT", [128, 8, 128], BF16)
        psP = [pm(f"psP{i}", [128, 512]) for i in range(4)]
        psQ = [pm(f"psQ{i}", [64, 8, 128], BF16) for i in range(2)]
        psF = pm("psF", [128, 512])

        S.add("sp", lambda e: e.dma_start(out=vrow[0:8, :], in_=A["norm_g"][1].rearrange("(j p) -> j p", p=128)), w=["vrow"], chan="vrow")
        S.add("pe", lambda e: e.transpose(psF[:, 0:8], vrow[0:8, :], ident_f[0:8, 0:8]), r=["vrow"], w=["psF"])
        S.add("dve", lambda e: e.tensor_copy(out=gcol[:], in_=psF[:, 0:8]), r=["psF"], w=["gcol"])
        S.add("sp", lambda e: e.dma_start(out=bfb[:], in_=A["b_f"].partition_broadcast(128)), w=["bfb"], chan="bfb")
        S.add("sp", lambda e: e.dma_start(out=gqb[:], in_=A["qg"].partition_broadcast(128)), w=["gqb"], chan="gqb")
        S.add("sp", lambda e: e.dma_start(out=gkb[:], in_=A["kg"].partition_broadcast(128)), w=["gkb"], chan="gkb")
        S.add("dve", lambda e: e.tensor_scalar_mul(out=gqb[:], in0=gqb[:], scalar1=0.125), r=["gqb"], w=["gqb"])
        win_v = A["w_fox_in"].rearrange("(c p) e -> p c e", p=128)
        for pc in range(9):
            st = stg[pc % 2]
            wd = 512 if pc < 8 else H
            S.add("sp", lambda e, st=st, pc=pc, wd=wd: e.dma_start(out=st[:, :, 0:wd], in_=win_v[:, :, pc * 512:pc * 512 + wd]),
                  w=[f"xt{pc % 2}"], chan=f"xt{pc % 2}")
            S.add("dve", lambda e, st=st, pc=pc, wd=wd: e.tensor_tensor(out=W1in[:, :, pc * 512:pc * 512 + wd], in0=st[:, :, 0:wd],
                                                                      in1=gcol[:].unsqueeze(2).to_broadcast([128, 8, wd]), op=ALU.mult),
                  r=[f"xt{pc % 2}", "gcol"], w=["W1in"])

        tiles = []
        for q in range(NSEQ):
            for i in range(T // 512):
                tiles.append(dict(kind="p", q=q, i=i, tok0=q * T + i * 512, PT=128, ns=4, last=(i == T // 512 - 1)))
        tiles.append(dict(kind="s", tok0=NP, PT=NS, ns=1, last=True))
        for g in range(NSTR):
            for i in range(PAST // 512):
                tiles.append(dict(kind="c", g=g, i=i, PT=128, ns=4, last=False))
        cnt = {"st": 0}

        def load_x(ti):
            t = tiles[ti]
            sl = ti % 2
            PT, ns = t["PT"], t["ns"]
            if t["kind"] == "c":
                src = A["ck"][t["g"], t["i"] * 512:(t["i"] + 1) * 512, :].rearrange("(s p) d -> p s d", p=128)
            else:
                src = A["x1"][t["tok0"]:t["tok0"] + PT * ns, :].rearrange("(s p) d -> p s d", p=PT)
            S.add("sp", lambda e: e.dma_start(out=xt[sl][0:PT, 0:ns, :], in_=src), w=[f"xt{sl}"], chan=f"xt{sl}")

        def transposes_to(src_b, PT, dstT, s, rkey, dkey):
            for half in range(2):
                def tr(e, half=half):
                    for hh in range(8):
                        h = half * 8 + hh
                        r_ = e.transpose(psQ[half][:, hh, 0:PT], src_b[0:PT, h * 64:(h + 1) * 64], ident_b[0:PT, 0:PT])
                    return r_
                S.add("pe", tr, r=[rkey, "ident_b"], w=[f"psQ{half}"])
                S.add("act", lambda e, half=half: e.activation(out=dstT[:, half * 8:(half + 1) * 8, s * PT:(s + 1) * PT],
                                                               in_=psQ[half][:, :, 0:PT], func=AF.Copy),
                      r=[f"psQ{half}"], w=[dkey])

        def cumsum_seq(sqd):
            nfull, part, key = sqd["nfull"], sqd["part"], sqd["key"]
            nb = nfull + (1 if part else 0)
            Call, Ball = K["C_" + key], K["B_" + key]
            S.add("dve", lambda e: e.memset(psa[:, 0, :], 0.0), w=["psa"])
            for b in range(nfull):
                S.add("dve", lambda e, b=b: e.tensor_tensor(out=psa[:, b + 1, :], in0=psa[:, b, :], in1=lf[:, b, :], op=ALU.add),
                      r=["lf", "psa"], w=["psa"])

            for c0 in range(0, nb, 32):
                c1 = min(nb, c0 + 32)

                def cs(e, c0=c0, c1=c1):
                    for b in range(c0, c1):
                        rows = 128 if b < nfull else part
                        o = psF[0:rows, (b - c0) * H:(b - c0 + 1) * H]
                        e.matmul(o, lhsT=tri_f[0:rows, 0:rows], rhs=lf[0:rows, b, :], start=True, stop=False)
                        r_ = e.matmul(o, lhsT=ones_f[:, 0:rows], rhs=psa[:, b, :], start=False, stop=True)
                    return r_
                S.add("pe", cs, r=["lf", "psa", "tri_f", "ones_f"], w=["psF"])
                f1 = min(c1, nfull)
                if f1 > c0:
                    S.add("dve", lambda e, c0=c0, f1=f1: e.tensor_copy(out=Call[:, c0:f1, :], in_=psF[:, 0:(f1 - c0) * H].rearrange("p (b h) -> p b h", h=H)),
                          r=["psF"], w=["C_" + key])
                if c1 > nfull:
                    S.add("dve", lambda e, c0=c0: e.tensor_copy(out=Call[0:part, nfull, :], in_=psF[0:part, (nfull - c0) * H:(nfull - c0 + 1) * H]),
                          r=["psF"], w=["C_" + key])
            nqt = sqd["nqt"]
            bidx = [4 * i for i in range(nqt)] if sqd["kind"] == "p" else [nfull]

            def bs(e):
                for i, bi in enumerate(bidx):
                    r_ = e.matmul(psF[:, i * H:(i + 1) * H], lhsT=ones_f[:], rhs=psa[:, bi, :], start=True, stop=True)
                return r_
            S.add("pe", bs, r=["psa", "ones_f", "C_" + key], w=["psF"])
            S.add("dve", lambda e: e.tensor_copy(out=Ball[:, :, :], in_=psF[:, 0:nqt * H].rearrange("p (b h) -> p b h", h=H)),
                  r=["psF"], w=["B_" + key])
            if sqd["kind"] == "p":
                S.add("dve", lambda e: e.tensor_tensor(out=clt[:, 0:nb, :].rearrange("p (q s) h -> p q s h", s=4),
                                                       in0=Call[:, 0:nb, :].rearrange("p (q s) h -> p q s h", s=4),
                                                       in1=Ball[:, :, :].unsqueeze(2).to_broadcast([128, nqt, 4, H]), op=ALU.subtract),
                      r=["C_" + key, "B_" + key], w=["clt"])
                qts = [(4 * i, 4, 128) for i in range(nqt)]
            else:
                S.add("dve", lambda e: e.tensor_tensor(out=clt[0:part, nfull, :], in0=Call[0:part, nfull, :], in1=Ball[0:part, 0, :], op=ALU.subtract),
                      r=["C_" + key, "B_" + key], w=["clt"])
                qts = [(nfull, 1, part)]
            for qi, (b0, nbq, rows) in enumerate(qts):
                wq = nbq * rows

                def trc(e, b0=b0, nbq=nbq, rows=rows):
                    for bb in range(nbq):
                        r_ = e.transpose(psF[0:16, bb * rows:(bb + 1) * rows], clt[0:rows, b0 + bb, :], ident_f[0:rows, 0:rows])
                    return r_
                S.add("pe", trc, r=["clt", "ident_f", "B_" + key], w=["psF"])
                S.add("act", lambda e, wq=wq: e.activation(out=clT[:, 0:wq], in_=psF[0:16, 0:wq], func=AF.Copy), r=["psF"], w=["clT"])
                S.add("dve", lambda e, wq=wq: e.tensor_copy(out=aug[:, 0, 0:wq], in_=clT[:, 0:wq]), r=["clT"], w=["aug"])
                S.add("dve", lambda e, wq=wq: e.tensor_tensor(out=r1[:, 0:wq], in0=clT[:, 0:wq], in1=aug[:, 0, 0:wq], op=ALU.subtract), r=["clT", "aug"], w=["sq"])
                S.add("dve", lambda e, wq=wq: e.tensor_copy(out=aug[:, 1, 0:wq], in_=r1[:, 0:wq]), r=["sq"], w=["aug"])
                S.add("dve", lambda e, wq=wq: e.tensor_tensor(out=r1[:, 0:wq], in0=r1[:, 0:wq], in1=aug[:, 1, 0:wq], op=ALU.subtract), r=["sq", "aug"], w=["sq"])
                S.add("dve", lambda e, wq=wq: e.tensor_copy(out=aug[:, 2, 0:wq], in_=r1[:, 0:wq]), r=["sq"], w=["aug"])
                if sqd["kind"] == "p":
                    dst = A["augp"][sqd["idx"]][:, :, qi * 512:(qi + 1) * 512]
                else:
                    dst = A["augs"][sqd["idx"]][:, :, :]
                S.add("pool", lambda e, dst=dst, wq=wq: e.dma_start(out=dst, in_=aug[:, :, 0:wq]), r=["aug"], chan="aug")

        def do_tile(ti, t):
            sl = ti % 2
            X = xt[sl]
            PT, ns = t["PT"], t["ns"]
            NT = PT * ns
            kind = t["kind"]
            if ti + 1 < len(tiles):
                load_x(ti + 1)
            if kind == "c":
                g, i = t["g"], t["i"]
                for s in range(ns):
                    S.add("act", lambda e, s=s: e.activation(out=kb[:, :], in_=X[:, s, :], func=AF.Copy), r=[f"xt{sl}"], w=["kb0"])
                    transposes_to(kb, 128, kTs, s, "kb0", "kTs")
                S.add("pool", lambda e: e.dma_start(out=A["kts"][g][:, :, i * 512:(i + 1) * 512].rearrange("h d t -> d h t"), in_=kTs[:, :, :]),
                      r=["kTs"], chan="kTs")
                for s in range(ns):
                    k2 = cnt["st"] % 2
                    cnt["st"] += 1
                    r0 = i * 512 + s * 128
                    S.add("sp", lambda e, k2=k2, r0=r0: e.dma_start(out=vf[k2][:, :], in_=A["cv"][g, r0:r0 + 128, :]), w=[f"vf{k2}"], chan=f"vf{k2}")
                    S.add("dve", lambda e, k2=k2: e.tensor_copy(out=vb[k2][:, :], in_=vf[k2][:, :]), r=[f"vf{k2}"], w=[f"vb{k2}"])
                    S.add("pool", lambda e, k2=k2, r0=r0: e.dma_start(out=A["vbs"][g][r0:r0 + 128, :], in_=vb[k2][:, :]),
                          r=[f"vb{k2}"], chan=f"vb{k2}")
                return
            for s in range(ns):
                S.add("act", lambda e, s=s: e.activation(out=sq[0:PT, :], in_=X[0:PT, s, :], func=AF.Square, accum_out=ssq[0:PT, s:s + 1]),
                      r=[f"xt{sl}"], w=["sq", "ssq"])
            S.add("act", lambda e: e.activation(out=rstd[0:PT, 0:ns], in_=ssq[0:PT, 0:ns], func=AF.Sqrt, scale=1.0 / D, bias=EPS), r=["ssq"], w=["rstd"])
            S.add("dve", lambda e: e.reciprocal(out=rstd[0:PT, 0:ns], in_=rstd[0:PT, 0:ns]), r=["rstd"], w=["rstd"])
            for s in range(ns):
                S.add("act", lambda e, s=s: e.activation(out=hbs[s % 2][0:PT, :], in_=X[0:PT, s, :], func=AF.Copy, scale=rstd[0:PT, s:s + 1]),
                      r=[f"xt{sl}", "rstd"], w=[f"hb{s % 2}"])

                def tr(e, s=s):
                    for cc in range(8):
                        r_ = e.transpose(psT[:, cc, 0:PT], hbs[s % 2][0:PT, cc * 128:(cc + 1) * 128], ident_b[0:PT, 0:PT])
                    return r_
                S.add("pe", tr, r=[f"hb{s % 2}", "ident_b"], w=["psT"])
                S.add("dve", lambda e, s=s: e.tensor_copy(out=hT[:, :, s * PT:(s + 1) * PT], in_=psT[:, :, 0:PT]), r=["psT"], w=["hT"])
            def make_sub(s):
                k2 = cnt["st"] % 2
                cnt["st"] += 1
                r0 = t["tok0"] + s * PT

                def proj(ps, col, wd, s=s):
                    def mm(e):
                        for cc in range(8):
                            r_ = e.matmul(ps[0:PT, 0:wd], lhsT=hT[:, cc, s * PT:(s + 1) * PT], rhs=W1in[:, cc, col:col + wd], start=(cc == 0), stop=(cc == 7))
                        return r_
                    return mm
                qb = qbs[s % 2]
                kb = kbs[s % 2]
                qbk, kbk = f"qb{s % 2}", f"kb{s % 2}"
                def qk_pe(which):
                    for hf in range(2):
                        S.add("pe", proj(psP[hf], which * D + hf * 512, 512), r=["hT", "W1in"], w=[f"psP{hf}"])

                def qk_sq(which):
                    for hf in range(2):
                        S.add("act", lambda e, hf=hf: e.activation(out=sq[0:PT, hf * 512:(hf + 1) * 512], in_=psP[hf][0:PT, :], func=AF.Square),
                              r=[f"psP{hf}"], w=["sq"])
                    S.add("dve", lambda e: e.tensor_reduce(out=ssh[0:PT, 0:H], in_=sq[0:PT, :].rearrange("p (h d) -> p h d", d=DH), axis=AX.X, op=ALU.add),
                          r=["sq"], w=["ssh"])

                def qk_sqrt(which):
                    S.add("act", lambda e: e.activation(out=ssh[0:PT, H:2 * H], in_=ssh[0:PT, 0:H], func=AF.Sqrt, scale=1.0 / DH, bias=EPS), r=["ssh"], w=["ssh"])
                    S.add("dve", lambda e: e.reciprocal(out=ssh[0:PT, H:2 * H], in_=ssh[0:PT, H:2 * H]), r=["ssh"], w=["ssh"])

                def qk_mul(which):
                    for hf in range(2):
                        S.add("dve", lambda e, hf=hf: e.tensor_tensor(out=sq[0:PT, hf * 512:(hf + 1) * 512].rearrange("p (h d) -> p h d", d=DH),
                                                                      in0=psP[hf][0:PT, :].rearrange("p (h d) -> p h d", d=DH),
                                                                      in1=ssh[0:PT, H + hf * 8:H + hf * 8 + 8].unsqueeze(2).to_broadcast([PT, 8, DH]), op=ALU.mult),
                              r=[f"psP{hf}", "ssh"], w=["sq"])
                    if which == 0:
                        S.add("dve", lambda e: e.tensor_tensor(out=qb[0:PT, :].rearrange("p (h d) -> p h d", d=DH), in0=sq[0:PT, :].rearrange("p (h d) -> p h d", d=DH),
                                                               in1=gqb[0:PT, :].unsqueeze(1).to_broadcast([PT, H, DH]), op=ALU.mult),
                              r=["sq", "gqb"], w=[qbk])
                    else:
                        S.add("dve", lambda e: e.tensor_tensor(out=kf[k2][0:PT, :].rearrange("p (h d) -> p h d", d=DH), in0=sq[0:PT, :].rearrange("p (h d) -> p h d", d=DH),
                                                               in1=gkb[0:PT, :].unsqueeze(1).to_broadcast([PT, H, DH]), op=ALU.mult),
                              r=["sq", "gkb"], w=[f"kf{k2}"])
                        kdst = A["kp"][r0:r0 + PT, :] if kind == "p" else A["ks"][0:PT, :]
                        S.add("sp", lambda e: e.dma_start(out=kdst, in_=kf[k2][0:PT, :]), r=[f"kf{k2}"], chan=f"kf{k2}")
                        S.add("act", lambda e: e.activation(out=kb[0:PT, :], in_=kf[k2][0:PT, :], func=AF.Copy), r=[f"kf{k2}"], w=[kbk])

                def stA():
                    qk_pe(0)
                    for hf in range(2):
                        S.add("pe", proj(psP[2 + hf], 2 * D + hf * 512, 512), r=["hT", "W1in"], w=[f"psP{2 + hf}"])
                    qk_sq(0)
                    for hf in range(2):
                        S.add("act", lambda e, hf=hf: e.activation(out=vf[k2][0:PT, hf * 512:(hf + 1) * 512], in_=psP[2 + hf][0:PT, :], func=AF.Copy),
                              r=[f"psP{2 + hf}"], w=[f"vf{k2}"])
                    qk_sqrt(0)
                    for hf in range(2):
                        S.add("dve", lambda e, hf=hf: e.tensor_copy(out=vb[k2][0:PT, hf * 512:(hf + 1) * 512], in_=vf[k2][0:PT, hf * 512:(hf + 1) * 512]),
                              r=[f"vf{k2}"], w=[f"vb{k2}"])
                    qk_mul(0)
                    vdst = A["vp"][r0:r0 + PT, :] if kind == "p" else A["vs"][0:PT, :]
                    S.add("sp", lambda e: e.dma_start(out=vdst, in_=vf[k2][0:PT, :]), r=[f"vf{k2}"], chan=f"vf{k2}")
                    if kind == "p":
                        tq = t["i"] * 512 + s * 128
                        S.add("sp", lambda e: e.dma_start(out=A["vbp"][t["q"]][tq:tq + 128, :], in_=vb[k2][:, :]), r=[f"vb{k2}"], chan=f"vb{k2}")
                    else:
                        for g in range(NSTR):
                            S.add("sp", lambda e, g=g: e.dma_start(out=A["vbs"][g][PAST:PAST + TS, :], in_=vb[k2][g * TS:(g + 1) * TS, :]),
                                  r=[f"vb{k2}"], chan=f"vb{k2}")

                def stB():
                    qk_pe(1)
                    for hf in range(2):
                        S.add("pe", proj(psP[2 + hf], 3 * D + hf * 512, 512), r=["hT", "W1in"], w=[f"psP{2 + hf}"])
                    S.add("pe", proj(psF, 4 * D, H), r=["hT", "W1in"], w=["psF"])
                    qk_sq(1)
                    S.add("dve", lambda e: e.tensor_tensor(out=lft[0:PT, 0:H], in0=psF[0:PT, 0:H], in1=bfb[0:PT, :], op=ALU.add), r=["psF", "bfb"], w=["lft"])
                    S.add("dve", lambda e: e.tensor_scalar_min(out=lft[0:PT, 2 * H:3 * H], in0=lft[0:PT, 0:H], scalar1=0.0), r=["lft"], w=["lft2"])
                    qk_sqrt(1)
                    S.add("act", lambda e: e.activation(out=lft[0:PT, H:2 * H], in_=lft[0:PT, 0:H], func=AF.Abs), r=["lft"], w=["lft1"])
                    qk_mul(1)
                    for hf in range(2):
                        S.add("act", lambda e, hf=hf: e.activation(out=szb[k2][0:PT, hf * 512:(hf + 1) * 512], in_=psP[2 + hf][0:PT, :], func=AF.Silu),
                              r=[f"psP{2 + hf}"], w=[f"szb{k2}"])
                    S.add("sp", lambda e: e.dma_start(out=A["szs"][r0:r0 + PT, :], in_=szb[k2][0:PT, :]), r=[f"szb{k2}"], chan=f"szb{k2}")
                    S.add("act", lambda e: e.activation(out=lft[0:PT, H:2 * H], in_=lft[0:PT, H:2 * H], func=AF.Exp, scale=-1.0), r=["lft1"], w=["lft1"])
                    S.add("act", lambda e: e.activation(out=lft[0:PT, H:2 * H], in_=lft[0:PT, H:2 * H], func=AF.Ln, bias=1.0), r=["lft1"], w=["lft1"])
                    if kind == "p":
                        blk = t["i"] * 4 + s
                        S.add("dve", lambda e: e.tensor_tensor(out=lf[:, blk, :], in0=lft[:, 2 * H:3 * H], in1=lft[:, H:2 * H], op=ALU.subtract),
                              r=["lft1", "lft2"], w=["lf"])
                        S.add("sp", lambda e: e.dma_start(out=A["lfp"][r0:r0 + 128, :], in_=lf[:, blk, :]), r=["lf"], chan="lf")
                    else:
                        S.add("dve", lambda e: e.tensor_tensor(out=lft[0:PT, 3 * H:4 * H], in0=lft[0:PT, 2 * H:3 * H], in1=lft[0:PT, H:2 * H], op=ALU.subtract),
                              r=["lft1", "lft2"], w=["lft3"])
                        S.add("sp", lambda e: e.dma_start(out=A["lfs"][0:PT, :], in_=lft[0:PT, 3 * H:4 * H]), r=["lft3"], w=["lfs_dram"], chan="lft")

                def stC():
                    transposes_to(qb, PT, qTs, s, qbk, "qTs")
                    transposes_to(kb, PT, kTs, s, kbk, "kTs")
                return stA, stB, stC

            stages = []
            for s in range(ns):
                stages.append(make_sub(s))
            stages[0][0]()
            stages[0][1]()
            for s in range(1, ns):
                stages[s][0]()
                stages[s - 1][2]()
                stages[s][1]()
            stages[ns - 1][2]()
            if kind == "p":
                q, i = t["q"], t["i"]
                S.add("pool", lambda e: e.dma_start(out=A["qtp"][q][:, :, i * 512:(i + 1) * 512].rearrange("h d t -> d h t"), in_=qTs[:, :, :]), r=["qTs"], chan="qTs")
                S.add("pool", lambda e: e.dma_start(out=A["ktp"][q][:, :, i * 512:(i + 1) * 512].rearrange("h d t -> d h t"), in_=kTs[:, :, :]), r=["kTs"], chan="kTs")
                if t["last"]:
                    cumsum_seq([x for x in seq_list(c) if x["kind"] == "p" and x["idx"] == q][0])
            else:
                for g in range(NSTR):
                    S.add("pool", lambda e, g=g: e.dma_start(out=A["qts"][g][:, :, :].rearrange("h d t -> d h t"), in_=qTs[:, :, g * TS:(g + 1) * TS]), r=["qTs"], chan="qTs")
                    S.add("pool", lambda e, g=g: e.dma_start(out=A["kts"][g][:, :, PAST:PAST + TS].rearrange("h d t -> d h t"), in_=kTs[:, :, g * TS:(g + 1) * TS]), r=["kTs"], chan="kTs")
                for g in range(NSTR):
                    sqd = [x for x in seq_list(c) if x["kind"] == "s" and x["idx"] == g][0]
                    nfull = sqd["nfull"]
                    S.add("sp", lambda e, g=g, nfull=nfull: e.dma_start(out=lf[:, 0:nfull, :], in_=A["clf"][g].rearrange("(b p) h -> p b h", p=128)),
                          w=["lf"], chan="lfin")
                    S.add("sp", lambda e, g=g, nfull=nfull: e.dma_start(out=lf[0:TS, nfull, :], in_=A["lfs"][g * TS:(g + 1) * TS, :]),
                          r=["lfs_dram"], w=["lf"], chan="lfin")
                    cumsum_seq(sqd)

        load_x(0)
        for ti_, t_ in enumerate(tiles):
            do_tile(ti_, t_)
        print('l1a adds', S.nadd)
        S.limit = None
        S.run_phase()


def phase_l1b(S, nc, c, A, K):
    T, NSEQ, NSTR, NP, NS, PAST, LS = c.T, c.NSEQ, c.NSTR, c.NP, c.NS, c.PAST, c.LS
    tri_b = K["tri_b"]
    LMAX = max(T, LS)
    NBMAX = max(T // 128, PAST // 128 + 1)
    NQTMAX = max(T // 512, 1)
    with ExitStack() as es:
        sb = lambda name, shape, dt=F32: es.enter_context(nc.sbuf_tensor("l1b_" + name, list(shape), dt))
        pm = lambda name, shape, dt=F32: es.enter_context(nc.psum_tensor("l1b_" + name, list(shape), dt))
        KTa = [sb(f"KTa{i}", [128, LMAX], BF16) for i in range(2)]
        QTa = [sb(f"QTa{i}", [128, T], BF16) for i in range(2)]
        Va = [sb(f"Va{i}", [128, NBMAX, 65], BF16) for i in range(2)]
        bias = [sb(f"bias{i}", [128, NBMAX, NQTMAX]) for i in range(2)]
        Pt = [sb(f"Pt{i}", [128, 512], BF16) for i in range(6)]
        ost = [sb(f"ost{i}", [128, 4, DH], BF16) for i in range(2)]
        rden = sb("rden", [128, 4])
        psS = [pm(f"psS{i}", [128, 512]) for i in range(4)]
        psO = [pm(f"psO{i}", [128, 128]) for i in range(4)]
        for i in range(2):
            S.add("pool", lambda e, i=i: e.memset(KTa[i][64:128, :], 0.0), w=[f"KTa{i}"])
            S.add("pool", lambda e, i=i: e.memset(KTa[i][64:67, :], 1.0), w=[f"KTa{i}"])
            S.add("pool", lambda e, i=i: e.memset(QTa[i][64:128, :], 0.0), w=[f"QTa{i}", f"QTa{i}x"])
            S.add("pool", lambda e, i=i: e.memset(Va[i][:, :, 64:65], 1.0), w=[f"Va{i}"])
        jobs = [(sq, h) for sq in seq_list(c) for h in range(H)]
        cnt = {"p": 0, "o": 0}

        def load(ji):
            sq, h = jobs[ji]
            sl = ji % 2
            nfull, part, nqt, key = sq["nfull"], sq["part"], sq["nqt"], sq["key"]
            L = nfull * 128 + part
            if sq["kind"] == "p":
                kt, qt, vbh, au, NQ = A["ktp"][sq["idx"]][h], A["qtp"][sq["idx"]][h], A["vbp"][sq["idx"]][:, h * DH:(h + 1) * DH], A["augp"][sq["idx"]][h], T
            else:
                kt, qt, vbh, au, NQ = A["kts"][sq["idx"]][h], A["qts"][sq["idx"]][h], A["vbs"][sq["idx"]][:, h * DH:(h + 1) * DH], A["augs"][sq["idx"]][h], TS
            S.add("sp", lambda e: e.dma_start(out=KTa[sl][0:64, 0:L], in_=kt), w=[f"KTa{sl}"], chan=f"KTa{sl}")
            S.add("sp", lambda e: e.dma_start(out=QTa[sl][0:64, 0:NQ], in_=qt), w=[f"QTa{sl}"], chan=f"QTa{sl}")
            S.add("sp", lambda e: e.dma_start(out=QTa[sl][64:67, 0:NQ], in_=au), w=[f"QTa{sl}x"], chan=f"QTa{sl}x")
            for b0 in range(0, nfull, 4):
                b1 = min(nfull, b0 + 4)
                S.add("sp", lambda e, b0=b0, b1=b1: e.dma_start(out=Va[sl][:, b0:b1, 0:DH], in_=vbh[b0 * 128:b1 * 128, :].rearrange("(b p) d -> p b d", p=128)),
                      w=[f"Va{sl}_{b0 // 4}"], chan=f"Va{sl}_{b0 // 4}")
            if part:
                S.add("sp", lambda e: e.dma_start(out=Va[sl][0:part, nfull, 0:DH], in_=vbh[nfull * 128:nfull * 128 + part, :]),
                      w=[f"Va{sl}_{nfull // 4}"], chan=f"Va{sl}_p")
            nb = nfull + (1 if part else 0)
            Call, Ball = K["C_" + key], K["B_" + key]
            S.add("dve", lambda e: e.tensor_tensor(out=bias[sl][:, 0:nfull, 0:nqt], in0=Ball[:, :, h].unsqueeze(1).to_broadcast([128, nfull, nqt]),
                                                   in1=Call[:, 0:nfull, h].unsqueeze(2).to_broadcast([128, nfull, nqt]), op=ALU.subtract),
                  w=[f"bias{sl}"])
            if part:
                S.add("dve", lambda e: e.tensor_tensor(out=bias[sl][0:part, nfull, 0:nqt], in0=Ball[0:part, :, h], in1=Call[0:part, nfull:nfull + 1, h], op=ALU.subtract),
                      w=[f"bias{sl}"])

        def run(ji):
            sq, h = jobs[ji]
            sl = ji % 2
            nfull, part, nqt = sq["nfull"], sq["part"], sq["nqt"]
            isp = sq["kind"] == "p"
            def qtile(qi):
                if isp:
                    q0, NQt, nsub, wsub = qi * 512, 512, 4, 128
                    blocks = list(range(4 * qi + 4))
                else:
                    q0, NQt, nsub, wsub = 0, TS, 1, TS
                    blocks = list(range(nfull + 1))
                oslot = cnt["o"] % 2
                cnt["o"] += 1
                LA = 3
                descs = []

                def front(b):
                    KB = 128 if b < nfull else part
                    if isp:
                        j = b - 4 * qi
                        diag = j >= 0
                        c0 = 128 * j if diag else 0
                    else:
                        diag = b == nfull
                        c0 = 0
                    c1 = NQt
                    pb = cnt["p"] % 6
                    sb_ = cnt["p"] % 4
                    cnt["p"] += 1
                    S.add("pe", lambda e: e.matmul(psS[sb_][0:KB, c0:c1], lhsT=KTa[sl][0:128, b * 128:b * 128 + KB],
                                                   rhs=QTa[sl][0:128, q0 + c0:q0 + c1], start=True, stop=True),
                          r=[f"KTa{sl}", f"QTa{sl}", f"QTa{sl}x"], w=[f"psS{sb_}"])
                    S.add("act", lambda e: e.activation(out=Pt[pb][0:KB, c0:c1], in_=psS[sb_][0:KB, c0:c1], func=AF.Exp,
                                                        bias=bias[sl][0:KB, b, qi:qi + 1]),
                          r=[f"psS{sb_}", f"bias{sl}"], w=[f"Pt{pb}"], noself=True)
                    if diag:
                        S.add("dve", lambda e: e.tensor_tensor(out=Pt[pb][0:KB, c0:c0 + KB], in0=Pt[pb][0:KB, c0:c0 + KB],
                                                                in1=tri_b[0:KB, 0:KB], op=ALU.mult),
                              r=[f"Pt{pb}", "tri_b"], w=[f"Pt{pb}"])
                    descs.append((b, KB, c0, pb))

                def back(b, KB, c0, pb):
                    def pv(e):
                        r_ = None
                        for s_ in range(nsub):
                            if s_ * wsub < c0:
                                continue
                            lastb = (4 * qi + s_) if isp else nfull
                            r_ = e.matmul(psO[s_][0:wsub, 0:65], lhsT=Pt[pb][0:KB, s_ * wsub:(s_ + 1) * wsub], rhs=Va[sl][0:KB, b, 0:65],
                                          start=(b == 0), stop=(b == lastb))
                        return r_
                    S.add("pe", pv, r=[f"Pt{pb}", f"Va{sl}_{b // 4}", f"Va{sl}"], w=["psO"])

                for idx in range(len(blocks) + LA):
                    if idx < len(blocks):
                        front(blocks[idx])
                    if idx - LA >= 0:
                        back(*descs[idx - LA])
                for s_ in range(nsub):
                    S.add("dve", lambda e, s_=s_: e.reciprocal(out=rden[0:wsub, s_:s_ + 1], in_=psO[s_][0:wsub, 64:65]), r=["psO"], w=["rden"])
                    S.add("dve", lambda e, s_=s_: e.tensor_scalar(out=ost[oslot][0:wsub, s_, :], in0=psO[s_][0:wsub, 0:DH], scalar1=rden[0:wsub, s_:s_ + 1],
                                                                 scalar2=None, op0=ALU.mult),
                          r=["psO", "rden"], w=[f"ost{oslot}"])
                if isp:
                    r0 = sq["idx"] * T + q0
                    S.add("pool", lambda e, r0=r0: e.dma_start(out=A["obuf"][r0:r0 + 512, h * DH:(h + 1) * DH].rearrange("(s p) d -> p s d", p=128),
                                                              in_=ost[oslot][:, :, :]), r=[f"ost{oslot}"], chan=f"ost{oslot}")
                else:
                    r0 = NP + sq["idx"] * TS
                    S.add("pool", lambda e, r0=r0: e.dma_start(out=A["obuf"][r0:r0 + TS, h * DH:(h + 1) * DH], in_=ost[oslot][0:TS, 0, :]),
                          r=[f"ost{oslot}"], chan=f"ost{oslot}")

            for qi_ in range(nqt):
                qtile(qi_)

        load(0)
        for ji in range(len(jobs)):
            if ji + 1 < len(jobs):
                load(ji + 1)
            run(ji)
        S.run_phase()


def phase_l1c(S, nc, c, A, K):
    T, NSEQ, NSTR, NP, NS = c.T, c.NSEQ, c.NSTR, c.NP, c.NS
    ident_b = K["ident_b"]
    with ExitStack() as es:
        sb = lambda name, shape, dt=F32: es.enter_context(nc.sbuf_tensor("l1c_" + name, list(shape), dt))
        pm = lambda name, shape, dt=F32: es.enter_context(nc.psum_tensor("l1c_" + name, list(shape), dt))
        W1out = sb("W1out", [128, 8, D], BF16)
        xt = [sb(f"xt{i}", [128, 4, D]) for i in range(2)]
        stg = [xt[i][:, :, :].rearrange("p s (a b) -> p (s a) b", b=512) for i in range(2)]
        ot = [sb(f"ot{i}", [128, 4, D], BF16) for i in range(2)]
        zt = [sb(f"zt{i}", [128, 4, D], BF16) for i in range(2)]
        fng = sb("fng", [128, D])
        mb = [sb(f"mb{i}", [128, D], BF16) for i in range(2)]
        mT = [sb(f"mT{i}", [128, 8, 128], BF16) for i in range(2)]
        x2 = [sb(f"x2{i}", [128, D]) for i in range(2)]
        junk = sb("junk", [128, D], BF16)
        ssq = [sb(f"ssq{i}", [128, 2]) for i in range(2)]
        yo = [sb(f"yo{i}", [128, D]) for i in range(2)]
        psT = [pm(f"psT{i}", [128, 8, 128], BF16) for i in range(2)]
        psP = [pm(f"psP{i}", [128, 512]) for i in range(4)]
        wout_v = A["w_fox_out"].rearrange("(c p) e -> p c e", p=128)
        for pc in range(2):
            st = stg[pc % 2]
            S.add("sp", lambda e, st=st, pc=pc: e.dma_start(out=st, in_=wout_v[:, :, pc * 512:(pc + 1) * 512]), w=[f"xt{pc % 2}"], chan=f"xt{pc % 2}")
            S.add("act", lambda e, st=st, pc=pc: e.activation(out=W1out[:, :, pc * 512:(pc + 1) * 512], in_=st, func=AF.Copy), r=[f"xt{pc % 2}"], w=["W1out"])
        S.add("sp", lambda e: e.dma_start(out=fng[:], in_=A["fng"].partition_broadcast(128)), w=["fng"], chan="fng")
        tiles = []
        for i in range(NP // 512):
            tiles.append(dict(kind="p", tok0=i * 512, PT=128, ns=4))
        tiles.append(dict(kind="s", tok0=NP, PT=NS, ns=1))

        def load(ti):
            t = tiles[ti]
            sl = ti % 2
            PT, ns = t["PT"], t["ns"]
            rows = slice(t["tok0"], t["tok0"] + PT * ns)
            S.add("sp", lambda e: e.dma_start(out=xt[sl][0:PT, 0:ns, :], in_=A["x1"][rows, :].rearrange("(s p) d -> p s d", p=PT)), w=[f"xt{sl}"], chan=f"xt{sl}")
            S.add("sp", lambda e: e.dma_start(out=ot[sl][0:PT, 0:ns, :], in_=A["obuf"][rows, :].rearrange("(s p) d -> p s d", p=PT)), w=[f"ot{sl}"], chan=f"ot{sl}")
            S.add("sp", lambda e: e.dma_start(out=zt[sl][0:PT, 0:ns, :], in_=A["szs"][rows, :].rearrange("(s p) d -> p s d", p=PT)), w=[f"zt{sl}"], chan=f"zt{sl}")

        subs = [(ti, s) for ti, t in enumerate(tiles) for s in range(t["ns"])]

        def front(k):
            ti, s = subs[k]
            t = tiles[ti]
            sl, PT, b2 = ti % 2, t["PT"], k % 2
            S.add("dve", lambda e: e.tensor_tensor(out=mb[b2][0:PT, :], in0=ot[sl][0:PT, s, :], in1=zt[sl][0:PT, s, :], op=ALU.mult),
                  r=[f"ot{sl}", f"zt{sl}"], w=[f"mb{b2}"])

            def tr(e):
                for cc in range(8):
                    r_ = e.transpose(psT[b2][:, cc, 0:PT], mb[b2][0:PT, cc * 128:(cc + 1) * 128], ident_b[0:PT, 0:PT])
                return r_
            S.add("pe", tr, r=[f"mb{b2}", "ident_b"], w=[f"psT{b2}"])
            S.add("act", lambda e: e.activation(out=mT[b2][:, :, 0:PT], in_=psT[b2][:, :, 0:PT], func=AF.Copy), r=[f"psT{b2}"], w=[f"mT{b2}"])

        def back(k):
            ti, s = subs[k]
            t = tiles[ti]
            sl, PT, b2 = ti % 2, t["PT"], k % 2
            for hf in range(2):
                pp = psP[b2 * 2 + hf]
                pk = f"psP{b2 * 2 + hf}"

                def op(e, pp=pp, hf=hf):
                    for j in range(8):
                        r_ = e.matmul(pp[0:PT, :], lhsT=mT[b2][:, j, 0:PT], rhs=W1out[:, j, hf * 512:(hf + 1) * 512], start=(j == 0), stop=(j == 7))
                    return r_
                S.add("pe", op, r=[f"mT{b2}", "W1out"], w=[pk])
                S.add("dve", lambda e, pp=pp, hf=hf: e.tensor_tensor(out=x2[b2][0:PT, hf * 512:(hf + 1) * 512], in0=pp[0:PT, :],
                                                                   in1=xt[sl][0:PT, s, hf * 512:(hf + 1) * 512], op=ALU.add),
                      r=[pk, f"xt{sl}"], w=[f"x2{b2}"])
            S.add("act", lambda e: e.activation(out=junk[0:PT, :], in_=x2[b2][0:PT, :], func=AF.Square, accum_out=ssq[b2][0:PT, 0:1]),
                  r=[f"x2{b2}"], w=["junk", f"ssq{b2}"])
            S.add("act", lambda e: e.activation(out=ssq[b2][0:PT, 1:2], in_=ssq[b2][0:PT, 0:1], func=AF.Sqrt, scale=1.0 / D, bias=EPS),
                  r=[f"ssq{b2}"], w=[f"ssq{b2}"])
            S.add("dve", lambda e: e.reciprocal(out=ssq[b2][0:PT, 1:2], in_=ssq[b2][0:PT, 1:2]), r=[f"ssq{b2}"], w=[f"ssq{b2}"])
            S.add("dve", lambda e: e.scalar_tensor_tensor(out=yo[b2][0:PT, :], in0=x2[b2][0:PT, :], scalar=ssq[b2][0:PT, 1:2], in1=fng[0:PT, :],
                                                          op0=ALU.mult, op1=ALU.mult), r=[f"x2{b2}", f"ssq{b2}", "fng"], w=[f"yo{b2}"])
            r0 = t["tok0"] + s * PT
            dst = A["yp"][r0:r0 + PT, :] if t["kind"] == "p" else A["ys"][0:PT, :]
            S.add("sp", lambda e: e.dma_start(out=dst, in_=yo[b2][0:PT, :]), r=[f"yo{b2}"], chan=f"yo{b2}")
            if s == t["ns"] - 1 and ti + 2 < len(tiles):
                load(ti + 2)

        load(0)
        if len(tiles) > 1:
            load(1)
        front(0)
        for k in range(len(subs)):
            if k + 1 < len(subs):
                front(k + 1)
            back(k)
        S.run_phase()


_NC_CACHE = {}


def kernel(**inputs):
    cfg = Cfg()
    n_cores = 8
    if "nc" not in _NC_CACHE:
        _NC_CACHE["nc"] = build(cfg)
    nc = _NC_CACHE["nc"]
    maps = make_in_maps(cfg, inputs)
    res = run_bass_kernel_spmd(nc, maps, core_ids=list(range(n_cores)))
    R = res.results
    B, Tn = 16, cfg.T
    cat = lambda k: np.concatenate([np.asarray(r[k]) for r in R], axis=0)
    y_prompt = cat("yp").reshape(B, Tn, D)
    y_sample = cat("ys").reshape(32, TS, D)
    conv_p = cat("convp").reshape(1, B, CS, D)
    conv_s = cat("convs").reshape(1, 32, CS, D)
    k_p = cat("kp").reshape(1, B, Tn, H, DH)
    v_p = cat("vp").reshape(1, B, Tn, H, DH)
    lf_p = cat("lfp").reshape(1, B, Tn, H)
    k_s = cat("ks").reshape(1, 32, TS, H, DH)
    v_s = cat("vs").reshape(1, 32, TS, H, DH)
    lf_s = cat("lfs").reshape(1, 32, TS, H)
    return (y_prompt, y_sample, conv_p, conv_s, k_p, v_p, lf_p, k_s, v_s, lf_s)
```

```python
import numpy as np
from contextlib import ExitStack
import concourse.bass as bass
import concourse.mybir as mybir
from concourse.bass_utils import run_bass_kernel_spmd

F32 = mybir.dt.float32
BF16 = mybir.dt.bfloat16
AF = mybir.ActivationFunctionType
ALU = mybir.AluOpType
AX = mybir.AxisListType
EPS = 1e-6
D = 1024
H = 16
DH = 64
CW = 31
CS = 30
TS = 16


class Cfg:
    def __init__(self, T=4096, NSEQ=2, NSTR=4, PAST=4096):
        self.T, self.NSEQ, self.NSTR, self.PAST = T, NSEQ, NSTR, PAST
        self.NP = NSEQ * T
        self.NS = NSTR * TS
        self.LS = PAST + TS
        self.do_l1 = 2
        self.debug = False


class Ins:
    __slots__ = ("eng", "fn", "dma", "chan", "deps", "need", "val", "sem")


class Sched:
    def __init__(self, nc, es):
        self.nc = nc
        self.es = es
        self.csem = {e: es.enter_context(nc.semaphore("c_" + e)) for e in ("pe", "act", "dve", "pool")}
        self.ccnt = dict.fromkeys(self.csem, 0)
        self.dsem = {}
        self.dcnt = {}
        self.waited = {e: {} for e in ("sp", "pe", "act", "dve", "pool")}
        self.ninst = 0
        self.dhist = []
        self.maxdma = 1 << 30
        self.reset()

    def reset(self):
        self.ins = []
        self.lw = {}
        self.rd = {}

    def add(self, eng, fn, r=(), w=(), chan=None, noself=False):
        self.nadd = getattr(self, 'nadd', 0) + 1
        if getattr(self, 'limit', None) is not None and self.nadd > self.limit:
            return None
        i = Ins()
        if chan is not None:
            eng = "sp"
        i.eng, i.fn, i.dma, i.chan = eng, fn, chan is not None, chan
        i.need = i.dma
        deps = {}
        for k in r:
            d = self.lw.get(k)
            if d is not None:
                deps[id(d)] = d
        for k in w:
            d = self.lw.get(k)
            if d is not None:
                deps[id(d)] = d
            for d in self.rd.get(k, ()):
                deps[id(d)] = d
        deps.pop(id(i), None)
        i.deps = [d for d in deps.values() if not (d.eng == "pe" and eng == "pe" and not d.dma and chan is None)]
        if noself:
            i.deps = [d for d in i.deps if d.eng != eng or d.dma]
        for d in i.deps:
            d.need = True
        for k in r:
            self.rd.setdefault(k, []).append(i)
        for k in w:
            self.lw[k] = i
            self.rd[k] = []
        self.ins.append(i)
        return i

    def run_phase(self):
        nc = self.nc
        last = {}
        for i in self.ins:
            if not i.dma:
                last[i.eng] = i
        for i in last.values():
            i.need = True
        for i in self.ins:
            if i.dma:
                if i.chan not in self.dsem:
                    self.dsem[i.chan] = self.es.enter_context(nc.semaphore("d_" + i.chan))
                    self.dcnt[i.chan] = 0
                self.dcnt[i.chan] += 16
                i.sem, i.val = self.dsem[i.chan], self.dcnt[i.chan]
            elif i.need:
                self.ccnt[i.eng] += 1
                i.sem, i.val = self.csem[i.eng], self.ccnt[i.eng]
        self.ninst += len(self.ins)
        allsems = [(self.csem[e], self.ccnt[e]) for e in self.csem] + [(self.dsem[c], self.dcnt[c]) for c in self.dsem]
        ins = self.ins
        with nc.Block() as blk:
            for ename, deco in (("sp", blk.sync), ("pe", blk.tensor), ("act", blk.scalar),
                                ("dve", blk.vector), ("pool", blk.gpsimd)):
                def body(eng, ename=ename):
                    wd = self.waited[ename]
                    for i in ins:
                        if i.eng != ename:
                            continue
                        req = {}
                        for d in i.deps:
                            k = id(d.sem)
                            if k not in req or req[k][1] < d.val:
                                req[k] = (d.sem, d.val)
                        for k, (sem, val) in req.items():
                            if wd.get(k, 0) < val:
                                eng.wait_ge(sem, val)
                                wd[k] = val
                        if i.dma:
                            hist = self.dhist
                            if len(hist) >= self.maxdma:
                                p = hist[-self.maxdma]
                                k = id(p.sem)
                                if wd.get(k, 0) < p.val:
                                    eng.wait_ge(p.sem, p.val)
                                    wd[k] = p.val
                            hist.append(i)
                        r = i.fn(eng)
                        if i.need:
                            r.then_inc(i.sem, 16 if i.dma else 1)
                    for sem, val in allsems:
                        k = id(sem)
                        if val > 0 and wd.get(k, 0) < val:
                            eng.wait_ge(sem, val)
                            wd[k] = val
                deco(body)
        self.reset()


def build(cfg):
    c = cfg
    nc = bass.Bass("TRN2", target_bir_lowering=False)
    A = {}

    def din(name, shape, dt=F32):
        A[name] = nc.dram_tensor(name, list(shape), dt, kind="ExternalInput").ap()

    def dout(name, shape, dt=F32):
        A[name] = nc.dram_tensor(name, list(shape), dt, kind="ExternalOutput").ap()

    def dscr(name, shape, dt=F32):
        A[name] = nc.dram_tensor(name, list(shape), dt, kind="Internal").ap()

    T, NSEQ, NSTR, PAST, NP, NS, LS = c.T, c.NSEQ, c.NSTR, c.PAST, c.NP, c.NS, c.LS
    din("xp", [NP, D]); din("xs", [NS, D]); din("sconv", [NSTR, CS, D])
    din("ck", [NSTR, PAST, D]); din("cv", [NSTR, PAST, D]); din("clf", [NSTR, PAST, H])
    din("norm_g", [2, D]); din("fng", [D]); din("w_conv_in", [D, 3 * D]); din("w_dw", [CW, D])
    din("b_dw", [D]); din("lng", [D]); din("lnb", [D]); din("w_conv_out", [D, D])
    din("w_fox_in", [D, 4 * D + H]); din("b_f", [H]); din("qg", [DH]); din("kg", [DH]); din("w_fox_out", [D, D])
    din("ident", [128, 128]); din("tri", [128, 128])
    dout("yp", [NP, D]); dout("ys", [NS, D]); dout("convp", [NSEQ, CS, D]); dout("convs", [NSTR, CS, D])
    dout("kp", [NP, D]); dout("vp", [NP, D]); dout("lfp", [NP, H])
    dout("ks", [NS, D]); dout("vs", [NS, D]); dout("lfs", [NS, H])
    (dout if c.debug else dscr)("x1", [NP + NS, D])
    dscr("szs", [NP + NS, D], BF16)
    dscr("obuf", [NP + NS, D], BF16)
    dscr("qtp", [NSEQ, H, DH, T], BF16); dscr("ktp", [NSEQ, H, DH, T], BF16); dscr("vbp", [NSEQ, T, D], BF16)
    dscr("augp", [NSEQ, H, 3, T], BF16)
    dscr("qts", [NSTR, H, DH, TS], BF16); dscr("kts", [NSTR, H, DH, LS], BF16); dscr("vbs", [NSTR, LS, D], BF16)
    dscr("augs", [NSTR, H, 3, TS], BF16)

    with ExitStack() as ges:
        S = Sched(nc, ges)
        gsb = lambda name, shape, dt=F32: ges.enter_context(nc.sbuf_tensor(name, list(shape), dt))
        K = {}
        K["ident_f"] = gsb("ident_f", [128, 128]); K["ident_b"] = gsb("ident_b", [128, 128], BF16)
        K["tri_f"] = gsb("tri_f", [128, 128]); K["tri_b"] = gsb("tri_b", [128, 128], BF16)
        K["ones_f"] = gsb("ones_f", [128, 128]); K["ones_b"] = gsb("ones_b", [128, 128], BF16)
        S.add("sp", lambda e: e.dma_start(out=K["ident_f"][:], in_=A["ident"]), w=["ident_f"], chan="ident_f")
        S.add("sp", lambda e: e.dma_start(out=K["tri_f"][:], in_=A["tri"]), w=["tri_f"], chan="tri_f")
        S.add("dve", lambda e: e.tensor_copy(out=K["ident_b"][:], in_=K["ident_f"][:]), r=["ident_f"], w=["ident_b"])
        S.add("dve", lambda e: e.tensor_copy(out=K["tri_b"][:], in_=K["tri_f"][:]), r=["tri_f"], w=["tri_b"])
        S.add("dve", lambda e: e.memset(K["ones_f"][:], 1.0), w=["ones_f"])
        S.add("dve", lambda e: e.memset(K["ones_b"][:], 1.0), w=["ones_b"])
        phase_l0(S, nc, c, A, K)
        alloc_persist(nc, ges, c, K)
        if c.do_l1:
            phase_l1a(S, nc, c, A, K)
            if c.do_l1 > 1:
                phase_l1b(S, nc, c, A, K)
                phase_l1c(S, nc, c, A, K)
        print("bass instructions (logical):", S.ninst)
    return nc


def phase_l0(S, nc, c, A, K):
    T, NSEQ, NSTR, NP, NS = c.T, c.NSEQ, c.NSTR, c.NP, c.NS
    ident_f, ident_b, ones_b = K["ident_f"], K["ident_b"], K["ones_b"]
    with ExitStack() as es:
        sb = lambda name, shape, dt=F32: es.enter_context(nc.sbuf_tensor("l0_" + name, list(shape), dt))
        pm = lambda name, shape, dt=F32: es.enter_context(nc.psum_tensor("l0_" + name, list(shape), dt))
        W0in = sb("W0in", [128, 8, 3 * D], BF16); W0out = sb("W0out", [128, 8, D], BF16)
        vrow = sb("vrow", [24, 128]); vcol = sb("vcol", [128, 24])
        gcol = sb("gcol", [128, 8])
        wdwc = sb("wdwc", [128, 8, 32])
        dg = [sb(f"dg{i}", [128, CW, 128], BF16) for i in range(2)]
        xt = [sb(f"xt{i}", [128, 4, D]) for i in range(2)]
        stg = [xt[i][:, :, :].rearrange("p s (a b) -> p (s a) b", b=512) for i in range(2)]
        hb = sb("hb", [128, 4, D], BF16); hT = sb("hT", [128, 8, 512], BF16)
        junk = sb("junk", [128, D], BF16)
        ssq = sb("ssq", [128, 4]); rstd = sb("rstd", [128, 4])
        vext = [sb(f"vext{i}", [128, 8, 512 + CS], BF16) for i in range(2)]
        sig = [sb(f"sig{i}", [128, 512]) for i in range(2)]
        szT = sb("szT", [128, 8, 512], BF16)
        ybf = sb("ybf", [128, 8, 512], BF16); ysq = [sb(f"ysq{i}", [128, 512], BF16) for i in range(2)]
        mean = sb("mean", [128, 512]); rs = sb("rs", [128, 512]); tmp = [sb(f"tmp{i}", [128, 512]) for i in range(2)]
        mT = sb("mT", [128, 8, 512], BF16)
        xo = [sb(f"xo{i}", [128, D]) for i in range(2)]
        vt = sb("vt", [128, 8, 64]); cst = sb("cst", [64, D]); hist = cst; wdwr = cst
        psT = pm("psT", [128, 8, 128], BF16)
        psA = pm("psA", [128, 512]); psG = pm("psG", [128, 512]); psZ = pm("psZ", [128, 512])
        psY = [pm(f"psY{i}", [128, 512]) for i in range(2)]
        psS = [pm(f"psS{i}", [128, 512]) for i in range(2)]

        for i, nm in enumerate(("b_dw", "lng", "lnb")):
            S.add("sp", lambda e, i=i, nm=nm: e.dma_start(out=vrow[i * 8:(i + 1) * 8, :], in_=A[nm].rearrange("(j p) -> j p", p=128)),
                  w=["vrow"], chan="vrow")
        S.add("pe", lambda e: e.transpose(psS[0][:, 0:24], vrow[:], ident_f[0:24, 0:24]), r=["vrow", "ident_f"], w=["psS0"])
        S.add("dve", lambda e: e.tensor_copy(out=vcol[:], in_=psS[0][:, 0:24]), r=["psS0"], w=["vcol"])
        bdw, lng, lnb = vcol[:, 0:8], vcol[:, 8:16], vcol[:, 16:24]
        S.add("sp", lambda e: e.dma_start(out=vrow[0:8, :], in_=A["norm_g"][0].rearrange("(j p) -> j p", p=128)),
              r=["vrow"], w=["vrow"], chan="vrow")
        S.add("pe", lambda e: e.transpose(psS[1][:, 0:8], vrow[0:8, :], ident_f[0:8, 0:8]), r=["vrow", "ident_f"], w=["psS1"])
        S.add("dve", lambda e: e.tensor_copy(out=gcol[:], in_=psS[1][:, 0:8]), r=["psS1"], w=["gcol"])
        S.add("sp", lambda e: e.dma_start(out=wdwr[0:CW, :], in_=A["w_dw"]), w=["cst"], chan="cst")

        def trw(e):
            for j in range(8):
                r_ = e.transpose(psS[0][:, j * 32:j * 32 + CW], wdwr[0:CW, j * 128:(j + 1) * 128], ident_f[0:CW, 0:CW])
            return r_
        S.add("pe", trw, r=["cst", "ident_f", "vcol"], w=["psS0"])
        S.add("dve", lambda e: e.tensor_copy(out=wdwc[:, :, 0:CW], in_=psS[0][:, 0:256].rearrange("p (j k) -> p j k", k=32)[:, :, 0:CW]),
              r=["psS0"], w=["wdwc"])
        win_v = A["w_conv_in"].rearrange("(c p) e -> p c e", p=128)
        for pc in range(6):
            st = stg[pc % 2]
            S.add("sp", lambda e, st=st, pc=pc: e.dma_start(out=st, in_=win_v[:, :, pc * 512:(pc + 1) * 512]),
                  w=[f"xt{pc % 2}"], chan=f"xt{pc % 2}")
            S.add("dve", lambda e, st=st, pc=pc: e.tensor_tensor(out=W0in[:, :, pc * 512:(pc + 1) * 512], in0=st,
                                                               in1=gcol[:].unsqueeze(2).to_broadcast([128, 8, 512]), op=ALU.mult),
                  r=[f"xt{pc % 2}", "gcol"], w=["W0in"])
        wout_v = A["w_conv_out"].rearrange("(c p) e -> p c e", p=128)
        for pc in range(2):
            st = stg[pc % 2]
            S.add("sp", lambda e, st=st, pc=pc: e.dma_start(out=st, in_=wout_v[:, :, pc * 512:(pc + 1) * 512]),
                  w=[f"xt{pc % 2}"], chan=f"xt{pc % 2}")
            S.add("act", lambda e, st=st, pc=pc: e.activation(out=W0out[:, :, pc * 512:(pc + 1) * 512], in_=st, func=AF.Copy),
                  r=[f"xt{pc % 2}"], w=["W0out"])

        tiles = []
        for q in range(NSEQ):
            for i in range(T // 512):
                tiles.append(dict(kind="p", q=q, i=i, tok0=q * T + i * 512, PT=128, ns=4, G=1, n=512,
                                  first=(i == 0), last=(i == T // 512 - 1)))
        tiles.append(dict(kind="s", tok0=NP, PT=NS, ns=1, G=NSTR, n=TS, first=True, last=True))
        vkeys = lambda sl: [f"vext{sl}_{j}" for j in range(8)]

        def load_x(ti):
            t = tiles[ti]
            sl = ti % 2
            PT, ns = t["PT"], t["ns"]
            if t["kind"] == "p":
                src = A["xp"][t["tok0"]:t["tok0"] + 512, :].rearrange("(s p) d -> p s d", p=128)
            else:
                src = A["xs"][:, :].rearrange("(s p) d -> p s d", p=PT)
            S.add("sp", lambda e: e.dma_start(out=xt[sl][0:PT, 0:ns, :], in_=src), w=[f"xt{sl}"], chan=f"xt{sl}")

        def do_tile(ti, t):
            sl = ti % 2
            X = xt[sl]
            PT, ns, G, n = t["PT"], t["ns"], t["G"], t["n"]
            NT = PT * ns
            seg = CS + n
            if ti + 1 < len(tiles):
                load_x(ti + 1)
            for s in range(ns):
                S.add("act", lambda e, s=s: e.activation(out=junk[0:PT, :], in_=X[0:PT, s, :], func=AF.Square,
                                                         accum_out=ssq[0:PT, s:s + 1]),
                      r=[f"xt{sl}"], w=["junk", "ssq"])
            S.add("act", lambda e: e.activation(out=rstd[0:PT, 0:ns], in_=ssq[0:PT, 0:ns], func=AF.Sqrt, scale=1.0 / D, bias=EPS),
                  r=["ssq"], w=["rstd"])
            S.add("dve", lambda e: e.reciprocal(out=rstd[0:PT, 0:ns], in_=rstd[0:PT, 0:ns]), r=["rstd"], w=["rstd"])
            for s in range(ns):
                S.add("act", lambda e, s=s: e.activation(out=hb[0:PT, s, :], in_=X[0:PT, s, :], func=AF.Copy, scale=rstd[0:PT, s:s + 1]),
                      r=[f"xt{sl}", "rstd"], w=[f"hb{s}"])

                def tr(e, s=s):
                    for cc in range(8):
                        r_ = e.transpose(psT[:, cc, 0:PT], hb[0:PT, s, cc * 128:(cc + 1) * 128], ident_b[0:PT, 0:PT])
                    return r_
                S.add("pe", tr, r=[f"hb{s}", "ident_b"], w=["psT"])
                S.add("dve", lambda e, s=s: e.tensor_copy(out=hT[:, :, s * PT:(s + 1) * PT], in_=psT[:, :, 0:PT]), r=["psT"], w=["hT"])
            if t["kind"] == "p":
                if t["first"]:
                    S.add("pool", lambda e: e.memset(vext[sl][:, :, 0:CS], 0.0), w=vkeys(sl))
                else:
                    S.add("pool", lambda e: e.tensor_copy(out=vext[sl][:, :, 0:CS], in_=vext[1 - sl][:, :, 512:512 + CS]),
                          r=vkeys(1 - sl), w=vkeys(sl))
            else:
                for g in range(G):
                    S.add("sp", lambda e, g=g: e.dma_start(out=hist[0:CS, :], in_=A["sconv"][g]), w=["cst"], chan="cst")

                    def trh(e):
                        for j in range(8):
                            r_ = e.transpose(psS[0][:, j * 32:j * 32 + CS], hist[0:CS, j * 128:(j + 1) * 128], ident_f[0:CS, 0:CS])
                        return r_
                    S.add("pe", trh, r=["cst", "ident_f"], w=["psS0"])
                    S.add("dve", lambda e, g=g: e.tensor_copy(out=vext[sl][:, :, g * seg:g * seg + CS],
                                                              in_=psS[0][:, 0:256].rearrange("p (j k) -> p j k", k=32)[:, :, 0:CS]),
                          r=["psS0"], w=vkeys(sl))
            for j in range(8):
                for ps, off, key in ((psG, D, "psG"), (psA, 0, "psA"), (psZ, 2 * D, "psZ")):
                    def mm(e, ps=ps, col=off + j * 128):
                        for cc in range(8):
                            r_ = e.matmul(ps[:, 0:NT], lhsT=W0in[:, cc, col:col + 128], rhs=hT[:, cc, 0:NT], start=(cc == 0), stop=(cc == 7))
                        return r_
                    S.add("pe", mm, r=["hT", "W0in"], w=[key])
                sg = sig[j % 2]
                S.add("act", lambda e, sg=sg: e.activation(out=sg[:, 0:NT], in_=psG[:, 0:NT], func=AF.Sigmoid), r=["psG"], w=[f"sig{j % 2}"])
                vdst = vext[sl][:, j, 0:G * seg].rearrange("p (g w) -> p g w", w=seg)[:, :, CS:seg]
                S.add("dve", lambda e, sg=sg, vdst=vdst: e.tensor_tensor(out=vdst, in0=psA[:, 0:NT].rearrange("p (g w) -> p g w", w=n),
                                                                         in1=sg[:, 0:NT].rearrange("p (g w) -> p g w", w=n), op=ALU.mult),
                      r=["psA", f"sig{j % 2}"], w=[f"vext{sl}_{j}"])
                if t["last"]:
                    lo, cnt = (NT - CS, CS) if t["kind"] == "p" else (0, NT)
                    S.add("dve", lambda e, sg=sg, j=j, lo=lo, cnt=cnt: e.tensor_tensor(out=vt[:, j, 0:cnt], in0=psA[:, lo:lo + cnt],
                                                                                     in1=sg[:, lo:lo + cnt], op=ALU.mult),
                          r=["psA", f"sig{j % 2}"], w=["vt"])
                S.add("act", lambda e, j=j: e.activation(out=szT[:, j, 0:NT], in_=psZ[:, 0:NT], func=AF.Silu), r=["psZ"], w=[f"sz{j}"])
            if t["last"]:
                cnt = CS if t["kind"] == "p" else NT

                def trv(e, cnt=cnt):
                    for j in range(8):
                        r_ = e.transpose(psS[j // 4][0:cnt, (j % 4) * 128:(j % 4 + 1) * 128], vt[:, j, 0:cnt], ident_f[:])
                    return r_
                S.add("pe", trv, r=["vt", "ident_f"], w=["psS0", "psS1"])
                for hf in range(2):
                    S.add("act", lambda e, hf=hf, cnt=cnt: e.activation(out=cst[0:cnt, hf * 512:(hf + 1) * 512], in_=psS[hf][0:cnt, :], func=AF.Copy),
                          r=[f"psS{hf}"], w=["cst"])
                if t["kind"] == "p":
                    S.add("pool", lambda e, q=t["q"]: e.dma_start(out=A["convp"][q], in_=cst[0:CS, :]), r=["cst"], chan="cst")
                else:
                    for g in range(G):
                        S.add("pool", lambda e, g=g: e.dma_start(out=A["convs"][g, CS - TS:CS, :], in_=cst[g * TS:(g + 1) * TS, :]), r=["cst"], chan="cst")
                    S.add("pool", lambda e: e.dma_start(out=A["convs"][:, 0:CS - TS, :], in_=A["sconv"][:, TS:CS, :]), chan="d2d")
            def gen_diag(j):
                dj = dg[j % 2]
                S.add("dve", lambda e: e.tensor_tensor(out=dj[:], in0=ident_b[:].unsqueeze(1).to_broadcast([128, CW, 128]),
                                                       in1=wdwc[:, j, 0:CW].unsqueeze(2).to_broadcast([128, CW, 128]), op=ALU.mult),
                      r=["wdwc", "ident_b"], w=[f"dg{j % 2}"])
            gen_diag(0)
            for j in range(8):
                dj = dg[j % 2]

                def cv(e, dj=dj, j=j):
                    for g in range(G):
                        for k in range(CW):
                            r_ = e.matmul(psY[j % 2][:, g * n:(g + 1) * n], lhsT=dj[:, k, :], rhs=vext[sl][:, j, g * seg + k:g * seg + k + n],
                                          start=(k == 0), stop=(k == CW - 1))
                    return r_
                S.add("pe", cv, r=[f"dg{j % 2}", f"vext{sl}_{j}"], w=[f"psY{j % 2}"])
                if j + 1 < 8:
                    gen_diag(j + 1)
                S.add("act", lambda e, j=j: e.activation(out=ybf[:, j, 0:NT], in_=psY[j % 2][:, 0:NT], func=AF.Identity, bias=bdw[:, j:j + 1]),
                      r=[f"psY{j % 2}", "vcol"], w=[f"ybf{j}"])
                S.add("dve", lambda e, j=j: e.tensor_tensor(out=ysq[j % 2][:, 0:NT], in0=ybf[:, j, 0:NT], in1=ybf[:, j, 0:NT], op=ALU.mult),
                      r=[f"ybf{j}"], w=[f"ysq{j % 2}"])
                S.add("pe", lambda e, j=j: e.matmul(psS[0][:, 0:NT], lhsT=ones_b[:], rhs=ybf[:, j, 0:NT], start=(j == 0), stop=(j == 7)),
                      r=[f"ybf{j}", "ones_b"], w=["psS0"])
                S.add("pe", lambda e, j=j: e.matmul(psS[1][:, 0:NT], lhsT=ones_b[:], rhs=ysq[j % 2][:, 0:NT], start=(j == 0), stop=(j == 7)),
                      r=[f"ysq{j % 2}", "ones_b"], w=["psS1"])
            S.add("act", lambda e: e.activation(out=mean[:, 0:NT], in_=psS[0][:, 0:NT], func=AF.Copy, scale=1.0 / D), r=["psS0"], w=["mean"])
            S.add("dve", lambda e: e.tensor_tensor(out=tmp[0][:, 0:NT], in0=mean[:, 0:NT], in1=mean[:, 0:NT], op=ALU.mult), r=["mean"], w=["tmp0"])
            S.add("dve", lambda e: e.scalar_tensor_tensor(out=rs[:, 0:NT], in0=psS[1][:, 0:NT], scalar=1.0 / D, in1=tmp[0][:, 0:NT],
                                                          op0=ALU.mult, op1=ALU.subtract), r=["psS1", "tmp0"], w=["rs"])
            S.add("act", lambda e: e.activation(out=rs[:, 0:NT], in_=rs[:, 0:NT], func=AF.Sqrt, bias=EPS), r=["rs"], w=["rs"])
            S.add("dve", lambda e: e.reciprocal(out=rs[:, 0:NT], in_=rs[:, 0:NT]), r=["rs"], w=["rs"])
            for j in range(8):
                tj = tmp[j % 2]
                S.add("dve", lambda e, j=j, tj=tj: e.tensor_tensor(out=tj[:, 0:NT], in0=ybf[:, j, 0:NT], in1=mean[:, 0:NT], op=ALU.subtract),
                      r=[f"ybf{j}", "mean"], w=[f"tmp{j % 2}"])
                S.add("pool", lambda e, tj=tj: e.tensor_tensor(out=tj[:, 0:NT], in0=tj[:, 0:NT], in1=rs[:, 0:NT], op=ALU.mult),
                      r=[f"tmp{j % 2}", "rs"], w=[f"tmp{j % 2}"])
                S.add("act", lambda e, j=j, tj=tj: e.activation(out=tj[:, 0:NT], in_=tj[:, 0:NT], func=AF.Silu, scale=lng[:, j:j + 1], bias=lnb[:, j:j + 1]),
                      r=[f"tmp{j % 2}", "vcol"], w=[f"tmp{j % 2}"])
                S.add("dve", lambda e, j=j, tj=tj: e.tensor_tensor(out=mT[:, j, 0:NT], in0=tj[:, 0:NT], in1=szT[:, j, 0:NT], op=ALU.mult),
                      r=[f"tmp{j % 2}", f"sz{j}"], w=["mT"])
            for s in range(ns):
                for hf in range(2):
                    def op(e, s=s, hf=hf):
                        for j in range(8):
                            r_ = e.matmul(psS[hf][0:PT, :], lhsT=mT[:, j, s * PT:(s + 1) * PT], rhs=W0out[:, j, hf * 512:(hf + 1) * 512],
                                          start=(j == 0), stop=(j == 7))
                        return r_
                    S.add("pe", op, r=["mT", "W0out"], w=[f"psS{hf}"])
                    S.add("dve", lambda e, s=s, hf=hf: e.tensor_tensor(out=xo[s % 2][0:PT, hf * 512:(hf + 1) * 512], in0=psS[hf][0:PT, :],
                                                                     in1=X[0:PT, s, hf * 512:(hf + 1) * 512], op=ALU.add),
                          r=[f"psS{hf}", f"xt{sl}"], w=[f"xo{s % 2}"])
                r0 = t["tok0"] + s * PT
                S.add("pool", lambda e, s=s, r0=r0: e.dma_start(out=A["x1"][r0:r0 + PT, :], in_=xo[s % 2][0:PT, :]), r=[f"xo{s % 2}"], chan=f"xo{s % 2}")
        load_x(0)
        for ti_, t_ in enumerate(tiles):
            do_tile(ti_, t_)
        S.run_phase()


def make_in_maps(cfg, inputs):
    c = cfg
    f = lambda a: np.ascontiguousarray(np.asarray(a, dtype=np.float32))
    ident = np.eye(128, dtype=np.float32)
    tri = np.triu(np.ones((128, 128), dtype=np.float32))
    n_cores = inputs["x_prompt"].shape[0] // c.NSEQ
    maps = []
    for i in range(n_cores):
        sq = slice(i * c.NSEQ, (i + 1) * c.NSEQ)
        st = slice(i * c.NSTR, (i + 1) * c.NSTR)
        m = {
            "xp": f(inputs["x_prompt"][sq]).reshape(c.NP, D),
            "xs": f(inputs["x_sample"][st]).reshape(c.NS, D),
            "sconv": f(inputs["state_conv"][0, st]),
            "ck": f(inputs["cache_k"][0, st]).reshape(c.NSTR, c.PAST, D),
            "cv": f(inputs["cache_v"][0, st]).reshape(c.NSTR, c.PAST, D),
            "clf": f(inputs["cache_logf"][0, st]),
            "norm_g": f(inputs["norm_g"]), "fng": f(inputs["final_norm_g"]),
            "w_conv_in": f(inputs["w_conv_in"][0]), "w_dw": f(inputs["w_dw"][0]), "b_dw": f(inputs["b_dw"][0]),
            "lng": f(inputs["conv_ln_g"][0]), "lnb": f(inputs["conv_ln_b"][0]), "w_conv_out": f(inputs["w_conv_out"][0]),
            "w_fox_in": f(inputs["w_fox_in"][0]), "b_f": f(inputs["b_forget"][0]), "qg": f(inputs["q_norm_g"][0]),
            "kg": f(inputs["k_norm_g"][0]), "w_fox_out": f(inputs["w_fox_out"][0]),
            "ident": ident, "tri": tri,
        }
        maps.append(m)
    return maps


def seq_list(c):
    L = []
    for q in range(c.NSEQ):
        L.append(dict(kind="p", idx=q, nfull=c.T // 128, part=0, nqt=c.T // 512, key=f"p{q}"))
    for g in range(c.NSTR):
        L.append(dict(kind="s", idx=g, nfull=c.PAST // 128, part=TS, nqt=1, key=f"s{g}"))
    return L


def alloc_persist(nc, ges, c, K):
    gsb = lambda name, shape, dt=F32: ges.enter_context(nc.sbuf_tensor("l0_" + name, list(shape), dt))
    for sq in seq_list(c):
        nb = sq["nfull"] + (1 if sq["part"] else 0)
        K["C_" + sq["key"]] = gsb("C_" + sq["key"], [128, nb, H])
        K["B_" + sq["key"]] = gsb("B_" + sq["key"], [128, sq["nqt"], H])


def phase_l1a(S, nc, c, A, K):
    import os
    S.nadd = 0
    S.limit = int(os.environ['L1A_LIMIT']) if 'L1A_LIMIT' in os.environ else None
    T, NSEQ, NSTR, NP, NS, PAST = c.T, c.NSEQ, c.NSTR, c.NP, c.NS, c.PAST
    ident_f, ident_b, ones_f, tri_f = K["ident_f"], K["ident_b"], K["ones_f"], K["tri_f"]
    NBMAX = max(T // 128, PAST // 128 + 1)
    with ExitStack() as es:
        sb = lambda name, shape, dt=F32: es.enter_context(nc.sbuf_tensor("l1a_" + name, list(shape), dt))
        pm = lambda name, shape, dt=F32: es.enter_context(nc.psum_tensor("l1a_" + name, list(shape), dt))
        NW = 4 * D + H
        W1in = sb("W1in", [128, 8, NW], BF16)
        xt = [sb(f"xt{i}", [128, 4, D]) for i in range(2)]
        stg = [xt[i][:, :, :].rearrange("p s (a b) -> p (s a) b", b=512) for i in range(2)]
        vrow = sb("vrow", [8, 128]); gcol = sb("gcol", [128, 8])
        bfb = sb("bfb", [128, H]); gqb = sb("gqb", [128, DH]); gkb = sb("gkb", [128, DH])
        hbs = [sb(f"hb{i}", [128, D], BF16) for i in range(2)]; hT = sb("hT", [128, 8, 512], BF16)
        ssq = sb("ssq", [128, 4]); rstd = sb("rstd", [128, 4])
        sq = sb("sq", [128, D]); ssh = sb("ssh", [128, 2 * H])
        kf = [sb(f"kf{i}", [128, D]) for i in range(2)]
        vf = [sb(f"vf{i}", [128, D]) for i in range(2)]
        qbs = [sb(f"qb{i}", [128, D], BF16) for i in range(2)]; kbs = [sb(f"kb{i}", [128, D], BF16) for i in range(2)]
        kb = kbs[0]
        vb = [sb(f"vb{i}", [128, D], BF16) for i in range(2)]
        szb = [sb(f"szb{i}", [128, D], BF16) for i in range(2)]
        qTs = sb("qTs", [128, 8, 512], BF16); kTs = sb("kTs", [128, 8, 512], BF16)
        lft = sb("lft", [128, 4 * H])
        lf = sb("lf", [128, NBMAX, H]); psa = sb("psa", [128, NBMAX + 1, H])
        clt = sb("clt", [128, NBMAX, H]); clT = sb("clT", [16, 512]); r1 = sq[0:16, 0:512]
        aug = sb("aug", [16, 3, 512], BF16)
        psT = pm("psT", [128, 8, 128], BF16)
        psP = [pm(f"psP{i}", [128, 512]) for i in range(4)]
        psQ = [pm(f"psQ{i}", [128, 4, 128], BF16) for i in range(2)]
        psF = pm("psF", [128, 512])

        S.add("sp", lambda e: e.dma_start(out=vrow[0:8, :], in_=A["norm_g"][1].rearrange("(j p) -> j p", p=128)), w=["vrow"], chan="vrow")
        S.add("pe", lambda e: e.transpose(psF[:, 0:8], vrow[0:8, :], ident_f[0:8, 0:8]), r=["vrow"], w=["psF"])
        S.add("dve", lambda e: e.tensor_copy(out=gcol[:], in_=psF[:, 0:8]), r=["psF"], w=["gcol"])
        S.add("sp", lambda e: e.dma_start(out=bfb[:], in_=A["b_f"].partition_broadcast(128)), w=["bfb"], chan="bfb")
        S.add("sp", lambda e: e.dma_start(out=gqb[:], in_=A["qg"].partition_broadcast(128)), w=["gqb"], chan="gqb")
        S.add("sp", lambda e: e.dma_start(out=gkb[:], in_=A["kg"].partition_broadcast(128)), w=["gkb"], chan="gkb")
        S.add("dve", lambda e: e.tensor_scalar_mul(out=gqb[:], in0=gqb[:], scalar1=0.125), r=["gqb"], w=["gqb"])
        win_v = A["w_fox_in"].rearrange("(c p) e -> p c e", p=128)
        for pc in range(9):
            st = stg[pc % 2]
            wd = 512 if pc < 8 else H
            S.add("sp", lambda e, st=st, pc=pc, wd=wd: e.dma_start(out=st[:, :, 0:wd], in_=win_v[:, :, pc * 512:pc * 512 + wd]),
                  w=[f"xt{pc % 2}"], chan=f"xt{pc % 2}")
            S.add("dve", lambda e, st=st, pc=pc, wd=wd: e.tensor_tensor(out=W1in[:, :, pc * 512:pc * 512 + wd], in0=st[:, :, 0:wd],
                                                                      in1=gcol[:].unsqueeze(2).to_broadcast([128, 8, wd]), op=ALU.mult),
                  r=[f"xt{pc % 2}", "gcol"], w=["W1in"])

        tiles = []
        for q in range(NSEQ):
            for i in range(T // 512):
                tiles.append(dict(kind="p", q=q, i=i, tok0=q * T + i * 512, PT=128, ns=4, last=(i == T // 512 - 1)))
        tiles.append(dict(kind="s", tok0=NP, PT=NS, ns=1, last=True))
        for g in range(NSTR):
            for i in range(PAST // 512):
                tiles.append(dict(kind="c", g=g, i=i, PT=128, ns=4, last=False))
        cnt = {"st": 0}

        def load_x(ti):
            t = tiles[ti]
            sl = ti % 2
            PT, ns = t["PT"], t["ns"]
            if t["kind"] == "c":
                src = A["ck"][t["g"], t["i"] * 512:(t["i"] + 1) * 512, :].rearrange("(s p) d -> p s d", p=128)
            else:
                src = A["x1"][t["tok0"]:t["tok0"] + PT * ns, :].rearrange("(s p) d -> p s d", p=PT)
            S.add("sp", lambda e: e.dma_start(out=xt[sl][0:PT, 0:ns, :], in_=src), w=[f"xt{sl}"], chan=f"xt{sl}")

        def transposes_to(src_b, PT, dstT, s, rkey, dkey):
            for half in range(2):
                def tr(e, half=half):
                    for pp in range(4):
                        p = half * 4 + pp
                        r_ = e.transpose(psQ[half][:, pp, 0:PT], src_b[0:PT, p * 128:(p + 1) * 128], ident_b[0:PT, 0:PT])
                    return r_
                S.add("pe", tr, r=[rkey, "ident_b"], w=[f"psQ{half}"])
                S.add("act", lambda e, half=half: e.activation(out=dstT[:, half * 4:(half + 1) * 4, s * PT:(s + 1) * PT],
                                                               in_=psQ[half][:, :, 0:PT], func=AF.Copy),
                      r=[f"psQ{half}"], w=[dkey])

        def cumsum_seq(sqd):
            nfull, part, key = sqd["nfull"], sqd["part"], sqd["key"]
            nb = nfull + (1 if part else 0)
            Call, Ball = K["C_" + key], K["B_" + key]
            S.add("dve", lambda e: e.memset(psa[:, 0, :], 0.0), w=["psa"])
            for b in range(nfull):
                S.add("dve", lambda e, b=b: e.tensor_tensor(out=psa[:, b + 1, :], in0=psa[:, b, :], in1=lf[:, b, :], op=ALU.add),
                      r=["lf", "psa"], w=["psa"])

            for c0 in range(0, nb, 32):
                c1 = min(nb, c0 + 32)

                def cs(e, c0=c0, c1=c1):
                    for b in range(c0, c1):
                        rows = 128 if b < nfull else part
                        o = psF[0:rows, (b - c0) * H:(b - c0 + 1) * H]
                        e.matmul(o, lhsT=tri_f[0:rows, 0:rows], rhs=lf[0:rows, b, :], start=True, stop=False)
                        r_ = e.matmul(o, lhsT=ones_f[:, 0:rows], rhs=psa[:, b, :], start=False, stop=True)
                    return r_
                S.add("pe", cs, r=["lf", "psa", "tri_f", "ones_f"], w=["psF"])
                f1 = min(c1, nfull)
                if f1 > c0:
                    S.add("dve", lambda e, c0=c0, f1=f1: e.tensor_copy(out=Call[:, c0:f1, :], in_=psF[:, 0:(f1 - c0) * H].rearrange("p (b h) -> p b h", h=H)),
                          r=["psF"], w=["C_" + key])
                if c1 > nfull:
                    S.add("dve", lambda e, c0=c0: e.tensor_copy(out=Call[0:part, nfull, :], in_=psF[0:part, (nfull - c0) * H:(nfull - c0 + 1) * H]),
                          r=["psF"], w=["C_" + key])
            nqt = sqd["nqt"]
            bidx = [4 * i for i in range(nqt)] if sqd["kind"] == "p" else [nfull]

            def bs(e):
                for i, bi in enumerate(bidx):
                    r_ = e.matmul(psF[:, i * H:(i + 1) * H], lhsT=ones_f[:], rhs=psa[:, bi, :], start=True, stop=True)
                return r_
            S.add("pe", bs, r=["psa", "ones_f", "C_" + key], w=["psF"])
            S.add("dve", lambda e: e.tensor_copy(out=Ball[:, :, :], in_=psF[:, 0:nqt * H].rearrange("p (b h) -> p b h", h=H)),
                  r=["psF"], w=["B_" + key])
            if sqd["kind"] == "p":
                S.add("dve", lambda e: e.tensor_tensor(out=clt[:, 0:nb, :].rearrange("p (q s) h -> p q s h", s=4),
                                                       in0=Call[:, 0:nb, :].rearrange("p (q s) h -> p q s h", s=4),
                                                       in1=Ball[:, :, :].unsqueeze(2).to_broadcast([128, nqt, 4, H]), op=ALU.subtract),
                      r=["C_" + key, "B_" + key], w=["clt"])
                qts = [(4 * i, 4, 128) for i in range(nqt)]
            else:
                S.add("dve", lambda e: e.tensor_tensor(out=clt[0:part, nfull, :], in0=Call[0:part, nfull, :], in1=Ball[0:part, 0, :], op=ALU.subtract),
                      r=["C_" + key, "B_" + key], w=["clt"])
                qts = [(nfull, 1, part)]
            for qi, (b0, nbq, rows) in enumerate(qts):
                wq = nbq * rows

                def trc(e, b0=b0, nbq=nbq, rows=rows):
                    for bb in range(nbq):
                        r_ = e.transpose(psF[0:16, bb * rows:(bb + 1) * rows], clt[0:rows, b0 + bb, :], ident_f[0:rows, 0:rows])
                    return r_
                S.add("pe", trc, r=["clt", "ident_f", "B_" + key], w=["psF"])
                S.add("act", lambda e, wq=wq: e.activation(out=clT[:, 0:wq], in_=psF[0:16, 0:wq], func=AF.Copy), r=["psF"], w=["clT"])
                S.add("dve", lambda e, wq=wq: e.tensor_copy(out=aug[:, 0, 0:wq], in_=clT[:, 0:wq]), r=["clT"], w=["aug"])
                S.add("dve", lambda e, wq=wq: e.tensor_tensor(out=r1[:, 0:wq], in0=clT[:, 0:wq], in1=aug[:, 0, 0:wq], op=ALU.subtract), r=["clT", "aug"], w=["sq"])
                S.add("dve", lambda e, wq=wq: e.tensor_copy(out=aug[:, 1, 0:wq], in_=r1[:, 0:wq]), r=["sq"], w=["aug"])
                S.add("dve", lambda e, wq=wq: e.tensor_tensor(out=r1[:, 0:wq], in0=r1[:, 0:wq], in1=aug[:, 1, 0:wq], op=ALU.subtract), r=["sq", "aug"], w=["sq"])
                S.add("dve", lambda e, wq=wq: e.tensor_copy(out=aug[:, 2, 0:wq], in_=r1[:, 0:wq]), r=["sq"], w=["aug"])
                if sqd["kind"] == "p":
                    dst = A["augp"][sqd["idx"]][:, :, qi * 512:(qi + 1) * 512]
                else:
                    dst = A["augs"][sqd["idx"]][:, :, :]
                S.add("pool", lambda e, dst=dst, wq=wq: e.dma_start(out=dst, in_=aug[:, :, 0:wq]), r=["aug"], chan="aug")

        def do_tile(ti, t):
            sl = ti % 2
            X = xt[sl]
            PT, ns = t["PT"], t["ns"]
            NT = PT * ns
            kind = t["kind"]
            if ti + 1 < len(tiles):
                load_x(ti + 1)
            if kind == "c":
                g, i = t["g"], t["i"]
                for s in range(ns):
                    S.add("act", lambda e, s=s: e.activation(out=kb[:, :], in_=X[:, s, :], func=AF.Copy), r=[f"xt{sl}"], w=["kb0"])
                    transposes_to(kb, 128, kTs, s, "kb0", "kTs")
                S.add("pool", lambda e: e.dma_start(out=A["kts"][g].rearrange("(p two) d t -> (two d) p t", two=2)[:, :, i * 512:(i + 1) * 512], in_=kTs[:, :, :]),
                      r=["kTs"], chan="kTs")
                for s in range(ns):
                    k2 = cnt["st"] % 2
                    cnt["st"] += 1
                    r0 = i * 512 + s * 128
                    S.add("sp", lambda e, k2=k2, r0=r0: e.dma_start(out=vf[k2][:, :], in_=A["cv"][g, r0:r0 + 128, :]), w=[f"vf{k2}"], chan=f"vf{k2}")
                    S.add("dve", lambda e, k2=k2: e.tensor_copy(out=vb[k2][:, :], in_=vf[k2][:, :]), r=[f"vf{k2}"], w=[f"vb{k2}"])
                    S.add("pool", lambda e, k2=k2, r0=r0: e.dma_start(out=A["vbs"][g][r0:r0 + 128, :], in_=vb[k2][:, :]),
                          r=[f"vb{k2}"], chan=f"vb{k2}")
                return
            for s in range(ns):
                S.add("act", lambda e, s=s: e.activation(out=sq[0:PT, :], in_=X[0:PT, s, :], func=AF.Square, accum_out=ssq[0:PT, s:s + 1]),
                      r=[f"xt{sl}"], w=["sq", "ssq"])
            S.add("act", lambda e: e.activation(out=rstd[0:PT, 0:ns], in_=ssq[0:PT, 0:ns], func=AF.Sqrt, scale=1.0 / D, bias=EPS), r=["ssq"], w=["rstd"])
            S.add("dve", lambda e: e.reciprocal(out=rstd[0:PT, 0:ns], in_=rstd[0:PT, 0:ns]), r=["rstd"], w=["rstd"])
            for s in range(ns):
                S.add("act", lambda e, s=s: e.activation(out=hbs[s % 2][0:PT, :], in_=X[0:PT, s, :], func=AF.Copy, scale=rstd[0:PT, s:s + 1]),
                      r=[f"xt{sl}", "rstd"], w=[f"hb{s % 2}"])

                def tr(e, s=s):
                    for cc in range(8):
                        r_ = e.transpose(psT[:, cc, 0:PT], hbs[s % 2][0:PT, cc * 128:(cc + 1) * 128], ident_b[0:PT, 0:PT])
                    return r_
                S.add("pe", tr, r=[f"hb{s % 2}", "ident_b"], w=["psT"])
                S.add("dve", lambda e, s=s: e.tensor_copy(out=hT[:, :, s * PT:(s + 1) * PT], in_=psT[:, :, 0:PT]), r=["psT"], w=["hT"])
            def make_sub(s):
                k2 = cnt["st"] % 2
                cnt["st"] += 1
                r0 = t["tok0"] + s * PT

                def proj(ps, col, wd, s=s):
                    def mm(e):
                        for cc in range(8):
                            r_ = e.matmul(ps[0:PT, 0:wd], lhsT=hT[:, cc, s * PT:(s + 1) * PT], rhs=W1in[:, cc, col:col + wd], start=(cc == 0), stop=(cc == 7))
                        return r_
                    return mm
                qb = qbs[s % 2]
                kb = kbs[s % 2]
                qbk, kbk = f"qb{s % 2}", f"kb{s % 2}"
                def qk_pe(which):
                    for hf in range(2):
                        S.add("pe", proj(psP[hf], which * D + hf * 512, 512), r=["hT", "W1in"], w=[f"psP{hf}"])

                def qk_sq(which):
                    for hf in range(2):
                        S.add("act", lambda e, hf=hf: e.activation(out=sq[0:PT, hf * 512:(hf + 1) * 512], in_=psP[hf][0:PT, :], func=AF.Square),
                              r=[f"psP{hf}"], w=["sq"])
                    S.add("dve", lambda e: e.tensor_reduce(out=ssh[0:PT, 0:H], in_=sq[0:PT, :].rearrange("p (h d) -> p h d", d=DH), axis=AX.X, op=ALU.add),
                          r=["sq"], w=["ssh"])

                def qk_sqrt(which):
                    S.add("act", lambda e: e.activation(out=ssh[0:PT, H:2 * H], in_=ssh[0:PT, 0:H], func=AF.Sqrt, scale=1.0 / DH, bias=EPS), r=["ssh"], w=["ssh"])
                    S.add("dve", lambda e: e.reciprocal(out=ssh[0:PT, H:2 * H], in_=ssh[0:PT, H:2 * H]), r=["ssh"], w=["ssh"])

                def qk_mul(which):
                    for hf in range(2):
                        S.add("dve", lambda e, hf=hf: e.tensor_tensor(out=sq[0:PT, hf * 512:(hf + 1) * 512].rearrange("p (h d) -> p h d", d=DH),
                                                                      in0=psP[hf][0:PT, :].rearrange("p (h d) -> p h d", d=DH),
                                                                      in1=ssh[0:PT, H + hf * 8:H + hf * 8 + 8].unsqueeze(2).to_broadcast([PT, 8, DH]), op=ALU.mult),
                              r=[f"psP{hf}", "ssh"], w=["sq"])
                    if which == 0:
                        S.add("dve", lambda e: e.tensor_tensor(out=qb[0:PT, :].rearrange("p (h d) -> p h d", d=DH), in0=sq[0:PT, :].rearrange("p (h d) -> p h d", d=DH),
                                                               in1=gqb[0:PT, :].unsqueeze(1).to_broadcast([PT, H, DH]), op=ALU.mult),
                              r=["sq", "gqb"], w=[qbk])
                    else:
                        S.add("dve", lambda e: e.tensor_tensor(out=kf[k2][0:PT, :].rearrange("p (h d) -> p h d", d=DH), in0=sq[0:PT, :].rearrange("p (h d) -> p h d", d=DH),
                                                               in1=gkb[0:PT, :].unsqueeze(1).to_broadcast([PT, H, DH]), op=ALU.mult),
                              r=["sq", "gkb"], w=[f"kf{k2}"])
                        kdst = A["kp"][r0:r0 + PT, :] if kind == "p" else A["ks"][0:PT, :]
                        S.add("sp", lambda e: e.dma_start(out=kdst, in_=kf[k2][0:PT, :]), r=[f"kf{k2}"], chan=f"kf{k2}")
                        S.add("act", lambda e: e.activation(out=kb[0:PT, :], in_=kf[k2][0:PT, :], func=AF.Copy), r=[f"kf{k2}"], w=[kbk])

                def stA():
                    qk_pe(0)
                    for hf in range(2):
                        S.add("pe", proj(psP[2 + hf], 2 * D + hf * 512, 512), r=["hT", "W1in"], w=[f"psP{2 + hf}"])
                    qk_sq(0)
                    for hf in range(2):
                        S.add("act", lambda e, hf=hf: e.activation(out=vf[k2][0:PT, hf * 512:(hf + 1) * 512], in_=psP[2 + hf][0:PT, :], func=AF.Copy),
                              r=[f"psP{2 + hf}"], w=[f"vf{k2}"])
                    qk_sqrt(0)
                    for hf in range(2):
                        S.add("dve", lambda e, hf=hf: e.tensor_copy(out=vb[k2][0:PT, hf * 512:(hf + 1) * 512], in_=vf[k2][0:PT, hf * 512:(hf + 1) * 512]),
                              r=[f"vf{k2}"], w=[f"vb{k2}"])
                    qk_mul(0)
                    vdst = A["vp"][r0:r0 + PT, :] if kind == "p" else A["vs"][0:PT, :]
                    S.add("sp", lambda e: e.dma_start(out=vdst, in_=vf[k2][0:PT, :]), r=[f"vf{k2}"], chan=f"vf{k2}")
                    if kind == "p":
                        tq = t["i"] * 512 + s * 128
                        S.add("sp", lambda e: e.dma_start(out=A["vbp"][t["q"]][tq:tq + 128, :], in_=vb[k2][:, :]), r=[f"vb{k2}"], chan=f"vb{k2}")
                    else:
                        for g in range(NSTR):
                            S.add("sp", lambda e, g=g: e.dma_start(out=A["vbs"][g][PAST:PAST + TS, :], in_=vb[k2][g * TS:(g + 1) * TS, :]),
                                  r=[f"vb{k2}"], chan=f"vb{k2}")

                def stB():
                    qk_pe(1)
                    for hf in range(2):
                        S.add("pe", proj(psP[2 + hf], 3 * D + hf * 512, 512), r=["hT", "W1in"], w=[f"psP{2 + hf}"])
                    S.add("pe", proj(psF, 4 * D, H), r=["hT", "W1in"], w=["psF"])
                    qk_sq(1)
                    S.add("dve", lambda e: e.tensor_tensor(out=lft[0:PT, 0:H], in0=psF[0:PT, 0:H], in1=bfb[0:PT, :], op=ALU.add), r=["psF", "bfb"], w=["lft"])
                    S.add("dve", lambda e: e.tensor_scalar_min(out=lft[0:PT, 2 * H:3 * H], in0=lft[0:PT, 0:H], scalar1=0.0), r=["lft"], w=["lft2"])
                    qk_sqrt(1)
                    S.add("act", lambda e: e.activation(out=lft[0:PT, H:2 * H], in_=lft[0:PT, 0:H], func=AF.Abs), r=["lft"], w=["lft1"])
                    qk_mul(1)
                    for hf in range(2):
                        S.add("act", lambda e, hf=hf: e.activation(out=szb[k2][0:PT, hf * 512:(hf + 1) * 512], in_=psP[2 + hf][0:PT, :], func=AF.Silu),
                              r=[f"psP{2 + hf}"], w=[f"szb{k2}"])
                    S.add("sp", lambda e: e.dma_start(out=A["szs"][r0:r0 + PT, :], in_=szb[k2][0:PT, :]), r=[f"szb{k2}"], chan=f"szb{k2}")
                    S.add("act", lambda e: e.activation(out=lft[0:PT, H:2 * H], in_=lft[0:PT, H:2 * H], func=AF.Exp, scale=-1.0), r=["lft1"], w=["lft1"])
                    S.add("act", lambda e: e.activation(out=lft[0:PT, H:2 * H], in_=lft[0:PT, H:2 * H], func=AF.Ln, bias=1.0), r=["lft1"], w=["lft1"])
                    if kind == "p":
                        blk = t["i"] * 4 + s
                        S.add("dve", lambda e: e.tensor_tensor(out=lf[:, blk, :], in0=lft[:, 2 * H:3 * H], in1=lft[:, H:2 * H], op=ALU.subtract),
                              r=["lft1", "lft2"], w=["lf"])
                        S.add("sp", lambda e: e.dma_start(out=A["lfp"][r0:r0 + 128, :], in_=lf[:, blk, :]), r=["lf"], chan="lf")
                    else:
                        S.add("dve", lambda e: e.tensor_tensor(out=lft[0:PT, 3 * H:4 * H], in0=lft[0:PT, 2 * H:3 * H], in1=lft[0:PT, H:2 * H], op=ALU.subtract),
                              r=["lft1", "lft2"], w=["lft3"])
                        S.add("sp", lambda e: e.dma_start(out=A["lfs"][0:PT, :], in_=lft[0:PT, 3 * H:4 * H]), r=["lft3"], w=["lfs_dram"], chan="lft")

                def stC():
                    transposes_to(qb, PT, qTs, s, qbk, "qTs")
                    transposes_to(kb, PT, kTs, s, kbk, "kTs")
                return stA, stB, stC

            stages = []
            for s in range(ns):
                stages.append(make_sub(s))
            stages[0][0]()
            stages[0][1]()
            for s in range(1, ns):
                stages[s][0]()
                stages[s - 1][2]()
                stages[s][1]()
            stages[ns - 1][2]()
            if kind == "p":
                q, i = t["q"], t["i"]
                S.add("pool", lambda e: e.dma_start(out=A["qtp"][q].rearrange("(p two) d t -> (two d) p t", two=2)[:, :, i * 512:(i + 1) * 512], in_=qTs[:, :, :]), r=["qTs"], chan="qTs")
                S.add("pool", lambda e: e.dma_start(out=A["ktp"][q].rearrange("(p two) d t -> (two d) p t", two=2)[:, :, i * 512:(i + 1) * 512], in_=kTs[:, :, :]), r=["kTs"], chan="kTs")
                if t["last"]:
                    cumsum_seq([x for x in seq_list(c) if x["kind"] == "p" and x["idx"] == q][0])
            else:
                for g in range(NSTR):
                    S.add("pool", lambda e, g=g: e.dma_start(out=A["qts"][g].rearrange("(p two) d t -> (two d) p t", two=2)[:, :, :], in_=qTs[:, :, g * TS:(g + 1) * TS]), r=["qTs"], chan="qTs")
                    S.add("pool", lambda e, g=g: e.dma_start(out=A["kts"][g].rearrange("(p two) d t -> (two d) p t", two=2)[:, :, PAST:PAST + TS], in_=kTs[:, :, g * TS:(g + 1) * TS]), r=["kTs"], chan="kTs")
                for g in range(NSTR):
                    sqd = [x for x in seq_list(c) if x["kind"] == "s" and x["idx"] == g][0]
                    nfull = sqd["nfull"]
                    S.add("sp", lambda e, g=g, nfull=nfull: e.dma_start(out=lf[:, 0:nfull, :], in_=A["clf"][g].rearrange("(b p) h -> p b h", p=128)),
                          w=["lf"], chan="lfin")
                    S.add("sp", lambda e, g=g, nfull=nfull: e.dma_start(out=lf[0:TS, nfull, :], in_=A["lfs"][g * TS:(g + 1) * TS, :]),
                          r=["lfs_dram"], w=["lf"], chan="lfin")
                    cumsum_seq(sqd)

        load_x(0)
        for ti_, t_ in enumerate(tiles):
            do_tile(ti_, t_)
        print('l1a adds', S.nadd)
        S.limit = None
        S.run_phase()


def phase_l1b(S, nc, c, A, K):
    T, NSEQ, NSTR, NP, NS, PAST, LS = c.T, c.NSEQ, c.NSTR, c.NP, c.NS, c.PAST, c.LS
    tri_b = K["tri_b"]
    LMAX = max(T, LS)
    NBMAX = max(T // 128, PAST // 128 + 1)
    NQTMAX = max(T // 512, 1)
    with ExitStack() as es:
        sb = lambda name, shape, dt=F32: es.enter_context(nc.sbuf_tensor("l1b_" + name, list(shape), dt))
        pm = lambda name, shape, dt=F32: es.enter_context(nc.psum_tensor("l1b_" + name, list(shape), dt))
        KTa = [sb(f"KTa{i}", [128, LMAX], BF16) for i in range(2)]
        QTa = [sb(f"QTa{i}", [128, T], BF16) for i in range(2)]
        Va = [sb(f"Va{i}", [128, NBMAX, 65], BF16) for i in range(2)]
        bias = [sb(f"bias{i}", [128, NBMAX, NQTMAX]) for i in range(2)]
        Pt = [sb(f"Pt{i}", [128, 512], BF16) for i in range(6)]
        ost = [sb(f"ost{i}", [128, 4, DH], BF16) for i in range(2)]
        rden = sb("rden", [128, 4])
        psS = [pm(f"psS{i}", [128, 512]) for i in range(4)]
        psO = [pm(f"psO{i}", [128, 128]) for i in range(4)]
        for i in range(2):
            S.add("pool", lambda e, i=i: e.memset(KTa[i][64:128, :], 0.0), w=[f"KTa{i}"])
            S.add("pool", lambda e, i=i: e.memset(KTa[i][64:67, :], 1.0), w=[f"KTa{i}"])
            S.add("pool", lambda e, i=i: e.memset(QTa[i][64:128, :], 0.0), w=[f"QTa{i}", f"QTa{i}x"])
            S.add("pool", lambda e, i=i: e.memset(Va[i][:, :, 64:65], 1.0), w=[f"Va{i}"])
        jobs = [(sq, h) for sq in seq_list(c) for h in range(H)]
        cnt = {"p": 0, "o": 0}

        def load(ji):
            sq, h = jobs[ji]
            sl = ji % 2
            nfull, part, nqt, key = sq["nfull"], sq["part"], sq["nqt"], sq["key"]
            L = nfull * 128 + part
            if sq["kind"] == "p":
                kt, qt, vbh, au, NQ = A["ktp"][sq["idx"]][h], A["qtp"][sq["idx"]][h], A["vbp"][sq["idx"]][:, h * DH:(h + 1) * DH], A["augp"][sq["idx"]][h], T
            else:
                kt, qt, vbh, au, NQ = A["kts"][sq["idx"]][h], A["qts"][sq["idx"]][h], A["vbs"][sq["idx"]][:, h * DH:(h + 1) * DH], A["augs"][sq["idx"]][h], TS
            S.add("sp", lambda e: e.dma_start(out=KTa[sl][0:64, 0:L], in_=kt), w=[f"KTa{sl}"], chan=f"KTa{sl}")
            S.add("sp", lambda e: e.dma_start(out=QTa[sl][0:64, 0:NQ], in_=qt), w=[f"QTa{sl}"], chan=f"QTa{sl}")
            S.add("sp", lambda e: e.dma_start(out=QTa[sl][64:67, 0:NQ], in_=au), w=[f"QTa{sl}x"], chan=f"QTa{sl}x")
            for b0 in range(0, nfull, 4):
                b1 = min(nfull, b0 + 4)
                S.add("sp", lambda e, b0=b0, b1=b1: e.dma_start(out=Va[sl][:, b0:b1, 0:DH], in_=vbh[b0 * 128:b1 * 128, :].rearrange("(b p) d -> p b d", p=128)),
                      w=[f"Va{sl}_{b0 // 4}"], chan=f"Va{sl}_{b0 // 4}")
            if part:
                S.add("sp", lambda e: e.dma_start(out=Va[sl][0:part, nfull, 0:DH], in_=vbh[nfull * 128:nfull * 128 + part, :]),
                      w=[f"Va{sl}_{nfull // 4}"], chan=f"Va{sl}_p")
            nb = nfull + (1 if part else 0)
            Call, Ball = K["C_" + key], K["B_" + key]
            S.add("dve", lambda e: e.tensor_tensor(out=bias[sl][:, 0:nfull, 0:nqt], in0=Ball[:, :, h].unsqueeze(1).to_broadcast([128, nfull, nqt]),
                                                   in1=Call[:, 0:nfull, h].unsqueeze(2).to_broadcast([128, nfull, nqt]), op=ALU.subtract),
                  w=[f"bias{sl}"])
            if part:
                S.add("dve", lambda e: e.tensor_tensor(out=bias[sl][0:part, nfull, 0:nqt], in0=Ball[0:part, :, h], in1=Call[0:part, nfull:nfull + 1, h], op=ALU.subtract),
                      w=[f"bias{sl}"])

        def run(ji):
            sq, h = jobs[ji]
            sl = ji % 2
            nfull, part, nqt = sq["nfull"], sq["part"], sq["nqt"]
            isp = sq["kind"] == "p"
            def qtile(qi):
                if isp:
                    q0, NQt, nsub, wsub = qi * 512, 512, 4, 128
                    blocks = list(range(4 * qi + 4))
                else:
                    q0, NQt, nsub, wsub = 0, TS, 1, TS
                    blocks = list(range(nfull + 1))
                oslot = cnt["o"] % 2
                cnt["o"] += 1
                LA = 3
                descs = []

                def front(b):
                    KB = 128 if b < nfull else part
                    if isp:
                        j = b - 4 * qi
                        diag = j >= 0
                        c0 = 128 * j if diag else 0
                    else:
                        diag = b == nfull
                        c0 = 0
                    c1 = NQt
                    pb = cnt["p"] % 6
                    sb_ = cnt["p"] % 4
                    cnt["p"] += 1
                    S.add("pe", lambda e: e.matmul(psS[sb_][0:KB, c0:c1], lhsT=KTa[sl][0:128, b * 128:b * 128 + KB],
                                                   rhs=QTa[sl][0:128, q0 + c0:q0 + c1], start=True, stop=True),
                          r=[f"KTa{sl}", f"QTa{sl}", f"QTa{sl}x"], w=[f"psS{sb_}"])
                    S.add("act", lambda e: e.activation(out=Pt[pb][0:KB, c0:c1], in_=psS[sb_][0:KB, c0:c1], func=AF.Exp,
                                                        bias=bias[sl][0:KB, b, qi:qi + 1]),
                          r=[f"psS{sb_}", f"bias{sl}"], w=[f"Pt{pb}"], noself=True)
                    if diag:
                        S.add("dve", lambda e: e.tensor_tensor(out=Pt[pb][0:KB, c0:c0 + KB], in0=Pt[pb][0:KB, c0:c0 + KB],
                                                                in1=tri_b[0:KB, 0:KB], op=ALU.mult),
                              r=[f"Pt{pb}", "tri_b"], w=[f"Pt{pb}"])
                    descs.append((b, KB, c0, pb))

                def back(b, KB, c0, pb):
                    def pv(e):
                        r_ = None
                        for s_ in range(nsub):
                            if s_ * wsub < c0:
                                continue
                            lastb = (4 * qi + s_) if isp else nfull
                            r_ = e.matmul(psO[s_][0:wsub, 0:65], lhsT=Pt[pb][0:KB, s_ * wsub:(s_ + 1) * wsub], rhs=Va[sl][0:KB, b, 0:65],
                                          start=(b == 0), stop=(b == lastb))
                        return r_
                    S.add("pe", pv, r=[f"Pt{pb}", f"Va{sl}_{b // 4}", f"Va{sl}"], w=["psO"])

                for idx in range(len(blocks) + LA):
                    if idx < len(blocks):
                        front(blocks[idx])
                    if idx - LA >= 0:
                        back(*descs[idx - LA])
                for s_ in range(nsub):
                    S.add("dve", lambda e, s_=s_: e.reciprocal(out=rden[0:wsub, s_:s_ + 1], in_=psO[s_][0:wsub, 64:65]), r=["psO"], w=["rden"])
                    S.add("dve", lambda e, s_=s_: e.tensor_scalar(out=ost[oslot][0:wsub, s_, :], in0=psO[s_][0:wsub, 0:DH], scalar1=rden[0:wsub, s_:s_ + 1],
                                                                 scalar2=None, op0=ALU.mult),
                          r=["psO", "rden"], w=[f"ost{oslot}"])
                if isp:
                    r0 = sq["idx"] * T + q0
                    S.add("pool", lambda e, r0=r0: e.dma_start(out=A["obuf"][r0:r0 + 512, h * DH:(h + 1) * DH].rearrange("(s p) d -> p s d", p=128),
                                                              in_=ost[oslot][:, :, :]), r=[f"ost{oslot}"], chan=f"ost{oslot}")
                else:
                    r0 = NP + sq["idx"] * TS
                    S.add("pool", lambda e, r0=r0: e.dma_start(out=A["obuf"][r0:r0 + TS, h * DH:(h + 1) * DH], in_=ost[oslot][0:TS, 0, :]),
                          r=[f"ost{oslot}"], chan=f"ost{oslot}")

            for qi_ in range(nqt):
                qtile(qi_)

        load(0)
        for ji in range(len(jobs)):
            if ji + 1 < len(jobs):
                load(ji + 1)
            run(ji)
        S.run_phase()


def phase_l1c(S, nc, c, A, K):
    T, NSEQ, NSTR, NP, NS = c.T, c.NSEQ, c.NSTR, c.NP, c.NS
    ident_b = K["ident_b"]
    with ExitStack() as es:
        sb = lambda name, shape, dt=F32: es.enter_context(nc.sbuf_tensor("l1c_" + name, list(shape), dt))
        pm = lambda name, shape, dt=F32: es.enter_context(nc.psum_tensor("l1c_" + name, list(shape), dt))
        W1out = sb("W1out", [128, 8, D], BF16)
        xt = [sb(f"xt{i}", [128, 4, D]) for i in range(2)]
        stg = [xt[i][:, :, :].rearrange("p s (a b) -> p (s a) b", b=512) for i in range(2)]
        ot = [sb(f"ot{i}", [128, 4, D], BF16) for i in range(2)]
        zt = [sb(f"zt{i}", [128, 4, D], BF16) for i in range(2)]
        fng = sb("fng", [128, D])
        mb = [sb(f"mb{i}", [128, D], BF16) for i in range(2)]
        mT = [sb(f"mT{i}", [128, 8, 128], BF16) for i in range(2)]
        x2 = [sb(f"x2{i}", [128, D]) for i in range(2)]
        junk = sb("junk", [128, D], BF16)
        ssq = [sb(f"ssq{i}", [128, 2]) for i in range(2)]
        yo = [sb(f"yo{i}", [128, D]) for i in range(2)]
        psT = [pm(f"psT{i}", [128, 8, 128], BF16) for i in range(2)]
        psP = [pm(f"psP{i}", [128, 512]) for i in range(4)]
        wout_v = A["w_fox_out"].rearrange("(c p) e -> p c e", p=128)
        for pc in range(2):
            st = stg[pc % 2]
            S.add("sp", lambda e, st=st, pc=pc: e.dma_start(out=st, in_=wout_v[:, :, pc * 512:(pc + 1) * 512]), w=[f"xt{pc % 2}"], chan=f"xt{pc % 2}")
            S.add("act", lambda e, st=st, pc=pc: e.activation(out=W1out[:, :, pc * 512:(pc + 1) * 512], in_=st, func=AF.Copy), r=[f"xt{pc % 2}"], w=["W1out"])
        S.add("sp", lambda e: e.dma_start(out=fng[:], in_=A["fng"].partition_broadcast(128)), w=["fng"], chan="fng")
        tiles = []
        for i in range(NP // 512):
            tiles.append(dict(kind="p", tok0=i * 512, PT=128, ns=4))
        tiles.append(dict(kind="s", tok0=NP, PT=NS, ns=1))

        def load(ti):
            t = tiles[ti]
            sl = ti % 2
            PT, ns = t["PT"], t["ns"]
            rows = slice(t["tok0"], t["tok0"] + PT * ns)
            S.add("sp", lambda e: e.dma_start(out=xt[sl][0:PT, 0:ns, :], in_=A["x1"][rows, :].rearrange("(s p) d -> p s d", p=PT)), w=[f"xt{sl}"], chan=f"xt{sl}")
            S.add("sp", lambda e: e.dma_start(out=ot[sl][0:PT, 0:ns, :], in_=A["obuf"][rows, :].rearrange("(s p) d -> p s d", p=PT)), w=[f"ot{sl}"], chan=f"ot{sl}")
            S.add("sp", lambda e: e.dma_start(out=zt[sl][0:PT, 0:ns, :], in_=A["szs"][rows, :].rearrange("(s p) d -> p s d", p=PT)), w=[f"zt{sl}"], chan=f"zt{sl}")

        subs = [(ti, s) for ti, t in enumerate(tiles) for s in range(t["ns"])]

        def front(k):
            ti, s = subs[k]
            t = tiles[ti]
            sl, PT, b2 = ti % 2, t["PT"], k % 2
            S.add("dve", lambda e: e.tensor_tensor(out=mb[b2][0:PT, :], in0=ot[sl][0:PT, s, :], in1=zt[sl][0:PT, s, :], op=ALU.mult),
                  r=[f"ot{sl}", f"zt{sl}"], w=[f"mb{b2}"])

            def tr(e):
                for cc in range(8):
                    r_ = e.transpose(psT[b2][:, cc, 0:PT], mb[b2][0:PT, cc * 128:(cc + 1) * 128], ident_b[0:PT, 0:PT])
                return r_
            S.add("pe", tr, r=[f"mb{b2}", "ident_b"], w=[f"psT{b2}"])
            S.add("act", lambda e: e.activation(out=mT[b2][:, :, 0:PT], in_=psT[b2][:, :, 0:PT], func=AF.Copy), r=[f"psT{b2}"], w=[f"mT{b2}"])

        def back(k):
            ti, s = subs[k]
            t = tiles[ti]
            sl, PT, b2 = ti % 2, t["PT"], k % 2
            for hf in range(2):
                pp = psP[b2 * 2 + hf]
                pk = f"psP{b2 * 2 + hf}"

                def op(e, pp=pp, hf=hf):
                    for j in range(8):
                        r_ = e.matmul(pp[0:PT, :], lhsT=mT[b2][:, j, 0:PT], rhs=W1out[:, j, hf * 512:(hf + 1) * 512], start=(j == 0), stop=(j == 7))
                    return r_
                S.add("pe", op, r=[f"mT{b2}", "W1out"], w=[pk])
                S.add("dve", lambda e, pp=pp, hf=hf: e.tensor_tensor(out=x2[b2][0:PT, hf * 512:(hf + 1) * 512], in0=pp[0:PT, :],
                                                                   in1=xt[sl][0:PT, s, hf * 512:(hf + 1) * 512], op=ALU.add),
                      r=[pk, f"xt{sl}"], w=[f"x2{b2}"])
            S.add("act", lambda e: e.activation(out=junk[0:PT, :], in_=x2[b2][0:PT, :], func=AF.Square, accum_out=ssq[b2][0:PT, 0:1]),
                  r=[f"x2{b2}"], w=["junk", f"ssq{b2}"])
            S.add("act", lambda e: e.activation(out=ssq[b2][0:PT, 1:2], in_=ssq[b2][0:PT, 0:1], func=AF.Sqrt, scale=1.0 / D, bias=EPS),
                  r=[f"ssq{b2}"], w=[f"ssq{b2}"])
            S.add("dve", lambda e: e.reciprocal(out=ssq[b2][0:PT, 1:2], in_=ssq[b2][0:PT, 1:2]), r=[f"ssq{b2}"], w=[f"ssq{b2}"])
            S.add("dve", lambda e: e.scalar_tensor_tensor(out=yo[b2][0:PT, :], in0=x2[b2][0:PT, :], scalar=ssq[b2][0:PT, 1:2], in1=fng[0:PT, :],
                                                          op0=ALU.mult, op1=ALU.mult), r=[f"x2{b2}", f"ssq{b2}", "fng"], w=[f"yo{b2}"])
            r0 = t["tok0"] + s * PT
            dst = A["yp"][r0:r0 + PT, :] if t["kind"] == "p" else A["ys"][0:PT, :]
            S.add("sp", lambda e: e.dma_start(out=dst, in_=yo[b2][0:PT, :]), r=[f"yo{b2}"], chan=f"yo{b2}")
            if s == t["ns"] - 1 and ti + 2 < len(tiles):
                load(ti + 2)

        load(0)
        if len(tiles) > 1:
            load(1)
        front(0)
        for k in range(len(subs)):
            if k + 1 < len(subs):
                front(k + 1)
            back(k)
        S.run_phase()


_NC_CACHE = {}


def kernel(**inputs):
    cfg = Cfg()
    n_cores = 8
    if "nc" not in _NC_CACHE:
        _NC_CACHE["nc"] = build(cfg)
    nc = _NC_CACHE["nc"]
    maps = make_in_maps(cfg, inputs)
    res = run_bass_kernel_spmd(nc, maps, core_ids=list(range(n_cores)))
    R = res.results
    B, Tn = 16, cfg.T
    cat = lambda k: np.concatenate([np.asarray(r[k]) for r in R], axis=0)
    y_prompt = cat("yp").reshape(B, Tn, D)
    y_sample = cat("ys").reshape(32, TS, D)
    conv_p = cat("convp").reshape(1, B, CS, D)
    conv_s = cat("convs").reshape(1, 32, CS, D)
    k_p = cat("kp").reshape(1, B, Tn, H, DH)
    v_p = cat("vp").reshape(1, B, Tn, H, DH)
    lf_p = cat("lfp").reshape(1, B, Tn, H)
    k_s = cat("ks").reshape(1, 32, TS, H, DH)
    v_s = cat("vs").reshape(1, 32, TS, H, DH)
    lf_s = cat("lfs").reshape(1, 32, TS, H)
    return (y_prompt, y_sample, conv_p, conv_s, k_p, v_p, lf_p, k_s, v_s, lf_s)
```
